# Optimizing a Trainium2 kernel written in Bass

```python
import math
import jax, jax.numpy as jnp
from jax import lax
import numpy as np

D_MODEL = 1024
BATCH = 4
SEQ = 8192
DEPTH = 1

MIX_WIDTH = D_MODEL
SSD_WIDTH = MIX_WIDTH // 2
SSD_HEAD_DIM = 64
SSD_HEADS = SSD_WIDTH // SSD_HEAD_DIM
SSD_GROUPS = 2
SSD_HEADS_PER_GROUP = SSD_HEADS // SSD_GROUPS
D_STATE = 128
CONV_K = 5
CHUNK = 128
ATTN_WIDTH = MIX_WIDTH - SSD_WIDTH
HEAD_DIM = 64
ATTN_HEADS = ATTN_WIDTH // HEAD_DIM
ATTN_KV_HEADS = 2
Q_PER_KV = ATTN_HEADS // ATTN_KV_HEADS
WINDOW = 128
BLOCK = 128
ROPE_DIMS = HEAD_DIM // 4
ROPE_THETA = 500000.0
D_FF = 2816
N_MOD = 9
EPS = 1e-6
DT_MIN = 0.001
DT_MAX = 0.1
A_MIN = 1.0
A_MAX = 16.0

CONV_DIM = SSD_WIDTH + 2 * SSD_GROUPS * D_STATE
N_DT = 2 * SSD_HEADS
KV_WIDTH = ATTN_KV_HEADS * HEAD_DIM
S_Z = SSD_WIDTH
S_XBC = S_Z + CONV_DIM
S_DT = S_XBC + N_DT
S_Q = S_DT + ATTN_WIDTH
S_K = S_Q + KV_WIDTH
IN_WIDTH = S_K + KV_WIDTH

kernel_name = "hybrid_ssd_swa_macaron_adaln_block"


def _rms(t):
    tf = t.astype(jnp.float32)
    return tf * lax.rsqrt(jnp.mean(tf * tf, axis=-1, keepdims=True) + EPS)


def _ada_norm(h, gain, shift, scale):
    return _rms(h) * gain * (1.0 + scale) + shift


def _swiglu(u, wg, wu, wd):
    return (jax.nn.silu(u @ wg) * (u @ wu)) @ wd


def _ssd_chunked(xs, dt, a, bm, cm, strict):
    b_, s_, g, r, p = xs.shape
    n = bm.shape[-1]
    nc = s_ // CHUNK
    da = (dt * a).reshape(b_, nc, CHUNK, g, r)
    xdt = (xs * dt[..., None]).reshape(b_, nc, CHUNK, g, r, p)
    bc = bm.reshape(b_, nc, CHUNK, g, n)
    cc = cm.reshape(b_, nc, CHUNK, g, n)
    cs = jnp.cumsum(jnp.moveaxis(da, (1, 2), (3, 4)), axis=-1)
    seg = cs[..., :, None] - cs[..., None, :]
    mask = jnp.tril(jnp.ones((CHUNK, CHUNK), dtype=bool), k=-1 if strict else 0)
    lmat = jnp.exp(jnp.where(mask, seg, -jnp.inf))
    cb = jnp.einsum('bclgn,bcsgn->bgcls', cc, bc)
    y_diag = jnp.einsum('bgcls,bgrcls,bcsgrp->bclgrp', cb, lmat, xdt)
    decay_states = jnp.exp(cs[..., -1:] - cs)
    states = jnp.einsum('bclgn,bgrcl,bclgrp->bcgrpn', bc, decay_states, xdt)
    chunk_decay = jnp.exp(cs[..., -1])

    def step(hstate, inp):
        st, dec = inp
        return hstate * dec[..., None, None] + st, hstate

    h0 = jnp.zeros((b_, g, r, p, n), jnp.float32)
    _, h_in = lax.scan(step, h0, (jnp.moveaxis(states, 1, 0), jnp.moveaxis(chunk_decay, -1, 0)))
    h_in = jnp.moveaxis(h_in, 0, 1)
    y_off = jnp.einsum('bclgn,bcgrpn,bgrcl->bclgrp', cc, h_in, jnp.exp(cs))
    return (y_diag + y_off).reshape(b_, s_, g, r, p)


def _partial_rope(t, positions):
    half = ROPE_DIMS // 2
    inv = ROPE_THETA ** (-jnp.arange(half, dtype=jnp.float32) * 2.0 / ROPE_DIMS)
    ang = positions.astype(jnp.float32)[:, :, None, None] * inv
    cos, sin = jnp.cos(ang), jnp.sin(ang)
    t1, t2, rest = t[..., :half], t[..., half:ROPE_DIMS], t[..., ROPE_DIMS:]
    return jnp.concatenate([t1 * cos - t2 * sin, t2 * cos + t1 * sin, rest], axis=-1)


def _window_attention(q, k, v, sink_logit):
    b_, s_ = q.shape[:2]
    nb = s_ // BLOCK
    qb = q.reshape(b_, nb, BLOCK, ATTN_KV_HEADS, Q_PER_KV, HEAD_DIM)

    def band(t):
        tp = jnp.pad(t, ((0, 0), (BLOCK, BLOCK), (0, 0), (0, 0)))
        tp = tp.reshape(b_, nb + 2, BLOCK, ATTN_KV_HEADS, HEAD_DIM)
        return jnp.concatenate([tp[:, :-2], tp[:, 1:-1], tp[:, 2:]], axis=2)

    kb, vb = band(k), band(v)
    scores = jnp.einsum('bnqkgd,bnjkd->bnkgqj', qb, kb) * (HEAD_DIM ** -0.5)
    qi = jnp.arange(nb)[:, None] * BLOCK + jnp.arange(BLOCK)[None, :]
    kj = jnp.arange(nb)[:, None] * BLOCK - BLOCK + jnp.arange(3 * BLOCK)[None, :]
    valid = (jnp.abs(qi[:, :, None] - kj[:, None, :]) <= WINDOW) & ((kj >= 0) & (kj < s_))[:, None, :]
    scores = jnp.where(valid[None, :, None, None], scores.astype(jnp.float32), -jnp.inf)
    sink = sink_logit.astype(jnp.float32).reshape(ATTN_KV_HEADS, Q_PER_KV)[:, :, None, None]
    m = jnp.maximum(jnp.max(scores, axis=-1, keepdims=True), sink)
    pr = jnp.exp(scores - m)
    denom = jnp.sum(pr, axis=-1, keepdims=True) + jnp.exp(sink - m)
    out = jnp.einsum('bnkgqj,bnjkd->bnqkgd', pr / denom, vb)
    return out.reshape(b_, s_, ATTN_WIDTH)


def _token_mix(u, positions, w_in, conv_w, conv_b, dt_bias, a_log, d_skip, ssd_norm_w,
               q_norm_w, k_norm_w, sink_logit, w_out):
    b_, s_, _ = u.shape
    proj = u @ w_in
    z, xbc, dt_raw, q, k, v = jnp.split(proj, [S_Z, S_XBC, S_DT, S_Q, S_K], axis=-1)
    pad = CONV_K // 2
    xbc = lax.conv_general_dilated(xbc, conv_w.astype(xbc.dtype)[:, None, :], (1,), [(pad, pad)],
                                   dimension_numbers=('NWC', 'WIO', 'NWC'),
                                   feature_group_count=CONV_DIM) + conv_b
    xbc = jax.nn.silu(xbc)
    xs, bm, cm = jnp.split(xbc, [SSD_WIDTH, SSD_WIDTH + SSD_GROUPS * D_STATE], axis=-1)
    xs = xs.reshape(b_, s_, SSD_GROUPS, SSD_HEADS_PER_GROUP, SSD_HEAD_DIM)
    bm = bm.reshape(b_, s_, SSD_GROUPS, D_STATE)
    cm = cm.reshape(b_, s_, SSD_GROUPS, D_STATE)
    dt = jax.nn.softplus(dt_raw.reshape(b_, s_, 2, SSD_GROUPS, SSD_HEADS_PER_GROUP)
                         + dt_bias.reshape(2, SSD_GROUPS, SSD_HEADS_PER_GROUP))
    a = -jnp.exp(a_log.astype(jnp.float32)).reshape(2, SSD_GROUPS, SSD_HEADS_PER_GROUP)
    y_fwd = _ssd_chunked(xs, dt[:, :, 0], a[0], bm, cm, strict=False)
    flip = lambda t: jnp.flip(t, axis=1)
    y_bwd = flip(_ssd_chunked(flip(xs), flip(dt[:, :, 1]), a[1], flip(bm), flip(cm), strict=True))
    y = y_fwd + y_bwd + xs * d_skip.reshape(SSD_GROUPS, SSD_HEADS_PER_GROUP)[:, :, None]
    y = y.reshape(b_, s_, SSD_WIDTH) * jax.nn.silu(z)
    y_ssd = _rms(y) * ssd_norm_w
    q = q.reshape(b_, s_, ATTN_HEADS, HEAD_DIM)
    k = k.reshape(b_, s_, ATTN_KV_HEADS, HEAD_DIM)
    v = v.reshape(b_, s_, ATTN_KV_HEADS, HEAD_DIM)
    q = _partial_rope(_rms(q) * q_norm_w, positions)
    k = _partial_rope(_rms(k) * k_norm_w, positions)
    q = q.reshape(b_, s_, ATTN_KV_HEADS, Q_PER_KV, HEAD_DIM)
    y_attn = _window_attention(q, k, v, sink_logit)
    return jnp.concatenate([y_ssd, y_attn], axis=-1) @ w_out


def setup_inputs(seed: int = 0) -> dict:
    key = jax.random.key(seed)
    ks = jax.random.split(key, 24)
    f32 = jnp.float32
    L = DEPTH

    def nrm(k, shape, s):
        return jax.random.normal(k, shape, f32) * s

    x = nrm(ks[0], (BATCH, SEQ, D_MODEL), 1.0)
    c = nrm(ks[1], (BATCH, D_MODEL), 1.0)
    positions = jnp.tile(jnp.arange(SEQ, dtype=jnp.int32)[None, :], (BATCH, 1))
    w_ada = nrm(ks[2], (L, D_MODEL, N_MOD * D_MODEL), 0.5 * D_MODEL ** -0.5)
    b_ada = nrm(ks[3], (L, N_MOD * D_MODEL), 0.01)
    norm_ffn1 = 1.0 + nrm(ks[4], (L, D_MODEL), 0.01)
    ffn1_wg = nrm(ks[5], (L, D_MODEL, D_FF), D_MODEL ** -0.5)
    ffn1_wu = nrm(ks[6], (L, D_MODEL, D_FF), D_MODEL ** -0.5)
    ffn1_wd = nrm(ks[7], (L, D_FF, D_MODEL), D_FF ** -0.5)
    norm_mix = 1.0 + nrm(ks[8], (L, D_MODEL), 0.01)
    w_in = nrm(ks[9], (L, D_MODEL, IN_WIDTH), D_MODEL ** -0.5)
    conv_w = nrm(ks[10], (L, CONV_K, CONV_DIM), CONV_K ** -0.5)
    conv_b = nrm(ks[11], (L, CONV_DIM), 0.01)
    dt0 = jnp.exp(jax.random.uniform(ks[12], (L, 2, SSD_HEADS), f32, math.log(DT_MIN), math.log(DT_MAX)))
    dt_bias = dt0 + jnp.log(-jnp.expm1(-dt0))
    a_log = jnp.log(jax.random.uniform(ks[13], (L, 2, SSD_HEADS), f32, A_MIN, A_MAX))
    d_skip = 1.0 + nrm(ks[14], (L, SSD_HEADS), 0.01)
    ssd_norm_w = 1.0 + nrm(ks[15], (L, SSD_WIDTH), 0.01)
    q_norm_w = 1.0 + nrm(ks[16], (L, HEAD_DIM), 0.01)
    k_norm_w = 1.0 + nrm(ks[17], (L, HEAD_DIM), 0.01)
    sink_logit = nrm(ks[18], (L, ATTN_HEADS), 0.5)
    w_out = nrm(ks[19], (L, MIX_WIDTH, D_MODEL), MIX_WIDTH ** -0.5)
    norm_ffn2 = 1.0 + nrm(ks[20], (L, D_MODEL), 0.01)
    ffn2_wg = nrm(ks[21], (L, D_MODEL, D_FF), D_MODEL ** -0.5)
    ffn2_wu = nrm(ks[22], (L, D_MODEL, D_FF), D_MODEL ** -0.5)
    ffn2_wd = nrm(ks[23], (L, D_FF, D_MODEL), D_FF ** -0.5)
    return {"x": x, "c": c, "positions": positions, "w_ada": w_ada, "b_ada": b_ada,
            "norm_ffn1": norm_ffn1, "ffn1_wg": ffn1_wg, "ffn1_wu": ffn1_wu, "ffn1_wd": ffn1_wd,
            "norm_mix": norm_mix, "w_in": w_in, "conv_w": conv_w, "conv_b": conv_b,
            "dt_bias": dt_bias, "a_log": a_log, "d_skip": d_skip, "ssd_norm_w": ssd_norm_w,
            "q_norm_w": q_norm_w, "k_norm_w": k_norm_w, "sink_logit": sink_logit, "w_out": w_out,
            "norm_ffn2": norm_ffn2, "ffn2_wg": ffn2_wg, "ffn2_wu": ffn2_wu, "ffn2_wd": ffn2_wd}


def reference(x, c, positions, w_ada, b_ada, norm_ffn1, ffn1_wg, ffn1_wu, ffn1_wd, norm_mix,
              w_in, conv_w, conv_b, dt_bias, a_log, d_skip, ssd_norm_w, q_norm_w, k_norm_w,
              sink_logit, w_out, norm_ffn2, ffn2_wg, ffn2_wu, ffn2_wd):
    h = x.astype(jnp.float32)
    cs = jax.nn.silu(c.astype(jnp.float32))
    b_ = c.shape[0]
    for l in range(DEPTH):
        mod = (cs @ w_ada[l] + b_ada[l]).reshape(b_, N_MOD, 1, D_MODEL)
        sh1, sc1, g1, sh2, sc2, g2, sh3, sc3, g3 = [mod[:, i] for i in range(N_MOD)]
        h = h + 0.5 * (1.0 + g1) * _swiglu(_ada_norm(h, norm_ffn1[l], sh1, sc1),
                                           ffn1_wg[l], ffn1_wu[l], ffn1_wd[l])
        u = _ada_norm(h, norm_mix[l], sh2, sc2)
        h = h + (1.0 + g2) * _token_mix(u, positions, w_in[l], conv_w[l], conv_b[l], dt_bias[l],
                                        a_log[l], d_skip[l], ssd_norm_w[l], q_norm_w[l],
                                        k_norm_w[l], sink_logit[l], w_out[l])
        h = h + 0.5 * (1.0 + g3) * _swiglu(_ada_norm(h, norm_ffn2[l], sh3, sc3),
                                           ffn2_wg[l], ffn2_wu[l], ffn2_wd[l])
    return h.astype(x.dtype)
```

```python
import math
import numpy as np
from contextlib import ExitStack
import concourse.bass as bass
import concourse.mybir as mybir
from concourse.bass_utils import run_bass_kernel_spmd

F32 = mybir.dt.float32; BF16 = mybir.dt.bfloat16; I32 = mybir.dt.int32
AF = mybir.ActivationFunctionType; ALU = mybir.AluOpType; AX = mybir.AxisListType

D = 1024; KC = 8; FF = 2816; FC = 22; INW = 2320
EPS = 1e-6
NSMALL = 1354
O_C = 0; O_BADA = 8; O_GAIN = 80; O_CW = 104; O_CB = 144; O_DTB = 152; O_ALOG = 168; O_DSK = 184
O_SINK = 192; O_KW = 200; O_QW = 328; O_SNW = 840; O_FLAG = 1352


class Op:
    __slots__ = ("eng", "fn", "idx", "dma", "sem", "count", "deps", "signal", "waits")

    def __init__(self, eng, fn, idx, dma):
        self.eng = eng; self.fn = fn; self.idx = idx; self.dma = dma
        self.sem = None; self.count = 0; self.deps = (); self.signal = False; self.waits = []


class Prog:
    ENGS = ("pe", "act", "dve", "pool", "sp")

    def __init__(self, nc, es):
        self.nc = nc; self.es = es
        self.ops = []; self.state = {}; self.streams = {}; self.esem = {}
        self.nsem = 0; self.bar_op = None; self.since_bar = []

    def newsem(self, name):
        self.nsem += 1
        return self.es.enter_context(self.nc.semaphore(f"{name}_{self.nsem}"))

    def op(self, eng, fn, reads=(), writes=(), stream=None):
        o = Op(eng, fn, len(self.ops), stream is not None)
        deps = {}
        st = self.state
        for k in reads:
            s = st.get(k)
            if s is None: s = st[k] = [None, []]
            if s[0] is not None: deps[s[0].idx] = s[0]
        for k in writes:
            s = st.get(k)
            if s is None: s = st[k] = [None, []]
            if s[0] is not None: deps[s[0].idx] = s[0]
            for r in s[1]: deps[r.idx] = r
        for k in reads: st[k][1].append(o)
        for k in writes: st[k] = [o, []]
        if self.bar_op is not None: deps[self.bar_op.idx] = self.bar_op
        deps.pop(o.idx, None)
        o.deps = list(deps.values())
        if stream is not None:
            s = self.streams.get(stream)
            if s is None: s = self.streams[stream] = [self.newsem("d"), 0]
            s[1] += 16
            o.sem = s[0]; o.count = s[1]
        self.ops.append(o); self.since_bar.append(o)
        return o

    def barrier(self, fn):
        o = Op("dve", fn, len(self.ops), False)
        last = {}
        for p in self.since_bar:
            if p.dma: last[("d", id(p.sem), p.count)] = p
            else: last[p.eng] = p
        if self.bar_op is not None: last["bar"] = self.bar_op
        o.deps = list(last.values())
        self.ops.append(o)
        self.bar_op = o; self.since_bar = []; self.state = {}
        return o

    def finish(self, final_streams=()):
        nc = self.nc
        for o in self.ops:
            for d in o.deps:
                if not d.dma:
                    if d.eng == "pe" and o.eng == "pe" and not o.dma: continue
                    d.signal = True
        cnt = {e: 0 for e in self.ENGS}
        for e in self.ENGS: self.esem[e] = self.newsem("e" + e)
        for o in self.ops:
            if not o.dma and o.signal:
                cnt[o.eng] += 1; o.count = cnt[o.eng]; o.sem = self.esem[o.eng]
        known = {e: {} for e in self.ENGS}
        nw = 0
        for o in self.ops:
            need = {}
            for d in o.deps:
                if not d.dma and d.eng == "pe" and o.eng == "pe" and not o.dma: continue
                key = id(d.sem)
                if need.get(key, (None, 0))[1] < d.count: need[key] = (d.sem, d.count)
            kn = known[o.eng]
            for key, (sem, c) in need.items():
                if kn.get(key, 0) >= c: continue
                kn[key] = c; o.waits.append((sem, c)); nw += 1
        self.nwaits = nw
        byeng = {e: [o for o in self.ops if o.eng == e] for e in self.ENGS}
        finals = [tuple(self.streams[s]) for s in final_streams]

        def run(e, lst, fin=False):
            for o in lst:
                for (sem, c) in o.waits: e.wait_ge(sem, c)
                ins = o.fn(e)
                if o.dma: ins.then_inc(o.sem, 16)
                elif o.signal: ins.then_inc(o.sem, 1)
            if fin:
                for (sem, c) in finals: e.wait_ge(sem, c)

        with nc.Block() as block:
            @block.tensor
            def _(e): run(e, byeng["pe"])

            @block.scalar
            def _(e): run(e, byeng["act"])

            @block.vector
            def _(e): run(e, byeng["dve"])

            @block.gpsimd
            def _(e): run(e, byeng["pool"])

            @block.sync
            def _(e): run(e, byeng["sp"], True)


def bc(ap, axis, n):
    l = [list(x) for x in ap.ap]
    l.insert(axis, [0, n])
    return bass.AP(ap.tensor, ap.offset, l)


class Arena:
    def __init__(self, nc, es, name, nbytes):
        self.t = es.enter_context(nc.sbuf_tensor(name, [128, nbytes // 4], F32))
        self.cap = nbytes; self.off = 0

    def reset(self): self.off = 0

    def alloc(self, shape, dt):
        esz = 2 if dt == BF16 else 4
        n = int(np.prod(shape)); nb = (n * esz + 31) // 32 * 32
        assert self.off + nb <= self.cap, (self.off, nb, self.cap)
        w0 = self.off // 4; self.off += nb
        ap = self.t[:, w0:w0 + nb // 4]
        if dt != F32: ap = ap.bitcast(dt)
        ap = ap[:, 0:n]
        if len(shape) == 2: ap = ap.rearrange("p (a b) -> p a b", b=shape[1])
        elif len(shape) == 3: ap = ap.rearrange("p (a b c) -> p a b c", b=shape[1], c=shape[2])
        return ap


def build(S, debug=False):
    NCH = S // 128
    NT1 = S // 512
    NT2 = S // 256
    NH = NCH // 2
    SH = S // 2
    nc = bass.Bass("TRN2", target_bir_lowering=False)
    ext_in = lambda n, sh, dt=F32: nc.dram_tensor(n, sh, dt, kind="ExternalInput").ap()
    dbgk = "ExternalOutput" if debug else "Internal"
    scr = lambda n, sh, dt: nc.dram_tensor(n, sh, dt, kind=dbgk).ap()
    xT = ext_in("xT", [D, S]); small = ext_in("small", [128, NSMALL]); pos_in = ext_in("pos", [128, NCH], I32)
    w_ada = ext_in("w_ada", [D, 9 * D])
    wg_in = [ext_in("wg1", [D, FF]), ext_in("wg2", [D, FF])]
    wu_in = [ext_in("wu1", [D, FF]), ext_in("wu2", [D, FF])]
    wd_in = [ext_in("wd1", [FF, D]), ext_in("wd2", [FF, D])]
    w_in_d = ext_in("w_in", [D, INW]); w_out_d = ext_in("w_out", [D, D])
    outT = nc.dram_tensor("outT", [D, SH], F32, kind="ExternalOutput").ap()
    wgb = [nc.dram_tensor(f"wgb{i}", [11, 128, KC * 256], BF16, kind="Internal").ap() for i in range(2)]
    wub = [nc.dram_tensor(f"wub{i}", [11, 128, KC * 256], BF16, kind="Internal").ap() for i in range(2)]
    wdb = [nc.dram_tensor(f"wdb{i}", [8, 128, FC * 128], BF16, kind="Internal").ap() for i in range(2)]
    winb = nc.dram_tensor("winb", [128, KC * INW], BF16, kind="Internal").ap()
    X1 = scr("X1", [D, SH], F32); U2 = scr("U2", [D, S], BF16)
    SZ = scr("SZ", [NH, 128, 512], BF16); YP = scr("YP", [NH, 128, 512], F32)
    SBS = scr("SBS", [NH, 128, 512], F32); EBD = scr("EBD", [NH, 128, 16], F32)
    CTS = scr("CTS", [NH, 128, 256], BF16); QTS = scr("QTS", [NH, 64, 1024], BF16)
    KTS = scr("KTS", [NH + 2, 64, 256], BF16); VES = scr("VES", [NH + 2, 128, 130], BF16)

    es = ExitStack()
    with es:
        P = Prog(nc, es)
        CA = Arena(nc, es, "carena", 33 * 1024)
        WA = Arena(nc, es, "warena", 164 * 1024)
        psf = [es.enter_context(nc.psum_tensor(f"ps{i}", [128, 512], F32)) for i in range(8)]
        ps = [t[:, :] for t in psf]
        psb = [t[:, :].bitcast(BF16) for t in psf]
        PSK = lambda b: ("ps", b)
        RSQ = AF.Abs_reciprocal_sqrt
        STORE_ENG = ["pool"]
        STORE_NAMES = {"X1", "U2", "SZ", "YP", "SBS", "EBD", "CTS", "QTS", "KTS", "VES", "outT"}

        def mm(out, lhsT, rhs, start, stop, r, w):
            P.op("pe", lambda e: e.matmul(out, lhsT, rhs, start=start, stop=stop), r, w)

        def tr(out, in_, ident_, r, w):
            P.op("pe", lambda e: e.transpose(out, in_, ident_), r, w)

        def act(out, in_, func, r, w, bias=None, scale=None, accum=None):
            kw = {}
            if bias is not None: kw["bias"] = bias
            if scale is not None: kw["scale"] = scale
            if accum is not None: kw["accum_out"] = accum
            P.op("act", lambda e: e.activation(out=out, in_=in_, func=func, **kw), r, w)

        def tt(eng, out, in0, in1, op, r, w):
            P.op(eng, lambda e: e.tensor_tensor(out=out, in0=in0, in1=in1, op=op), r, w)

        def ts(eng, out, in0, s1, s2, op0, op1, r, w):
            if s2 is None:
                P.op(eng, lambda e: e.tensor_scalar(out=out, in0=in0, scalar1=s1, scalar2=None, op0=op0), r, w)
            else:
                P.op(eng, lambda e: e.tensor_scalar(out=out, in0=in0, scalar1=s1, scalar2=s2, op0=op0, op1=op1), r, w)

        def stt(eng, out, in0, scalar, in1, op0, op1, r, w):
            P.op(eng, lambda e: e.scalar_tensor_tensor(out=out, in0=in0, scalar=scalar, in1=in1, op0=op0, op1=op1), r, w)

        def cp(eng, out, in_, r, w):
            if eng == "act":
                P.op("act", lambda e: e.activation(out=out, in_=in_, func=AF.Copy), r, w)
            else:
                P.op(eng, lambda e: e.tensor_copy(out=out, in_=in_), r, w)

        def red(out, in_, r, w):
            P.op("dve", lambda e: e.tensor_reduce(out=out, in_=in_, axis=AX.X, op=ALU.add), r, w)

        def recip(out, in_, r, w):
            P.op("dve", lambda e: e.reciprocal(out=out, in_=in_), r, w)

        def memset(eng, ap, val, w):
            P.op(eng, lambda e: e.memset(ap, val), (), w)

        def dma(out, in_, r, w, stream, eng="sp"):
            if eng == "sp" and getattr(out.tensor, "name", "") in STORE_NAMES: eng = STORE_ENG[0]
            P.op(eng, lambda e: e.dma_start(out=out, in_=in_), r, w, stream=stream)

        sm = CA.alloc([NSMALL], F32)
        dma(sm, small, (), ["sm"], "sm")
        posi = CA.alloc([NCH], I32)
        dma(posi, pos_in, (), ["posi"], "posi")
        ones_bf = CA.alloc([128], BF16); ident_bf = CA.alloc([128], BF16)
        LEb = CA.alloc([128], BF16); GEb = CA.alloc([128], BF16); GTb = CA.alloc([128], BF16); LTb = CA.alloc([128], BF16)
        LEf = CA.alloc([128], F32); GEf = CA.alloc([128], F32); GTf = CA.alloc([128], F32); onesf = CA.alloc([128], F32)
        NEGp = CA.alloc([512], BF16); NEGn = CA.alloc([512], BF16)
        dif = WA.alloc([128], F32); tmpm = WA.alloc([128], F32)
        P.op("pool", lambda e: e.iota(dif, pattern=[[1, 128]], base=0, channel_multiplier=-1,
                                      allow_small_or_imprecise_dtypes=True), (), ["dif"])
        memset("dve", onesf, 1.0, ["onesf"])
        cp("dve", ones_bf, onesf, ["onesf"], ["ones_bf"])
        for (mf, mb, op_) in ((LEf, LEb, ALU.is_ge), (GEf, GEb, ALU.is_le), (GTf, GTb, ALU.is_lt), (tmpm, LTb, ALU.is_gt)):
            ts("dve", mf, dif, 0.0, None, op_, None, ["dif"], [("m", id(mf))])
            cp("dve", mb, mf, [("m", id(mf))], [("mb", id(mb))])
        identf = CA.alloc([128], F32)
        ts("dve", identf, dif, 0.0, None, ALU.is_equal, None, ["dif"], ["identf"])
        cp("dve", ident_bf, identf, ["identf"], ["ident"])
        ts("dve", tmpm, dif, 0.0, -30000.0, ALU.is_gt, ALU.mult, ["dif"], [("m", id(tmpm))])
        cp("dve", NEGp.rearrange("p (a b) -> p a b", b=128), bc(tmpm, 1, 4), [("m", id(tmpm))], ["NEGp"])
        ts("dve", tmpm, dif, 0.0, -30000.0, ALU.is_lt, ALU.mult, ["dif"], [("m", id(tmpm))])
        cp("dve", NEGn.rearrange("p (a b) -> p a b", b=128), bc(tmpm, 1, 4), [("m", id(tmpm))], ["NEGn"])
        CONSTS = ["ones_bf", "ident", "NEGp", "NEGn", "onesf"] + [("mb", id(x)) for x in (LEb, GEb, GTb, LTb)] + \
                 [("m", id(x)) for x in (LEf, GEf, GTf)]
        cosT = CA.alloc([NCH, 8], F32); sinT = CA.alloc([NCH, 8], F32)
        posf = WA.alloc([NCH], F32); invf = WA.alloc([8], F32)
        ang = WA.alloc([NCH, 8], F32); kf = WA.alloc([NCH, 8], F32); ki = WA.alloc([NCH, 8], I32)
        cp("dve", posf, posi, ["posi"], ["posf"])
        for i in range(8):
            memset("dve", invf[:, i:i + 1], float(500000.0 ** (-(i * 2.0) / 16.0)), ["invf"])
        tt("dve", ang, bc(posf, 2, 8), bc(invf, 1, NCH), ALU.mult, ["posf", "invf"], ["ang"])
        for (tab, shift) in ((sinT, 0.0), (cosT, math.pi / 2)):
            if shift != 0.0:
                ts("dve", ang, ang, shift, None, ALU.add, None, ["ang"], ["ang"])
            ts("dve", ki, ang, 1.0 / (2 * math.pi), None, ALU.mult, None, ["ang"], ["ki"])
            cp("dve", kf, ki, ["ki"], ["kf"])
            stt("dve", kf, kf, -2 * math.pi, ang, ALU.mult, ALU.add, ["kf", "ang"], ["kf"])
            ts("dve", kf, kf, 3.1415925, -3.1415925, ALU.min, ALU.max, ["kf"], ["kf"])
            act(tab, kf, AF.Sin, ["kf"], [("tab", id(tab))])
        a_neg = CA.alloc([16], F32); esink = CA.alloc([8], F32); qw8 = CA.alloc([512], F32)
        act(a_neg, sm[:, O_ALOG:O_ALOG + 16], AF.Exp, ["sm"], ["a_neg"])
        ts("dve", a_neg, a_neg, -1.0, None, ALU.mult, None, ["a_neg"], ["a_neg"])
        act(esink, sm[:, O_SINK:O_SINK + 8], AF.Exp, ["sm"], ["esink"])
        ts("dve", qw8, sm[:, O_QW:O_QW + 512], 0.125, None, ALU.mult, None, ["sm"], ["qw8"])
        LTf = CA.alloc([128], F32)
        ts("dve", LTf, dif, 0.0, None, ALU.is_gt, None, ["dif"], ["LTf0"])
        MKF = CA.alloc([128], F32); MKB = CA.alloc([128], F32)
        stt("dve", MKF, identf, sm[:, O_FLAG:O_FLAG + 1], LTf, ALU.mult, ALU.add, ["identf", "sm", "LTf0"], ["MKF"])
        stt("dve", MKB, identf, sm[:, O_FLAG + 1:O_FLAG + 2], GTf, ALU.mult, ALU.add, ["identf", "sm", ("m", id(GTf))], ["MKB"])
        hb = CA.alloc([512], F32); hbb = CA.alloc([512], BF16)
        DI = CA.alloc([8, 128], BF16)
        for h in range(8):
            ts("dve", DI[:, h, :], identf, sm[:, O_DSK + h:O_DSK + h + 1], None, ALU.mult, None, ["sm", "identf"], ["DI"])
        cvs = CA.alloc([8], F32)
        act(cvs, sm[:, O_C:O_C + 8], AF.Silu, ["sm"], ["cvs"])
        modv = CA.alloc([72], F32); Am = CA.alloc([24], F32); Gm = CA.alloc([24], F32)

        def cast_list(i):
            lst = []
            svg = wg_in[i].rearrange("(kc p) (s f) -> s p kc f", p=128, f=256)
            svu = wu_in[i].rearrange("(kc p) (s f) -> s p kc f", p=128, f=256)
            for s_ in range(11):
                lst.append(lambda s_=s_: dma(wgb[i][s_].rearrange("p (kc f) -> p kc f", f=256), svg[s_], (), [("wgb", i)], ("cast", "wgb", i), eng="pool"))
                lst.append(lambda s_=s_: dma(wub[i][s_].rearrange("p (kc f) -> p kc f", f=256), svu[s_], (), [("wub", i)], ("cast", "wub", i), eng="pool"))
            svd = wd_in[i].rearrange("(fc p) (m c) -> m p fc c", p=128, c=128)
            for m in range(8):
                lst.append(lambda m=m: dma(wdb[i][m].rearrange("p (fc c) -> p fc c", c=128), svd[m], (), [("wdb", i)], ("cast", "wdb", i), eng="pool"))
            return lst

        def cast_ffn(i):
            for f in cast_list(i): f()
        casts1 = cast_list(0)
        for f in casts1[:22]: f()

        wav = w_ada.rearrange("(kc p) n -> p kc n", p=128)
        wab = [WA.alloc([KC, 1024], F32) for _ in range(2)]
        wbb = [WA.alloc([KC, 1024], BF16) for _ in range(2)]
        cvsb = CA.alloc([8], BF16)
        cp("dve", cvsb, cvs, ["cvs"], ["cvsb"])
        for blk in range(9):
            buf = wab[blk % 2]; bk = ("wab", blk % 2)
            bb = wbb[blk % 2]; bbk = ("wbb", blk % 2)
            dma(buf, wav[:, :, blk * 1024:(blk + 1) * 1024], (), [bk], bk)
            cp("dve", bb[:, 0:4, :], buf[:, 0:4, :], [bk], [(bbk, 0)])
            cp("act", bb[:, 4:8, :], buf[:, 4:8, :], [bk], [(bbk, 1)])
            for j in range(8):
                for kc in range(KC):
                    mm(ps[0][:, blk * 8 + j: blk * 8 + j + 1], bb[:, kc, j * 128:(j + 1) * 128], cvsb[:, kc:kc + 1],
                       kc == 0, kc == KC - 1, [(bbk, kc // 4), "cvsb"], [PSK(0)])
        tt("dve", modv, ps[0][:, 0:72], sm[:, O_BADA:O_BADA + 72], ALU.add, [PSK(0), "sm"], ["modv"])
        for i in range(3):
            stt("dve", Am[:, i * 8:(i + 1) * 8], modv[:, (3 * i + 1) * 8:(3 * i + 2) * 8], 1.0,
                sm[:, O_GAIN + i * 8:O_GAIN + (i + 1) * 8], ALU.add, ALU.mult, ["modv", "sm"], ["Am"])
            ts("dve", Gm[:, i * 8:(i + 1) * 8], modv[:, (3 * i + 2) * 8:(3 * i + 3) * 8], 1.0, (1.0 if i == 1 else 0.5),
               ALU.add, ALU.mult, ["modv"], ["Gm"])
        Bm = lambda i, kc: modv[:, (3 * i) * 8 + kc:(3 * i) * 8 + kc + 1]
        scratch1 = CA.alloc([8], F32)
        dtraw = CA.alloc([NCH, 16], F32)
        wdt16 = CA.alloc([KC, 16], BF16)
        dma(wdt16, w_in_d.rearrange("(kc p) n -> p kc n", p=128)[:, :, 2304:2320], (), ["wdt16"], "wdt16", eng="pool")

        def barrier():
            P.barrier(lambda e: e.memset(scratch1, 0.0))
            WA.reset()

        def norm_a(xt, xkey, sq, sqk):
            act(sq, xt, AF.Square, [xkey], [sqk])

        def norm_b(sq, sqk, bank, T=512):
            for kc in range(KC):
                mm(ps[bank][:, 0:T], ones_bf, sq[:, kc, :], kc == 0, kc == KC - 1, [sqk], [PSK(bank)])

        def norm_c1(rstd, rk, bank, T=512):
            ts("dve", rstd, ps[bank][:, 0:T], 1.0 / D, EPS, ALU.mult, ALU.add, [PSK(bank)], [rk])

        def norm_c2(rstd, rk):
            act(rstd, rstd, AF.Sqrt, [rk], [rk])

        def norm_c2r(rstd, rk):
            recip(rstd, rstd, [rk], [rk])

        def norm_c3(xt, xkey, u, ukey, rstd, rk, i, tmps, tks, kcs):
            for kc in kcs:
                tb = tmps[kc % 2]; tk = tks[kc % 2]
                tt("dve", tb, xt[:, kc, :], rstd, ALU.mult, [xkey, rk], [tk])
                ts("dve", u[:, kc, :], tb, Am[:, i * 8 + kc:i * 8 + kc + 1], Bm(i, kc), ALU.mult, ALU.add, [tk], [ukey])

        def norm_c(xt, xkey, u, ukey, rstd, rk, i, tmps, tks, bank, T=512):
            norm_c1(rstd, rk, bank); norm_c2(rstd, rk); norm_c2r(rstd, rk)
            norm_c3(xt, xkey, u, ukey, rstd, rk, i, tmps, tks, range(KC))

        def norm_tile(xt, xkey, u, ukey, sq, rstd, i, tmps, T=512):
            norm_a(xt, xkey, sq, "sq")
            norm_b(sq, "sq", 6)
            norm_c(xt, xkey, u, ukey, rstd, "rstd", i, tmps, (("sg", 0), ("sg", 1)), 6)

        def run_hooks(hooks, key):
            if hooks and key in hooks:
                for f in hooks[key]: f()

        def ffn_up(fi, u, ukey, hT, rings, hooks=None, T=512):
            wgu, wdr, sgb = rings
            for s in range(11):
                slot = s % len(wgu)
                gs, us = wgu[slot]
                gk = ("wg", slot); uk_ = ("wu", slot)
                dma(gs, wgb[fi][s].rearrange("p (kc f) -> p kc f", f=256), [("wgb", fi)], [gk], ("wg", slot))
                dma(us, wub[fi][s].rearrange("p (kc f) -> p kc f", f=256), [("wub", fi)], [uk_], ("wu", slot))
                for j in range(2):
                    fc = 2 * s + j
                    bG = (fc % 2) * 2; bU = bG + 1
                    for kc in range(KC):
                        mm(ps[bG][:, 0:T], gs[:, kc, j * 128:(j + 1) * 128], u[:, kc, :], kc == 0, kc == KC - 1, [gk, ukey], [PSK(bG)])
                    for kc in range(KC):
                        mm(ps[bU][:, 0:T], us[:, kc, j * 128:(j + 1) * 128], u[:, kc, :], kc == 0, kc == KC - 1, [uk_, ukey], [PSK(bU)])
                    sg = sgb[fc % 2]; sk = ("sg", fc % 2)
                    act(sg, ps[bG][:, 0:T], AF.Silu, [PSK(bG)], [sk])
                    tt("dve", hT[:, fc, :], ps[bU][:, 0:T], sg, ALU.mult, [PSK(bU), sk], [("h", fc)])
                    run_hooks(hooks, ("c", fc))
                run_hooks(hooks, s)

        def ffn_down(fi, xt, xkey, hT, gi, rings, hooks=None, T=512, banks=(4, 5)):
            wgu, wdr, sgb = rings
            run_hooks(hooks, -1)
            for m in range(8):
                slot = m % len(wdr)
                ws = wdr[slot]; wk = ("wd", slot)
                dma(ws, wdb[fi][m].rearrange("p (fc c) -> p fc c", c=128), [("wdb", fi)], [wk], wk)
                b = banks[m % 2]
                for fc in range(FC):
                    mm(ps[b][:, 0:T], ws[:, fc, :], hT[:, fc, :], fc == 0, fc == FC - 1, [wk, ("h", fc)], [PSK(b)])
                    if fc == 10: run_hooks(hooks, ("h", m))
                stt("dve", xt[:, m, :], ps[b][:, 0:T], Gm[:, gi * 8 + m:gi * 8 + m + 1], xt[:, m, :], ALU.mult, ALU.add,
                    [PSK(b), xkey], [xkey])
                run_hooks(hooks, m)

        def ffn_tile(fi, xt, xkey, u, ukey, hT, gi, rings, T=512):
            ffn_up(fi, u, ukey, hT, rings)
            ffn_down(fi, xt, xkey, hT, gi, rings)

        barrier()
        for f in casts1[22:]: f()
        casts2 = cast_list(1)
        dma(winb.rearrange("p (kc n) -> p kc n", n=INW), w_in_d.rearrange("(kc p) n -> p kc n", p=128), (), ["winb"], "winb", eng="pool")
        xts = [WA.alloc([KC, 512], F32) for _ in range(2)]
        sqA = WA.alloc([KC, 512], BF16); sqB = WA.alloc([KC, 512], BF16)
        u1 = [WA.alloc([KC, 512], BF16) for _ in range(2)]; u2 = WA.alloc([KC, 512], BF16)
        hT = WA.alloc([FC, 512], BF16)
        rstdA = WA.alloc([512], F32); rstdB = WA.alloc([512], F32)
        sgb = [WA.alloc([512], F32) for _ in range(2)]
        ntm = [WA.alloc([512], F32) for _ in range(2)]
        wgu = [(WA.alloc([KC, 256], BF16), WA.alloc([KC, 256], BF16)) for _ in range(3)]
        wdr = [WA.alloc([FC, 128], BF16) for _ in range(3)]
        rings = (wgu, wdr, sgb)
        xTv = xT.rearrange("(kc p) t -> p kc t", p=128)
        X1v = X1.rearrange("(kc p) t -> p kc t", p=128)
        U2v = U2.rearrange("(kc p) t -> p kc t", p=128)
        outv = outT.rearrange("(kc p) t -> p kc t", p=128)
        NTK = (("ntm", 0), ("ntm", 1))

        def p1_x(ti): return xts[ti % 2], ("x", ti % 2)

        def p1_load(ti):
            dma(xts[ti % 2], xTv[:, :, ti * 512:(ti + 1) * 512], (), [("x", ti % 2)], ("xl", ti % 2))

        def n1a(ti): norm_a(*p1_x(ti), sqA, "sqA")
        def n1b(ti): norm_b(sqA, "sqA", 6)
        def n1c1(ti): norm_c1(rstdA, "rstdA", 6)
        def n1c2(ti): norm_c2(rstdA, "rstdA")
        def n1c2r(ti): norm_c2r(rstdA, "rstdA")
        def n1c3(ti, kcs):
            xt, xk = p1_x(ti)
            norm_c3(xt, xk, u1[ti % 2], ("u1", ti % 2), rstdA, "rstdA", 0, ntm, NTK, kcs)

        def n2a(ti):
            xt, xk = p1_x(ti)
            if ti < NT1 // 2:
                dma(X1v[:, :, ti * 512:(ti + 1) * 512], xt, [xk], [("X1", ti)], ("xs", ti % 2))
            norm_a(xt, xk, sqB, "sqB")
        def n2b(ti): norm_b(sqB, "sqB", 7)
        def n2c1(ti): norm_c1(rstdB, "rstdB", 7)
        def n2c2(ti): norm_c2(rstdB, "rstdB")
        def n2c2r(ti): norm_c2r(rstdB, "rstdB")
        def n2c3(ti, kcs):
            xt, xk = p1_x(ti)
            norm_c3(xt, xk, u2, "u2", rstdB, "rstdB", 1, ntm2, NTK2, kcs)
        def n2s(ti):
            dma(U2v[:, :, ti * 512:(ti + 1) * 512], u2, ["u2"], [("U2", ti)], "u2s")
        def n2d(ti):
            for ci in range(4):
                for kc in range(KC):
                    mm(ps[6][:, ci * 16:(ci + 1) * 16], u2[:, kc, ci * 128:(ci + 1) * 128], wdt16[:, kc, :], kc == 0, kc == KC - 1,
                       ["u2"], [PSK(6)])
            cp("act", dtraw[:, ti * 4:(ti + 1) * 4, :], ps[6][:, 0:64].rearrange("p (c h) -> p c h", h=16), [PSK(6)], ["dtraw"])
            if ti + 2 < NT1: p1_load(ti + 2)
            per = -(-len(casts2) // NT1)
            for f in casts2[ti * per:(ti + 1) * per]: f()

        ntm2 = [WA.alloc([512], F32) for _ in range(2)]
        NTK2 = (("ntm2", 0), ("ntm2", 1))
        p1_load(0)
        if NT1 > 1: p1_load(1)
        n1a(0); n1b(0); n1c1(0); n1c2(0); n1c2r(0); n1c3(0, range(KC))
        L = lambda f, *a: (lambda: f(*a))
        for ti in range(NT1):
            xt, xk = p1_x(ti)
            hu = {}
            if ti > 0:
                t = ti - 1
                hu = {0: [L(n2b, t)], 1: [L(n2c1, t)], 2: [L(n2c2, t)], 3: [L(n2c2r, t)], 4: [L(n2c3, t, (0, 1))], 5: [L(n2c3, t, (2, 3))],
                      6: [L(n2c3, t, (4, 5))], 7: [L(n2c3, t, (6, 7)), L(n2s, t)], 9: [L(n2d, t)]}
            ffn_up(0, u1[ti % 2], ("u1", ti % 2), hT, rings, hu)
            hd = {7: [L(n2a, ti)]}
            if ti + 1 < NT1:
                t = ti + 1
                hd[-1] = [L(n1a, t)]
                hd[0] = [L(n1b, t)]; hd[1] = [L(n1c1, t)]; hd[2] = [L(n1c2, t)]; hd[3] = [L(n1c2r, t)]
                hd[4] = [L(n1c3, t, (0, 1, 2))]; hd[5] = [L(n1c3, t, (3, 4, 5))]; hd[6] = [L(n1c3, t, (6, 7))]
            ffn_down(0, xt, xk, hT, 0, rings, hd)
        t = NT1 - 1
        n2b(t); n2c1(t); n2c2(t); n2c2r(t); n2c3(t, range(KC)); n2s(t); n2d(t)

        barrier()
        STORE_ENG[0] = "sp"
        win = WA.alloc([KC, INW], BF16)
        dma(win, winb.rearrange("p (kc n) -> p kc n", n=INW), (), ["win"], "win")
        dtv = WA.alloc([NCH, 16], F32); da = WA.alloc([NCH, 16], F32)
        T1 = WA.alloc([NCH, 16], F32); T2 = WA.alloc([NCH, 16], F32); T3 = WA.alloc([NCH, 16], F32)
        DEC = WA.alloc([NCH, 16], F32); WDT = WA.alloc([NCH, 16], F32); EBDt = WA.alloc([NCH, 16], F32)
        tt("dve", T1, dtraw, bc(sm[:, O_DTB:O_DTB + 16], 1, NCH), ALU.add, ["dtraw"], ["T1"])
        ts("dve", T2, T1, -1.0, None, ALU.mult, None, ["T1"], ["T2"])
        tt("dve", T2, T2, T1, ALU.min, ["T2", "T1"], ["T2"])
        act(T3, T2, AF.Exp, ["T2"], ["T3"])
        act(T3, T3, AF.Ln, ["T3"], ["T3"], bias=1.0)
        stt("dve", dtv, T1, 0.0, T3, ALU.max, ALU.add, ["T1", "T3"], ["dtv"])
        tt("dve", da, dtv, bc(a_neg, 1, NCH), ALU.mult, ["dtv"], ["da"])
        CW = 512 // 8
        for c0 in range(0, NCH, CW):
            c1 = min(NCH, c0 + CW); n = c1 - c0
            for (half, msk, bnk) in ((0, LEf, 0), (1, GEf, 1)):
                o3 = ps[bnk][:, 0:n * 8].rearrange("p (c h) -> p c h", h=8)
                mm(o3, msk, da[:, c0:c1, half * 8:(half + 1) * 8], True, True, ["da"], [PSK(bnk)])
                cp("act", T2[:, c0:c1, half * 8:(half + 1) * 8], o3, [PSK(bnk)], ["T2"])
        CW2 = 512 // 16
        for c0 in range(0, NCH, CW2):
            c1 = min(NCH, c0 + CW2); n = c1 - c0
            bnk = 2 + (c0 // CW2) % 2
            o3 = ps[bnk][:, 0:n * 16].rearrange("p (c h) -> p c h", h=16)
            mm(o3, onesf, da[:, c0:c1, :], True, True, ["da"], [PSK(bnk)])
            cp("act", T1[:, c0:c1, :], o3, [PSK(bnk)], ["T1"])
        act(T3, T2, AF.Exp, ["T2"], ["T3"])
        act(DEC, T1, AF.Exp, ["T1"], ["DEC"])
        tt("dve", WDT, T1, T2, ALU.subtract, ["T1", "T2"], ["WDT"])
        act(WDT, WDT, AF.Exp, ["WDT"], ["WDT"])
        tt("dve", WDT, WDT, dtv, ALU.mult, ["WDT", "dtv"], ["WDT"])
        cp("dve", EBDt[:, :, 0:8], T3[:, :, 8:16], ["T3"], ["EBDt"])
        cp("dve", EBDt[:, :, 8:16], DEC[:, :, 8:16], ["DEC"], ["EBDt"])
        for c0 in range(0, NH, 16):
            c1 = min(NH, c0 + 16)
            dma(EBD[c0:c1].rearrange("c p n -> p c n"), EBDt[:, c0:c1, :], ["EBDt"], [("EBD", c0)], "ebds")
        TAB = ["dtv", "da", "T3", "DEC", "WDT"]
        DG = WA.alloc([8, 5, 128], BF16)
        for cc in range(8):
            for k in range(5):
                ts("dve", DG[:, cc, k, :], identf, sm[:, O_CW + cc * 5 + k:O_CW + cc * 5 + k + 1], None, ALU.mult, None, [], ["DG"])
        u2w = [WA.alloc([KC, 260], BF16) for _ in range(2)]
        xpres = [WA.alloc([8, 260], BF16) for _ in range(2)]
        xcs = [WA.alloc([8, 256], BF16) for _ in range(2)]
        PB = 2
        szb = [WA.alloc([512], BF16) for _ in range(PB)]
        qf = [WA.alloc([512], F32) for _ in range(PB)]; qq = [WA.alloc([512], F32) for _ in range(PB)]
        qss = [WA.alloc([8], F32) for _ in range(PB)]; qb = [WA.alloc([512], BF16) for _ in range(PB)]
        kfb = [WA.alloc([128], F32) for _ in range(PB)]; kq = [WA.alloc([128], F32) for _ in range(PB)]
        kss = [WA.alloc([8], F32) for _ in range(PB)]; kb = [WA.alloc([128], BF16) for _ in range(PB)]
        ra = [WA.alloc([64], F32) for _ in range(PB)]; rb = [WA.alloc([64], F32) for _ in range(PB)]
        vext = [WA.alloc([2, 65], BF16) for _ in range(PB)]
        QTb = [WA.alloc([1024], BF16) for _ in range(PB)]; KTb = [WA.alloc([256], BF16) for _ in range(PB)]
        ypart = [WA.alloc([512], F32) for _ in range(PB)]; sbt = [WA.alloc([512], F32) for _ in range(PB)]
        xsB = WA.alloc([768], BF16)
        xdt_f = WA.alloc([512], BF16); xdt_b = WA.alloc([512], BF16); xw_f = WA.alloc([512], BF16); xw_b = WA.alloc([512], BF16)
        CBf = WA.alloc([2, 128], F32); CBb = WA.alloc([2, 128], F32)
        lT = [WA.alloc([4, 128], BF16) for _ in range(4)]
        exs = [WA.alloc([512], F32) for _ in range(4)]
        MT = [WA.alloc([4, 128], BF16) for _ in range(4)]
        hf = WA.alloc([512], F32); hfb = WA.alloc([512], BF16); htmp = WA.alloc([512], F32)
        ytmp = WA.alloc([512], F32)
        zt = WA.alloc([256], BF16)
        memset("dve", hf, 0.0, ["hf"]); memset("dve", hfb, 0.0, ["hfb"])
        for pb in range(PB): memset("dve", vext[pb], 1.0, [("vext", pb)])
        memset("dve", zt, 0.0, ["zt"])
        dma(KTS[0], zt[0:64, 0:256], ["zt"], [("KTS", 0)], "zpad")
        dma(VES[0], zt[:, 0:130], ["zt"], [("VES", 0)], "zpad")

        def load_u2w(tj, slot):
            buf = u2w[slot % 2]; k = ("u2w", slot % 2)
            t0 = tj * 256
            lo = max(t0 - 2, 0); hi = min(t0 + 258, S)
            if lo != t0 - 2 or hi != t0 + 258:
                memset("pool", buf, 0.0, [k])
            dma(buf[:, :, lo - (t0 - 2):hi - (t0 - 2)], U2v[:, :, lo:hi], (), [k], k)

        def emit_zq(n_t_, ci_):
            uw_ = u2w[n_t_ % 2]; uk__ = ("u2w", n_t_ % 2)
            cols_ = slice(2 + ci_ * 128, 2 + ci_ * 128 + 128)
            for kc in range(KC):
                mm(ps[0][:, 0:512], uw_[:, kc, cols_], win[:, kc, 0:512], kc == 0, kc == KC - 1, ["win", uk__], [PSK(0)])
            for kc in range(KC):
                mm(ps[1][:, 0:512], uw_[:, kc, cols_], win[:, kc, 1536:2048], kc == 0, kc == KC - 1, ["win", uk__], [PSK(1)])

        memset("dve", hb, 0.0, ["hb"])
        tile_order = list(range(NT2 - 1, NT2 // 2 - 1, -1)) + list(range(0, NT2 // 2))
        own_next = {}
        _own = [(n_t_, tj_ * 2 + ci_, ci_) for n_t_, tj_ in enumerate(tile_order) if tj_ < NT2 // 2 for ci_ in (0, 1)]
        for a_, b_ in zip(_own[:-1], _own[1:]): own_next[a_[1]] = (b_[0], b_[2])
        xbc_done = set()
        zq_pending = [None]

        def xbc_steps(n_t_, far_, shared_):
            uw_ = u2w[n_t_ % 2]; uk_ = ("u2w", n_t_ % 2); par_ = n_t_ % 2
            xc_ = xcs[par_]; xp_ = xpres[par_]
            ncc_ = 6 if far_ else 8

            def xproj(cc):
                b = 6 + cc % 2
                for kc in range(KC):
                    mm(ps[b][:, 0:260], win[:, kc, 512 + cc * 128:512 + (cc + 1) * 128], uw_[:, kc, :], kc == 0, kc == KC - 1,
                       ["win", uk_], [PSK(b)])
                cp("dve" if far_ else "act", xp_[:, cc, :], ps[b][:, 0:260], [PSK(b)], [("xpre", par_, cc)])

            def xconv(cc):
                b = (6 if shared_ else 4) + cc % 2
                for k in range(5):
                    mm(ps[b][:, 0:256], DG[:, cc, k, :], xp_[:, cc, k:k + 256], k == 0, k == 4, ["DG", ("xpre", par_, cc)], [PSK(b)])
                act(xc_[:, cc, :], ps[b][:, 0:256], AF.Silu, [PSK(b), "sm"], [("xc", par_, cc)], bias=sm[:, O_CB + cc:O_CB + cc + 1])

            steps = []
            for cc in range(ncc_):
                def f(cc=cc):
                    if cc == 0: xproj(0)
                    if cc + 1 < ncc_: xproj(cc + 1)
                    xconv(cc)
                steps.append(f)
            return steps

        load_u2w(tile_order[0], 0)
        for n_t, tj in enumerate(tile_order):
            far = tj >= NT2 // 2
            uw = u2w[n_t % 2]; uk = ("u2w", n_t % 2)
            if n_t + 1 < len(tile_order): load_u2w(tile_order[n_t + 1], n_t + 1)
            par = n_t % 2
            xc = xcs[par]; xpre = xpres[par]
            if n_t not in xbc_done:
                for f in xbc_steps(n_t, far, False): f()
            XC = [("xc", par, i) for i in range(8)]
            xsteps = []
            if (not far) and n_t + 1 < len(tile_order):
                xsteps = xbc_steps(n_t + 1, False, True)
                xbc_done.add(n_t + 1)

            def xstep():
                if ovl and xsteps: xsteps.pop(0)()
            for ci in ((1, 0) if far else (0, 1)):
                c = tj * 2 + ci; pb = c % PB
                own = not far; halo = (c == NH)
                ovl = own and ci == 1 and len(xsteps) > 0
                bY, bO = (0, 1) if ovl else (6, 7)
                tc0 = ci * 128
                ucols = slice(2 + tc0, 2 + tc0 + 128)
                xcols = slice(tc0, tc0 + 128)
                dtc = dtv[:, c, :]; dac = da[:, c, :]; Ec = T3[:, c, :]; decc = DEC[:, c, :]; wc = WDT[:, c, :]
                branches = []
                if own: branches.append((qf[pb], ("qf", pb), qq[pb], qss[pb], 8, qb[pb], "q"))
                if own or halo: branches.append((kfb[pb], ("kfb", pb), kq[pb], kss[pb], 2, kb[pb], "k"))
                if own and c == 0:
                    emit_zq(n_t, 0)
                if own or halo:
                    for kc in range(KC):
                        mm(ps[2][:, 0:256], uw[:, kc, ucols], win[:, kc, 2048:2304], kc == 0, kc == KC - 1, ["win", uk], [PSK(2)])
                combos = ((0, 0, GTb, LEb, CBf), (0, 1, GTb, LEb, CBf), (1, 0, LTb, GEb, CBb), (1, 1, LTb, GEb, CBb))
                if own:
                    for mi, (dr, g, msk, rhsm, CBd) in enumerate(combos):
                        col = dr * 8 + g * 4
                        tt("dve", lT[mi], bc(msk, 1, 4), bc(dac[:, col:col + 4], 2, 128), ALU.mult, ["da"], [("lT", mi)])
                for i in range(6):
                    tr(psb[3][:, i * 128:(i + 1) * 128], xc[:, i, xcols], ident_bf, [("xc", par, i)], [PSK(3)])
                if own:
                    cp("act", qf[pb], ps[1][:, 0:512], [PSK(1)], [("qf", pb)])
                    act(szb[pb], ps[0][:, 0:512], AF.Silu, [PSK(0)], [("szb", pb)])
                    dma(SZ[c], szb[pb], [("szb", pb)], [("SZ", c)], ("szs", pb))
                    nxt = own_next.get(c)
                    if nxt is not None:
                        if ovl: zq_pending[0] = nxt
                        else: emit_zq(*nxt)
                if own or halo:
                    cp("act", kfb[pb], ps[2][:, 0:128], [PSK(2)], [("kfb", pb)])
                    cp("act", vext[pb][:, :, 0:64], ps[2][:, 128:256].rearrange("p (a b) -> p a b", b=64), [PSK(2)], [("vext", pb)])
                cp("act", xsB, psb[3][:, 0:768], [PSK(3)], ["xsB"])
                xs3 = xsB[:, 0:512].rearrange("p (h d) -> p h d", d=64)
                if own:
                    for g in range(2):
                        mm(ps[2][:, 256 + g * 128:256 + (g + 1) * 128], xc[:, 4 + g, xcols], xc[:, 6 + g, xcols], True, True, XC, [PSK(2)])
                    dma(CTS[c].rearrange("p (g l) -> p g l", l=128), xc[:, 6:8, xcols], XC, [("CTS", c)], "cts")
                    xstep()
                    for mi, (dr, g, msk, rhsm, CBd) in enumerate(combos):
                        bS = 4 + mi % 2
                        for r in range(4):
                            mm(ps[bS][:, r * 128:(r + 1) * 128], lT[mi][:, r, :], rhsm, True, True, [("lT", mi)], [PSK(bS)])
                        act(exs[mi], ps[bS][:, 0:512], AF.Exp, [PSK(bS)], [("exs", mi)])
                        xstep()
                        if mi == 1:
                            p2v = ps[2][:, 256:512].rearrange("p (g l) -> p g l", l=128)
                            tt("dve", CBf, p2v, bc(MKF, 1, 2), ALU.mult, [PSK(2)], ["CBf"])
                            tt("dve", CBb, p2v, bc(MKB, 1, 2), ALU.mult, [PSK(2)], ["CBb"])
                            for (dst, col_, nm) in ((xdt_f, dtc[:, 0:8], "xdt_f"), (xdt_b, dtc[:, 8:16], "xdt_b")):
                                tt("dve", dst.rearrange("p (h d) -> p h d", d=64), xs3, bc(col_, 2, 64), ALU.mult, ["xsB", "dtv"], [nm])
                    for mi, (dr, g, msk, rhsm, CBd) in enumerate(combos):
                        tt("dve", MT[mi], exs[mi].rearrange("p (r l) -> p r l", l=128), bc(CBd[:, g, :], 1, 4), ALU.mult,
                           [("exs", mi), "CBf", "CBb"], [("MT", mi)])
                for (src, sk, sqb, ssb, nh, outb, nm) in branches:
                    tt("dve", sqb[:, 0:nh * 64], src, src, ALU.mult, [sk], [(nm + "sq", pb)])
                    red(ssb[:, 0:nh], sqb[:, 0:nh * 64].rearrange("p (h d) -> p h d", d=64), [(nm + "sq", pb)], [(nm + "ss", pb)])
                    ts("dve", ssb[:, 0:nh], ssb[:, 0:nh], 1.0 / 64, EPS, ALU.mult, ALU.add, [(nm + "ss", pb)], [(nm + "ss", pb)])
                    act(ssb[:, 0:nh], ssb[:, 0:nh], AF.Ln, [(nm + "ss", pb)], [(nm + "ss", pb)])
                    act(ssb[:, 0:nh], ssb[:, 0:nh], AF.Exp, [(nm + "ss", pb)], [(nm + "ss", pb)], scale=-0.5)
                if own:
                    for h in range(8):
                        g = h // 4; r = h % 4
                        hs = slice(h * 64, (h + 1) * 64)
                        mm(ps[bY][:, hs], MT[g][:, r, :], xdt_f[:, hs], True, False, [("MT", g), "xdt_f"], [PSK(bY)])
                        mm(ps[bY][:, hs], MT[2 + g][:, r, :], xdt_b[:, hs], False, False, [("MT", 2 + g), "xdt_b"], [PSK(bY)])
                        mm(ps[bY][:, hs], DI[:, h, :], xsB[:, hs], False, True, ["xsB"], [PSK(bY)])
                    for g in range(2):
                        mm(ps[bO][:, g * 256:(g + 1) * 256], xc[:, 6 + g, xcols], hfb[:, g * 256:(g + 1) * 256], True, True,
                           XC + ["hfb"], [PSK(bO)])
                    xstep()
                    tt("dve", xw_f.rearrange("p (h d) -> p h d", d=64), xs3, bc(wc[:, 0:8], 2, 64), ALU.mult, ["xsB", "WDT"], ["xw_f"])
                tt("dve", xw_b.rearrange("p (h d) -> p h d", d=64), xs3, bc(wc[:, 8:16], 2, 64), ALU.mult, ["xsB", "WDT"], ["xw_b"])
                if own:
                    tt("dve", ytmp.rearrange("p (h d) -> p h d", d=64), ps[bO][:, 0:512].rearrange("p (h d) -> p h d", d=64),
                       bc(Ec[:, 0:8], 2, 64), ALU.mult, [PSK(bO), "T3"], ["ytmp"])
                    tt("dve", ypart[pb], ytmp, ps[bY][:, 0:512], ALU.add, ["ytmp", PSK(bY)], [("ypart", pb)])
                    if zq_pending[0] is not None:
                        emit_zq(*zq_pending[0]); zq_pending[0] = None
                    dma(YP[c], ypart[pb], [("ypart", pb)], [("YP", c)], ("yps", pb))
                for (bH, xw, xwn) in (((2, xw_f, "xw_f"), (3, xw_b, "xw_b")) if own else ((3, xw_b, "xw_b"),)):
                    for g in range(2):
                        mm(ps[bH][:, g * 256:(g + 1) * 256], xsB[:, 512 + g * 128:512 + (g + 1) * 128], xw[:, g * 256:(g + 1) * 256],
                           True, True, ["xsB", xwn] + ([("kfb", pb), ("vext", pb)] if bH == 2 else []), [PSK(bH)])
                if own:
                    tt("dve", htmp.rearrange("p (h d) -> p h d", d=64), hf.rearrange("p (h d) -> p h d", d=64), bc(decc[:, 0:8], 2, 64),
                       ALU.mult, ["hf", "DEC"], ["htmp"])
                    tt("dve", hf, htmp, ps[2][:, 0:512], ALU.add, ["htmp", PSK(2)], ["hf"])
                    cp("act", hfb, hf, ["hf"], ["hfb"])
                    cp("act", sbt[pb], ps[3][:, 0:512], [PSK(3)], [("sbt", pb)])
                    dma(SBS[c], sbt[pb], [("sbt", pb)], [("SBS", c)], ("sbss", pb))
                else:
                    tt("dve", htmp.rearrange("p (h d) -> p h d", d=64), hb.rearrange("p (h d) -> p h d", d=64), bc(decc[:, 8:16], 2, 64),
                       ALU.mult, ["hb", "DEC"], ["htmp"])
                    tt("dve", hb, htmp, ps[3][:, 0:512], ALU.add, ["htmp", PSK(3)], ["hb"])
                    if halo:
                        cp("act", hbb, hb, ["hb"], ["hbb"])
                if own:
                    xstep(); xstep()
                    while ovl and xsteps: xsteps.pop(0)()
                for (src, sk, sqb, ssb, nh, outb, nm) in branches:
                    s3 = src.rearrange("p (h d) -> p h d", d=64)
                    tt("dve", s3, s3, bc(ssb[:, 0:nh], 2, 64), ALU.mult, [sk, (nm + "ss", pb)], [sk])
                    if nm == "q":
                        tt("dve", src, src, qw8, ALU.mult, [sk], [sk])
                    else:
                        tt("dve", s3, s3, bc(sm[:, O_KW:O_KW + 64], 1, 2), ALU.mult, [sk], [sk])
                    cp("act", outb, src, [sk], [(nm + "b", pb)])
                    o3 = outb.rearrange("p (h d) -> p h d", d=64)
                    t1 = s3[:, :, 0:8]; t2 = s3[:, :, 8:16]
                    cb_ = bc(cosT[:, c, :], 1, nh); sb_ = bc(sinT[:, c, :], 1, nh)
                    ra3 = ra[pb][:, 0:nh * 8].rearrange("p (h d) -> p h d", d=8); rb3 = rb[pb][:, 0:nh * 8].rearrange("p (h d) -> p h d", d=8)
                    tt("dve", ra3, t1, cb_, ALU.mult, [sk], [("ra", pb)])
                    tt("dve", rb3, t2, sb_, ALU.mult, [sk], [("rb", pb)])
                    tt("dve", o3[:, :, 0:8], ra3, rb3, ALU.subtract, [("ra", pb), ("rb", pb)], [(nm + "b", pb)])
                    tt("dve", ra3, t2, cb_, ALU.mult, [sk], [("ra", pb)])
                    tt("dve", rb3, t1, sb_, ALU.mult, [sk], [("rb", pb)])
                    tt("dve", o3[:, :, 8:16], ra3, rb3, ALU.add, [("ra", pb), ("rb", pb)], [(nm + "b", pb)])
                if own:
                    for h in range(8):
                        tr(psb[4][0:64, h * 128:(h + 1) * 128], qb[pb][:, h * 64:(h + 1) * 64], ident_bf, [("qb", pb)], [PSK(4)])
                    cp("act", QTb[pb][0:64, :], psb[4][0:64, 0:1024], [PSK(4)], [("QTb", pb)])
                    dma(QTS[c], QTb[pb][0:64, :], [("QTb", pb)], [("QTS", c)], ("qts", pb))
                if own or halo:
                    for h in range(2):
                        tr(psb[5][0:64, h * 128:(h + 1) * 128], kb[pb][:, h * 64:(h + 1) * 64], ident_bf, [("kb", pb)], [PSK(5)])
                    cp("act", KTb[pb][0:64, :], psb[5][0:64, 0:256], [PSK(5)], [("KTb", pb)])
                    dma(KTS[c + 1], KTb[pb][0:64, :], [("KTb", pb)], [("KTS", c + 1)], ("kts", pb))
                    dma(VES[c + 1].rearrange("p (a b) -> p a b", b=65), vext[pb], [("vext", pb)], [("VES", c + 1)], ("ves", pb))

        barrier()
        STORE_ENG[0] = "pool"
        wout = WA.alloc([KC, D], BF16)
        dma(wout, w_out_d.rearrange("(kc p) n -> p kc n", p=128), (), ["wout"], "wout", eng="pool")
        xts = [WA.alloc([KC, 512], F32) for _ in range(2)]
        sq = WA.alloc([KC, 512], BF16)
        u3 = WA.alloc([KC, 512], BF16)
        hT = WA.alloc([FC, 512], BF16)
        rstd = WA.alloc([512], F32)
        sgb = [WA.alloc([512], F32) for _ in range(2)]
        ntm3 = sgb
        NTK3 = (("sg", 0), ("sg", 1))
        wgu = [(WA.alloc([KC, 256], BF16), WA.alloc([KC, 256], BF16)) for _ in range(2)]
        wdr = [WA.alloc([FC, 128], BF16) for _ in range(2)]
        rings = (wgu, wdr, sgb)
        ymT = WA.alloc([KC, 512], BF16)
        NB = 2
        ctl = [WA.alloc([2, 128], BF16) for _ in range(NB)]
        ebl = [WA.alloc([16], F32) for _ in range(NB)]
        sbl = [WA.alloc([512], F32) for _ in range(NB)]
        ypl = [WA.alloc([512], F32) for _ in range(NB)]
        szl = [WA.alloc([512], BF16) for _ in range(NB)]
        qtl = [WA.alloc([1024], BF16) for _ in range(NB)]
        ktl = [WA.alloc([3, 256], BF16) for _ in range(NB)]
        vel = [WA.alloc([3, 130], BF16) for _ in range(NB)]
        hb2 = WA.alloc([512], F32)
        yt1 = WA.alloc([512], F32); yg = WA.alloc([512], F32); junk = WA.alloc([512], BF16)
        gss = WA.alloc([4], F32)
        ymix = WA.alloc([1024], BF16)
        PT = [[WA.alloc([512], BF16) for _ in range(3)] for _ in range(2)]
        den = WA.alloc([8], F32)
        NT3 = NT1 // 2
        order = [c for ti in range(NT3 - 1, -1, -1) for c in range(ti * 4 + 3, ti * 4 - 1, -1)]
        BA = (4, 5); BV = 6; BT = 7

        def load_chunk(n):
            c = order[n]; s = n % NB
            dma(ctl[s], CTS[c].rearrange("p (g l) -> p g l", l=128), (), [("ctl", s)], ("l_ct", s))
            dma(ebl[s], EBD[c], (), [("ebl", s)], ("l_eb", s))
            dma(sbl[s], SBS[c], (), [("sbl", s)], ("l_sb", s))
            dma(ypl[s], YP[c], (), [("ypl", s)], ("l_yp", s))
            dma(szl[s], SZ[c], (), [("szl", s)], ("l_sz", s))
            dma(qtl[s][0:64, :], QTS[c], (), [("qtl", s)], ("l_qt", s))
            dma(ktl[s][0:64, :, :], KTS[c:c + 3].rearrange("b p n -> p b n"), (), [("ktl", s)], ("l_kt", s))
            dma(vel[s], VES[c:c + 3].rearrange("b p n -> p b n"), (), [("vel", s)], ("l_ve", s))

        def chunk_stage_fns(n):
            c = order[n]; s = n % NB; ci = c % 4

            def A():
                if n + 1 < len(order): load_chunk(n + 1)
                for g in range(2):
                    mm(ps[BV][:, g * 256:(g + 1) * 256], ctl[s][:, g, :], hbb[:, g * 256:(g + 1) * 256], True, True,
                       [("ctl", s), "hbb"], [PSK(BV)])
                tt("dve", yt1.rearrange("p (h d) -> p h d", d=64), ps[BV][:, 0:512].rearrange("p (h d) -> p h d", d=64),
                   bc(ebl[s][:, 0:8], 2, 64), ALU.mult, [PSK(BV), ("ebl", s)], ["yt1"])
                tt("dve", yt1, yt1, ypl[s], ALU.add, ["yt1", ("ypl", s)], ["yt1"])
                tt("dve", hb2.rearrange("p (h d) -> p h d", d=64), hb.rearrange("p (h d) -> p h d", d=64), bc(ebl[s][:, 8:16], 2, 64),
                   ALU.mult, ["hb", ("ebl", s)], ["hb2"])
                tt("dve", hb, hb2, sbl[s], ALU.add, ["hb2", ("sbl", s)], ["hb"])
                cp("act", hbb, hb, ["hb"], ["hbb"])
                tt("dve", yg, yt1, szl[s], ALU.mult, ["yt1", ("szl", s)], ["yg"])
                memset("dve", gss[:, 0:1], 0.0, ["gss"])
                act(junk, yg, AF.Square, ["yg", "gss"], ["junk", "gss"], accum=gss[:, 0:1])

            def scores(kvh, blk):
                b = BA[(kvh * 3 + blk) % 2]
                mm(ps[b][:, 0:512], ktl[s][0:64, blk, kvh * 128:(kvh + 1) * 128], qtl[s][0:64, kvh * 512:(kvh + 1) * 512],
                   True, blk == 1, [("ktl", s), ("qtl", s)], [PSK(b)])
                if blk != 1:
                    mm(ps[b][:, 0:512], ident_bf, NEGp if blk == 0 else NEGn, False, True, [], [PSK(b)])
                act(PT[kvh][blk], ps[b][:, 0:512], AF.Exp, [PSK(b)], [("PT", kvh, blk)])

            def pv(kvh):
                for r in range(4):
                    for blk in range(3):
                        mm(ps[BV][:, r * 65:(r + 1) * 65], PT[kvh][blk][:, r * 128:(r + 1) * 128], vel[s][:, blk, kvh * 65:(kvh + 1) * 65],
                           blk == 0, blk == 2, [("PT", kvh, blk), ("vel", s)], [PSK(BV)])
                pv3 = ps[BV][:, 0:260].rearrange("p (r d) -> p r d", d=65)
                tt("dve", den[:, kvh * 4:(kvh + 1) * 4], pv3[:, :, 64], esink[:, kvh * 4:(kvh + 1) * 4], ALU.add,
                   [PSK(BV)], [("den", kvh)])
                recip(den[:, kvh * 4:(kvh + 1) * 4], den[:, kvh * 4:(kvh + 1) * 4], [("den", kvh)], [("den", kvh)])
                tt("dve", ymix[:, 512 + kvh * 256:512 + (kvh + 1) * 256].rearrange("p (r d) -> p r d", d=64), pv3[:, :, 0:64],
                   bc(den[:, kvh * 4:(kvh + 1) * 4], 2, 64), ALU.mult, [PSK(BV), ("den", kvh)], [("ymix_a", kvh)])

            def B():
                scores(0, 0)
                ts("dve", gss[:, 1:2], gss[:, 0:1], 1.0 / 512, EPS, ALU.mult, ALU.add, ["gss"], ["gss1"])
                act(gss[:, 1:2], gss[:, 1:2], AF.Sqrt, ["gss1"], ["gss1"])

            def C():
                scores(0, 1)
                recip(gss[:, 2:3], gss[:, 1:2], ["gss1"], ["gss2"])
                stt("dve", ymix[:, 0:512], yg, gss[:, 2:3], sm[:, O_SNW:O_SNW + 512], ALU.mult, ALU.mult, ["yg", "gss2"], ["ymix_s"])

            def Dd(): scores(0, 2)
            def E(): scores(1, 0); pv(0)
            def F(): scores(1, 1)
            def G(): scores(1, 2)
            def H(): pv(1)

            def I():
                for cc in range(8):
                    tr(psb[BT][:, cc * 128:(cc + 1) * 128], ymix[:, cc * 128:(cc + 1) * 128], ident_bf,
                       ["ymix_s", ("ymix_a", 0), ("ymix_a", 1)], [PSK(BT)])
                cp("act", ymT[:, :, ci * 128:(ci + 1) * 128], psb[BT][:, 0:1024].rearrange("p (a b) -> p a b", b=128), [PSK(BT)], [("ymT", ci)])
            return A, B, C, Dd, E, F, G, H, I

        def p3_x(tix): return xts[tix % 2], ("x", tix % 2)

        def seq(*fs):
            def f():
                for g_ in fs: g_()
            return f

        def tile_pre_stages(tix):
            xt, xk = p3_x(tix)
            ch = [chunk_stage_fns(tix * 4 + cq) for cq in range(4)]
            def chunk_list(q, first):
                c_ = ch[q]
                return [first, seq(c_[2], c_[3]), c_[4], seq(c_[5], c_[6]), c_[7]]
            st = chunk_list(0, seq(ch[0][0], ch[0][1]))
            for q in range(1, 4):
                st.append(seq(ch[q - 1][8], ch[q][0]))
                st += chunk_list(q, ch[q][1])
            st.append(ch[3][8])

            def mk_o(ms):
                def f():
                    for m in ms:
                        b = BV if m % 2 == 0 else BT
                        for cc in range(KC):
                            mm(ps[b][:, 0:512], wout[:, cc, m * 128:(m + 1) * 128], ymT[:, cc, :], cc == 0, cc == KC - 1,
                               ["wout"] + [("ymT", i) for i in range(4)], [PSK(b)])
                        stt("dve", xt[:, m, :], ps[b][:, 0:512], Gm[:, 8 + m:8 + m + 1], xt[:, m, :], ALU.mult, ALU.add, [PSK(b), xk], [xk])
                return f
            st.append(mk_o((0, 1, 2, 3)))
            st.append(seq(mk_o((4, 5, 6, 7)), lambda: norm_a(xt, xk, sq, "sq")))
            st.append(lambda: norm_b(sq, "sq", BV))
            st.append(seq(lambda: norm_c1(rstd, "rstd", BV), lambda: norm_c2(rstd, "rstd")))
            st.append(seq(lambda: norm_c2r(rstd, "rstd"), lambda: norm_c3(xt, xk, u3, "u3", rstd, "rstd", 2, ntm3, NTK3, (0, 1, 2, 3))))
            st.append(lambda: norm_c3(xt, xk, u3, "u3", rstd, "rstd", 2, ntm3, NTK3, (4, 5, 6, 7)))
            return st

        def p3_loadx(tix):
            ti = NT3 - 1 - tix
            dma(xts[tix % 2], X1v[:, :, ti * 512:(ti + 1) * 512], (), [("x", tix % 2)], ("xl3", tix % 2))

        load_chunk(0)
        p3_loadx(0)
        for f in tile_pre_stages(0): f()
        slots = [("u", ("c", fc)) for fc in range(FC)] + [("d", -1)]
        for m in range(8): slots += [("d", ("h", m)), ("d", m)]
        for tix in range(NT3):
            ti = NT3 - 1 - tix
            xt, xk = p3_x(tix)
            hu = {}; hd = {}
            if tix + 1 < NT3:
                hu.setdefault(("c", 9), []).append(lambda t=tix + 1: p3_loadx(t))
                stg = tile_pre_stages(tix + 1)
                assert len(stg) <= len(slots), (len(stg), len(slots))
                for f, (kind, k) in zip(stg, slots):
                    (hu if kind == "u" else hd).setdefault(k, []).append(f)
            ffn_up(1, u3, "u3", hT, rings, hu)
            ffn_down(1, xt, xk, hT, 2, rings, hd, banks=(0, 2))
            dma(outv[:, :, ti * 512:(ti + 1) * 512], xt, [xk], [("out", ti)], ("os", tix % 2))
        P.finish(final_streams=[("os", 0), ("os", 1)] if NT3 > 1 else [("os", 0)])
        stats = (len(P.ops), P.nwaits, P.nsem)
    return nc, stats


def _prep_core(b, rev, S, x, c, positions, w_ada, b_ada, norm_ffn1, norm_mix, norm_ffn2, conv_w, conv_b, dt_bias, a_log,
               d_skip, ssd_norm_w, q_norm_w, k_norm_w, sink_logit):
    f = np.float32
    small = np.zeros((128, NSMALL), f)
    pk = lambda v: np.ascontiguousarray(np.asarray(v, f).reshape(-1, 128).T)
    rep = lambda v: np.broadcast_to(np.asarray(v, f).reshape(1, -1), (128, np.asarray(v).size))
    small[:, O_C:O_C + 8] = pk(c[b])
    small[:, O_BADA:O_BADA + 72] = pk(b_ada[0])
    small[:, O_GAIN:O_GAIN + 8] = pk(norm_ffn1[0]); small[:, O_GAIN + 8:O_GAIN + 16] = pk(norm_mix[0])
    small[:, O_GAIN + 16:O_GAIN + 24] = pk(norm_ffn2[0])
    cw = np.asarray(conv_w[0], f)
    if rev: cw = cw[::-1]
    small[:, O_CW:O_CW + 40] = cw.reshape(5, 8, 128).transpose(2, 1, 0).reshape(128, 40)
    small[:, O_CB:O_CB + 8] = pk(conv_b[0])
    dtb = np.asarray(dt_bias[0], f); alg = np.asarray(a_log[0], f)
    if rev: dtb = dtb[::-1]; alg = alg[::-1]
    small[:, O_DTB:O_DTB + 16] = rep(dtb.reshape(-1))
    small[:, O_ALOG:O_ALOG + 16] = rep(alg.reshape(-1))
    small[:, O_DSK:O_DSK + 8] = rep(d_skip[0])
    small[:, O_SINK:O_SINK + 8] = rep(sink_logit[0])
    small[:, O_KW:O_KW + 128] = rep(np.tile(np.asarray(k_norm_w[0], f), 2))
    small[:, O_QW:O_QW + 512] = rep(np.tile(np.asarray(q_norm_w[0], f), 8))
    small[:, O_SNW:O_SNW + 512] = rep(ssd_norm_w[0])
    small[:, O_FLAG] = 0.0 if rev else 1.0
    small[:, O_FLAG + 1] = 1.0 if rev else 0.0
    p = np.asarray(positions[b][:S], np.int32); xb = np.asarray(x[b][:S], f)
    if rev: p = p[::-1]; xb = xb[::-1]
    pos = np.ascontiguousarray(p.reshape(-1, 128).T)
    xT = np.ascontiguousarray(xb.T)
    return {"xT": xT, "small": small, "pos": pos}


_CACHE = {}


def run(inputs, S, debug=False, n_cores=8):
    g = {k: np.asarray(v) for k, v in inputs.items()}
    key = (S, debug)
    if key not in _CACHE:
        _CACHE[key] = build(S, debug)
    nc, stats = _CACHE[key]
    wi = np.asarray(g["w_in"][0], np.float32)
    perm = np.concatenate([np.arange(0, 512), np.arange(512, 1536), np.arange(1552, 2064), np.arange(2064, 2192),
                           np.arange(2192, 2320), np.arange(1536, 1552)])
    perm_r = np.concatenate([perm[:2304], np.arange(1544, 1552), np.arange(1536, 1544)])
    ca = lambda a: np.ascontiguousarray(a, dtype=np.float32)
    shared = {
        "w_ada": ca(g["w_ada"][0]), "wg1": ca(g["ffn1_wg"][0]), "wu1": ca(g["ffn1_wu"][0]), "wd1": ca(g["ffn1_wd"][0]),
        "wg2": ca(g["ffn2_wg"][0]), "wu2": ca(g["ffn2_wu"][0]), "wd2": ca(g["ffn2_wd"][0]), "w_out": ca(g["w_out"][0]),
    }
    w_in_n = np.ascontiguousarray(wi[:, perm]); w_in_r = np.ascontiguousarray(wi[:, perm_r])
    in_maps = []
    for core in range(n_cores):
        b = (core // 2) % 4; rev = bool(core % 2)
        m = _prep_core(b, rev, S, g["x"], g["c"], g["positions"], g["w_ada"], g["b_ada"], g["norm_ffn1"], g["norm_mix"],
                       g["norm_ffn2"], g["conv_w"], g["conv_b"], g["dt_bias"], g["a_log"], g["d_skip"], g["ssd_norm_w"],
                       g["q_norm_w"], g["k_norm_w"], g["sink_logit"])
        m.update(shared)
        m["w_in"] = w_in_r if rev else w_in_n
        in_maps.append(m)
    res = run_bass_kernel_spmd(nc, in_maps, core_ids=list(range(n_cores)))
    return res, stats


def assemble(res, S, nb=4):
    out = np.empty((nb, S, D), np.float32)
    for b in range(nb):
        out[b, :S // 2] = res.results[2 * b]["outT"].T
        out[b, S // 2:] = res.results[2 * b + 1]["outT"].T[::-1]
    return out


def kernel(**inputs):
    S = 8192
    res, _ = run(inputs, S, debug=False, n_cores=8)
    return assemble(res, S, 4)
```

```python
import math
import numpy as np
from contextlib import ExitStack
import concourse.bass as bass
import concourse.mybir as mybir
from concourse.bass_utils import run_bass_kernel_spmd

F32 = mybir.dt.float32; BF16 = mybir.dt.bfloat16; I32 = mybir.dt.int32
AF = mybir.ActivationFunctionType; ALU = mybir.AluOpType; AX = mybir.AxisListType

D = 1024; KC = 8; FF = 2816; FC = 22; INW = 2320
EPS = 1e-6
NSMALL = 1354
O_C = 0; O_BADA = 8; O_GAIN = 80; O_CW = 104; O_CB = 144; O_DTB = 152; O_ALOG = 168; O_DSK = 184
O_SINK = 192; O_KW = 200; O_QW = 328; O_SNW = 840; O_FLAG = 1352


class Op:
    __slots__ = ("eng", "fn", "idx", "dma", "sem", "count", "deps", "signal", "waits")

    def __init__(self, eng, fn, idx, dma):
        self.eng = eng; self.fn = fn; self.idx = idx; self.dma = dma
        self.sem = None; self.count = 0; self.deps = (); self.signal = False; self.waits = []


class Prog:
    ENGS = ("pe", "act", "dve", "pool", "sp")

    def __init__(self, nc, es):
        self.nc = nc; self.es = es
        self.ops = []; self.state = {}; self.streams = {}; self.esem = {}
        self.nsem = 0; self.bar_op = None; self.since_bar = []

    def newsem(self, name):
        self.nsem += 1
        return self.es.enter_context(self.nc.semaphore(f"{name}_{self.nsem}"))

    def op(self, eng, fn, reads=(), writes=(), stream=None):
        o = Op(eng, fn, len(self.ops), stream is not None)
        deps = {}
        st = self.state
        for k in reads:
            s = st.get(k)
            if s is None: s = st[k] = [None, []]
            if s[0] is not None: deps[s[0].idx] = s[0]
        for k in writes:
            s = st.get(k)
            if s is None: s = st[k] = [None, []]
            if s[0] is not None: deps[s[0].idx] = s[0]
            for r in s[1]: deps[r.idx] = r
        for k in reads: st[k][1].append(o)
        for k in writes: st[k] = [o, []]
        if self.bar_op is not None: deps[self.bar_op.idx] = self.bar_op
        deps.pop(o.idx, None)
        o.deps = list(deps.values())
        if stream is not None:
            s = self.streams.get(stream)
            if s is None: s = self.streams[stream] = [self.newsem("d"), 0]
            s[1] += 16
            o.sem = s[0]; o.count = s[1]
        self.ops.append(o); self.since_bar.append(o)
        return o

    def barrier(self, fn):
        o = Op("dve", fn, len(self.ops), False)
        last = {}
        for p in self.since_bar:
            if p.dma: last[("d", id(p.sem), p.count)] = p
            else: last[p.eng] = p
        if self.bar_op is not None: last["bar"] = self.bar_op
        o.deps = list(last.values())
        self.ops.append(o)
        self.bar_op = o; self.since_bar = []; self.state = {}
        return o

    def finish(self, final_streams=()):
        nc = self.nc
        epos = {}; ecnt = {e: 0 for e in self.ENGS}
        for o in self.ops:
            epos[o.idx] = ecnt[o.eng]; ecnt[o.eng] += 1

        def skip(o, d):
            if d.dma or o.dma: return False
            if d.eng != o.eng: return False
            if d.eng == "pe": return True
            if d.eng in ("dve", "act") and epos[o.idx] - epos[d.idx] >= 3: return True
            return False
        self._skip = skip
        for o in self.ops:
            for d in o.deps:
                if not d.dma:
                    if skip(o, d): continue
                    d.signal = True
        cnt = {e: 0 for e in self.ENGS}
        for e in self.ENGS: self.esem[e] = self.newsem("e" + e)
        for o in self.ops:
            if not o.dma and o.signal:
                cnt[o.eng] += 1; o.count = cnt[o.eng]; o.sem = self.esem[o.eng]
        known = {e: {} for e in self.ENGS}
        nw = 0
        for o in self.ops:
            need = {}
            for d in o.deps:
                if skip(o, d): continue
                key = id(d.sem)
                if need.get(key, (None, 0))[1] < d.count: need[key] = (d.sem, d.count)
            kn = known[o.eng]
            for key, (sem, c) in need.items():
                if kn.get(key, 0) >= c: continue
                kn[key] = c; o.waits.append((sem, c)); nw += 1
        self.nwaits = nw
        byeng = {e: [o for o in self.ops if o.eng == e] for e in self.ENGS}
        finals = [tuple(self.streams[s]) for s in final_streams]

        def run(e, lst, fin=False):
            for o in lst:
                for (sem, c) in o.waits: e.wait_ge(sem, c)
                ins = o.fn(e)
                if o.dma: ins.then_inc(o.sem, 16)
                elif o.signal: ins.then_inc(o.sem, 1)
            if fin:
                for (sem, c) in finals: e.wait_ge(sem, c)

        with nc.Block() as block:
            @block.tensor
            def _(e): run(e, byeng["pe"])

            @block.scalar
            def _(e): run(e, byeng["act"])

            @block.vector
            def _(e): run(e, byeng["dve"])

            @block.gpsimd
            def _(e): run(e, byeng["pool"])

            @block.sync
            def _(e): run(e, byeng["sp"], True)


def bc(ap, axis, n):
    l = [list(x) for x in ap.ap]
    l.insert(axis, [0, n])
    return bass.AP(ap.tensor, ap.offset, l)


class Arena:
    def __init__(self, nc, es, name, nbytes):
        self.t = es.enter_context(nc.sbuf_tensor(name, [128, nbytes // 4], F32))
        self.cap = nbytes; self.off = 0

    def reset(self): self.off = 0

    def alloc(self, shape, dt):
        esz = 2 if dt == BF16 else 4
        n = int(np.prod(shape)); nb = (n * esz + 31) // 32 * 32
        assert self.off + nb <= self.cap, (self.off, nb, self.cap)
        w0 = self.off // 4; self.off += nb
        ap = self.t[:, w0:w0 + nb // 4]
        if dt != F32: ap = ap.bitcast(dt)
        ap = ap[:, 0:n]
        if len(shape) == 2: ap = ap.rearrange("p (a b) -> p a b", b=shape[1])
        elif len(shape) == 3: ap = ap.rearrange("p (a b c) -> p a b c", b=shape[1], c=shape[2])
        return ap


def build(S, debug=False):
    NCH = S // 128
    NT1 = S // 512
    NT2 = S // 256
    NH = NCH // 2
    SH = S // 2
    nc = bass.Bass("TRN2", target_bir_lowering=False)
    ext_in = lambda n, sh, dt=F32: nc.dram_tensor(n, sh, dt, kind="ExternalInput").ap()
    dbgk = "ExternalOutput" if debug else "Internal"
    scr = lambda n, sh, dt: nc.dram_tensor(n, sh, dt, kind=dbgk).ap()
    xT = ext_in("xT", [D, S]); small = ext_in("small", [128, NSMALL]); pos_in = ext_in("pos", [128, NCH], I32)
    w_ada = ext_in("w_ada", [D, 9 * D])
    wg_in = [ext_in("wg1", [D, FF]), ext_in("wg2", [D, FF])]
    wu_in = [ext_in("wu1", [D, FF]), ext_in("wu2", [D, FF])]
    wd_in = [ext_in("wd1", [FF, D]), ext_in("wd2", [FF, D])]
    w_in_d = ext_in("w_in", [D, INW]); w_out_d = ext_in("w_out", [D, D])
    outT = nc.dram_tensor("outT", [D, SH], F32, kind="ExternalOutput").ap()
    wgb = [nc.dram_tensor(f"wgb{i}", [11, 128, KC * 256], BF16, kind="Internal").ap() for i in range(2)]
    wub = [nc.dram_tensor(f"wub{i}", [11, 128, KC * 256], BF16, kind="Internal").ap() for i in range(2)]
    wdb = [nc.dram_tensor(f"wdb{i}", [8, 128, FC * 128], BF16, kind="Internal").ap() for i in range(2)]
    winb = nc.dram_tensor("winb", [128, KC * INW], BF16, kind="Internal").ap()
    X1 = scr("X1", [D, SH], F32); U2 = scr("U2", [D, S], BF16)
    SZ = scr("SZ", [NH, 128, 512], BF16); YP = scr("YP", [NH, 128, 512], F32)
    SBS = scr("SBS", [NH, 128, 512], F32); EBD = scr("EBD", [NH, 128, 16], F32)
    CTS = scr("CTS", [NH, 128, 256], BF16); QTS = scr("QTS", [NH, 64, 1024], BF16)
    KTS = scr("KTS", [NH + 2, 64, 256], BF16); VES = scr("VES", [NH + 2, 128, 130], BF16)

    es = ExitStack()
    with es:
        P = Prog(nc, es)
        CA = Arena(nc, es, "carena", 33 * 1024)
        WA = Arena(nc, es, "warena", 164 * 1024)
        psf = [es.enter_context(nc.psum_tensor(f"ps{i}", [128, 512], F32)) for i in range(8)]
        ps = [t[:, :] for t in psf]
        psb = [t[:, :].bitcast(BF16) for t in psf]
        PSK = lambda b: ("ps", b)
        RSQ = AF.Abs_reciprocal_sqrt
        STORE_ENG = ["pool"]
        STORE_NAMES = {"X1", "U2", "SZ", "YP", "SBS", "EBD", "CTS", "QTS", "KTS", "VES", "outT"}

        def mm(out, lhsT, rhs, start, stop, r, w):
            P.op("pe", lambda e: e.matmul(out, lhsT, rhs, start=start, stop=stop), r, w)

        def tr(out, in_, ident_, r, w):
            P.op("pe", lambda e: e.transpose(out, in_, ident_), r, w)

        def act(out, in_, func, r, w, bias=None, scale=None, accum=None):
            kw = {}
            if bias is not None: kw["bias"] = bias
            if scale is not None: kw["scale"] = scale
            if accum is not None: kw["accum_out"] = accum
            P.op("act", lambda e: e.activation(out=out, in_=in_, func=func, **kw), r, w)

        def tt(eng, out, in0, in1, op, r, w):
            P.op(eng, lambda e: e.tensor_tensor(out=out, in0=in0, in1=in1, op=op), r, w)

        def ts(eng, out, in0, s1, s2, op0, op1, r, w):
            if s2 is None:
                P.op(eng, lambda e: e.tensor_scalar(out=out, in0=in0, scalar1=s1, scalar2=None, op0=op0), r, w)
            else:
                P.op(eng, lambda e: e.tensor_scalar(out=out, in0=in0, scalar1=s1, scalar2=s2, op0=op0, op1=op1), r, w)

        def stt(eng, out, in0, scalar, in1, op0, op1, r, w):
            P.op(eng, lambda e: e.scalar_tensor_tensor(out=out, in0=in0, scalar=scalar, in1=in1, op0=op0, op1=op1), r, w)

        def cp(eng, out, in_, r, w):
            if eng == "act":
                P.op("act", lambda e: e.activation(out=out, in_=in_, func=AF.Copy), r, w)
            else:
                P.op(eng, lambda e: e.tensor_copy(out=out, in_=in_), r, w)

        def red(out, in_, r, w):
            P.op("dve", lambda e: e.tensor_reduce(out=out, in_=in_, axis=AX.X, op=ALU.add), r, w)

        def recip(out, in_, r, w):
            P.op("dve", lambda e: e.reciprocal(out=out, in_=in_), r, w)

        def memset(eng, ap, val, w):
            P.op(eng, lambda e: e.memset(ap, val), (), w)

        def dma(out, in_, r, w, stream, eng="sp"):
            if eng == "sp" and getattr(out.tensor, "name", "") in STORE_NAMES: eng = STORE_ENG[0]
            P.op(eng, lambda e: e.dma_start(out=out, in_=in_), r, w, stream=stream)

        sm = CA.alloc([NSMALL], F32)
        dma(sm, small, (), ["sm"], "sm")
        posi = CA.alloc([NCH], I32)
        dma(posi, pos_in, (), ["posi"], "posi")
        ones_bf = CA.alloc([128], BF16); ident_bf = CA.alloc([128], BF16)
        LEb = CA.alloc([128], BF16); GEb = CA.alloc([128], BF16); GTb = CA.alloc([128], BF16); LTb = CA.alloc([128], BF16)
        LEf = CA.alloc([128], F32); GEf = CA.alloc([128], F32); GTf = CA.alloc([128], F32); onesf = CA.alloc([128], F32)
        NEGp = CA.alloc([512], BF16); NEGn = CA.alloc([512], BF16)
        dif = WA.alloc([128], F32); tmpm = WA.alloc([128], F32)
        P.op("pool", lambda e: e.iota(dif, pattern=[[1, 128]], base=0, channel_multiplier=-1,
                                      allow_small_or_imprecise_dtypes=True), (), ["dif"])
        memset("dve", onesf, 1.0, ["onesf"])
        cp("dve", ones_bf, onesf, ["onesf"], ["ones_bf"])
        for (mf, mb, op_) in ((LEf, LEb, ALU.is_ge), (GEf, GEb, ALU.is_le), (GTf, GTb, ALU.is_lt), (tmpm, LTb, ALU.is_gt)):
            ts("dve", mf, dif, 0.0, None, op_, None, ["dif"], [("m", id(mf))])
            cp("dve", mb, mf, [("m", id(mf))], [("mb", id(mb))])
        identf = CA.alloc([128], F32)
        ts("dve", identf, dif, 0.0, None, ALU.is_equal, None, ["dif"], ["identf"])
        cp("dve", ident_bf, identf, ["identf"], ["ident"])
        ts("dve", tmpm, dif, 0.0, -30000.0, ALU.is_gt, ALU.mult, ["dif"], [("m", id(tmpm))])
        cp("dve", NEGp.rearrange("p (a b) -> p a b", b=128), bc(tmpm, 1, 4), [("m", id(tmpm))], ["NEGp"])
        ts("dve", tmpm, dif, 0.0, -30000.0, ALU.is_lt, ALU.mult, ["dif"], [("m", id(tmpm))])
        cp("dve", NEGn.rearrange("p (a b) -> p a b", b=128), bc(tmpm, 1, 4), [("m", id(tmpm))], ["NEGn"])
        CONSTS = ["ones_bf", "ident", "NEGp", "NEGn", "onesf"] + [("mb", id(x)) for x in (LEb, GEb, GTb, LTb)] + \
                 [("m", id(x)) for x in (LEf, GEf, GTf)]
        cosT = CA.alloc([NCH, 8], F32); sinT = CA.alloc([NCH, 8], F32)
        posf = WA.alloc([NCH], F32); invf = WA.alloc([8], F32)
        ang = WA.alloc([NCH, 8], F32); kf = WA.alloc([NCH, 8], F32); ki = WA.alloc([NCH, 8], I32)
        cp("dve", posf, posi, ["posi"], ["posf"])
        for i in range(8):
            memset("dve", invf[:, i:i + 1], float(500000.0 ** (-(i * 2.0) / 16.0)), ["invf"])
        tt("dve", ang, bc(posf, 2, 8), bc(invf, 1, NCH), ALU.mult, ["posf", "invf"], ["ang"])
        for (tab, shift) in ((sinT, 0.0), (cosT, math.pi / 2)):
            if shift != 0.0:
                ts("dve", ang, ang, shift, None, ALU.add, None, ["ang"], ["ang"])
            ts("dve", ki, ang, 1.0 / (2 * math.pi), None, ALU.mult, None, ["ang"], ["ki"])
            cp("dve", kf, ki, ["ki"], ["kf"])
            stt("dve", kf, kf, -2 * math.pi, ang, ALU.mult, ALU.add, ["kf", "ang"], ["kf"])
            ts("dve", kf, kf, 3.1415925, -3.1415925, ALU.min, ALU.max, ["kf"], ["kf"])
            act(tab, kf, AF.Sin, ["kf"], [("tab", id(tab))])
        a_neg = CA.alloc([16], F32); esink = CA.alloc([8], F32); qw8 = CA.alloc([512], F32)
        act(a_neg, sm[:, O_ALOG:O_ALOG + 16], AF.Exp, ["sm"], ["a_neg"])
        ts("dve", a_neg, a_neg, -1.0, None, ALU.mult, None, ["a_neg"], ["a_neg"])
        act(esink, sm[:, O_SINK:O_SINK + 8], AF.Exp, ["sm"], ["esink"])
        ts("dve", qw8, sm[:, O_QW:O_QW + 512], 0.125, None, ALU.mult, None, ["sm"], ["qw8"])
        LTf = CA.alloc([128], F32)
        ts("dve", LTf, dif, 0.0, None, ALU.is_gt, None, ["dif"], ["LTf0"])
        MKF = CA.alloc([128], F32); MKB = CA.alloc([128], F32)
        stt("dve", MKF, identf, sm[:, O_FLAG:O_FLAG + 1], LTf, ALU.mult, ALU.add, ["identf", "sm", "LTf0"], ["MKF"])
        stt("dve", MKB, identf, sm[:, O_FLAG + 1:O_FLAG + 2], GTf, ALU.mult, ALU.add, ["identf", "sm", ("m", id(GTf))], ["MKB"])
        hb = CA.alloc([512], F32); hbb = CA.alloc([512], BF16)
        DI = CA.alloc([8, 128], BF16)
        for h in range(8):
            ts("dve", DI[:, h, :], identf, sm[:, O_DSK + h:O_DSK + h + 1], None, ALU.mult, None, ["sm", "identf"], ["DI"])
        cvs = CA.alloc([8], F32)
        act(cvs, sm[:, O_C:O_C + 8], AF.Silu, ["sm"], ["cvs"])
        modv = CA.alloc([72], F32); Am = CA.alloc([24], F32); Gm = CA.alloc([24], F32)

        def cast_list(i):
            lst = []
            svg = wg_in[i].rearrange("(kc p) (s f) -> s p kc f", p=128, f=256)
            svu = wu_in[i].rearrange("(kc p) (s f) -> s p kc f", p=128, f=256)
            for s_ in range(11):
                lst.append(lambda s_=s_: dma(wgb[i][s_].rearrange("p (kc f) -> p kc f", f=256), svg[s_], (), [("wgb", i)], ("cast", "wgb", i), eng="pool"))
                lst.append(lambda s_=s_: dma(wub[i][s_].rearrange("p (kc f) -> p kc f", f=256), svu[s_], (), [("wub", i)], ("cast", "wub", i), eng="pool"))
            svd = wd_in[i].rearrange("(fc p) (m c) -> m p fc c", p=128, c=128)
            for m in range(8):
                lst.append(lambda m=m: dma(wdb[i][m].rearrange("p (fc c) -> p fc c", c=128), svd[m], (), [("wdb", i)], ("cast", "wdb", i), eng="pool"))
            return lst

        def cast_ffn(i):
            for f in cast_list(i): f()
        casts1 = cast_list(0)
        for f in casts1[:22]: f()

        wav = w_ada.rearrange("(kc p) n -> p kc n", p=128)
        wab = [WA.alloc([KC, 1024], F32) for _ in range(2)]
        wbb = [WA.alloc([KC, 1024], BF16) for _ in range(2)]
        cvsb = CA.alloc([8], BF16)
        cp("dve", cvsb, cvs, ["cvs"], ["cvsb"])
        for blk in range(9):
            buf = wab[blk % 2]; bk = ("wab", blk % 2)
            bb = wbb[blk % 2]; bbk = ("wbb", blk % 2)
            dma(buf, wav[:, :, blk * 1024:(blk + 1) * 1024], (), [bk], bk)
            cp("dve", bb[:, 0:4, :], buf[:, 0:4, :], [bk], [(bbk, 0)])
            cp("act", bb[:, 4:8, :], buf[:, 4:8, :], [bk], [(bbk, 1)])
            for j in range(8):
                for kc in range(KC):
                    mm(ps[0][:, blk * 8 + j: blk * 8 + j + 1], bb[:, kc, j * 128:(j + 1) * 128], cvsb[:, kc:kc + 1],
                       kc == 0, kc == KC - 1, [(bbk, kc // 4), "cvsb"], [PSK(0)])
        tt("dve", modv, ps[0][:, 0:72], sm[:, O_BADA:O_BADA + 72], ALU.add, [PSK(0), "sm"], ["modv"])
        for i in range(3):
            stt("dve", Am[:, i * 8:(i + 1) * 8], modv[:, (3 * i + 1) * 8:(3 * i + 2) * 8], 1.0,
                sm[:, O_GAIN + i * 8:O_GAIN + (i + 1) * 8], ALU.add, ALU.mult, ["modv", "sm"], ["Am"])
            ts("dve", Gm[:, i * 8:(i + 1) * 8], modv[:, (3 * i + 2) * 8:(3 * i + 3) * 8], 1.0, (1.0 if i == 1 else 0.5),
               ALU.add, ALU.mult, ["modv"], ["Gm"])
        Bm = lambda i, kc: modv[:, (3 * i) * 8 + kc:(3 * i) * 8 + kc + 1]
        scratch1 = CA.alloc([8], F32)
        dtraw = CA.alloc([NCH, 16], F32)
        wdt16 = CA.alloc([KC, 16], BF16)
        dma(wdt16, w_in_d.rearrange("(kc p) n -> p kc n", p=128)[:, :, 2304:2320], (), ["wdt16"], "wdt16", eng="pool")

        def barrier():
            P.barrier(lambda e: e.memset(scratch1, 0.0))
            WA.reset()

        def norm_a(xt, xkey, sq, sqk):
            act(sq, xt, AF.Square, [xkey], [sqk])

        def norm_b(sq, sqk, bank, T=512):
            for kc in range(KC):
                mm(ps[bank][:, 0:T], ones_bf, sq[:, kc, :], kc == 0, kc == KC - 1, [sqk], [PSK(bank)])

        def norm_c1(rstd, rk, bank, T=512):
            ts("dve", rstd, ps[bank][:, 0:T], 1.0 / D, EPS, ALU.mult, ALU.add, [PSK(bank)], [rk])

        def norm_c2(rstd, rk):
            act(rstd, rstd, AF.Sqrt, [rk], [rk])

        def norm_c2r(rstd, rk):
            recip(rstd, rstd, [rk], [rk])

        def norm_c3(xt, xkey, u, ukey, rstd, rk, i, tmps, tks, kcs):
            for kc in kcs:
                tb = tmps[kc % 2]; tk = tks[kc % 2]
                tt("dve", tb, xt[:, kc, :], rstd, ALU.mult, [xkey, rk], [tk])
                ts("dve", u[:, kc, :], tb, Am[:, i * 8 + kc:i * 8 + kc + 1], Bm(i, kc), ALU.mult, ALU.add, [tk], [ukey])

        def norm_c(xt, xkey, u, ukey, rstd, rk, i, tmps, tks, bank, T=512):
            norm_c1(rstd, rk, bank); norm_c2(rstd, rk); norm_c2r(rstd, rk)
            norm_c3(xt, xkey, u, ukey, rstd, rk, i, tmps, tks, range(KC))

        def norm_tile(xt, xkey, u, ukey, sq, rstd, i, tmps, T=512):
            norm_a(xt, xkey, sq, "sq")
            norm_b(sq, "sq", 6)
            norm_c(xt, xkey, u, ukey, rstd, "rstd", i, tmps, (("sg", 0), ("sg", 1)), 6)

        def run_hooks(hooks, key):
            if hooks and key in hooks:
                for f in hooks[key]: f()

        def ffn_up(fi, u, ukey, hT, rings, hooks=None, T=512):
            wgu, wdr, sgb = rings
            for s in range(11):
                slot = s % len(wgu)
                gs, us = wgu[slot]
                gk = ("wg", slot); uk_ = ("wu", slot)
                dma(gs, wgb[fi][s].rearrange("p (kc f) -> p kc f", f=256), [("wgb", fi)], [gk], ("wg", slot))
                dma(us, wub[fi][s].rearrange("p (kc f) -> p kc f", f=256), [("wub", fi)], [uk_], ("wu", slot))
                for j in range(2):
                    fc = 2 * s + j
                    bG = (fc % 2) * 2; bU = bG + 1
                    for kc in range(KC):
                        mm(ps[bG][:, 0:T], gs[:, kc, j * 128:(j + 1) * 128], u[:, kc, :], kc == 0, kc == KC - 1, [gk, ukey], [PSK(bG)])
                    for kc in range(KC):
                        mm(ps[bU][:, 0:T], us[:, kc, j * 128:(j + 1) * 128], u[:, kc, :], kc == 0, kc == KC - 1, [uk_, ukey], [PSK(bU)])
                    sg = sgb[fc % 2]; sk = ("sg", fc % 2)
                    act(sg, ps[bG][:, 0:T], AF.Silu, [PSK(bG)], [sk])
                    tt("dve", hT[:, fc, :], ps[bU][:, 0:T], sg, ALU.mult, [PSK(bU), sk], [("h", fc)])
                    run_hooks(hooks, ("c", fc))
                run_hooks(hooks, s)

        def ffn_down(fi, xt, xkey, hT, gi, rings, hooks=None, T=512, banks=(4, 5)):
            wgu, wdr, sgb = rings
            run_hooks(hooks, -1)
            for m in range(8):
                slot = m % len(wdr)
                ws = wdr[slot]; wk = ("wd", slot)
                dma(ws, wdb[fi][m].rearrange("p (fc c) -> p fc c", c=128), [("wdb", fi)], [wk], wk)
                b = banks[m % 2]
                for fc in range(FC):
                    mm(ps[b][:, 0:T], ws[:, fc, :], hT[:, fc, :], fc == 0, fc == FC - 1, [wk, ("h", fc)], [PSK(b)])
                    if fc == 10: run_hooks(hooks, ("h", m))
                stt("dve", xt[:, m, :], ps[b][:, 0:T], Gm[:, gi * 8 + m:gi * 8 + m + 1], xt[:, m, :], ALU.mult, ALU.add,
                    [PSK(b), xkey], [xkey])
                run_hooks(hooks, m)

        def ffn_tile(fi, xt, xkey, u, ukey, hT, gi, rings, T=512):
            ffn_up(fi, u, ukey, hT, rings)
            ffn_down(fi, xt, xkey, hT, gi, rings)

        barrier()
        for f in casts1[22:]: f()
        casts2 = cast_list(1)
        dma(winb.rearrange("p (kc n) -> p kc n", n=INW), w_in_d.rearrange("(kc p) n -> p kc n", p=128), (), ["winb"], "winb", eng="pool")
        xts = [WA.alloc([KC, 512], F32) for _ in range(2)]
        sqA = WA.alloc([KC, 512], BF16); sqB = WA.alloc([KC, 512], BF16)
        u1 = [WA.alloc([KC, 512], BF16) for _ in range(2)]; u2 = WA.alloc([KC, 512], BF16)
        hT = WA.alloc([FC, 512], BF16)
        rstdA = WA.alloc([512], F32); rstdB = WA.alloc([512], F32)
        sgb = [WA.alloc([512], F32) for _ in range(2)]
        ntm = [WA.alloc([512], F32) for _ in range(2)]
        wgu = [(WA.alloc([KC, 256], BF16), WA.alloc([KC, 256], BF16)) for _ in range(3)]
        wdr = [WA.alloc([FC, 128], BF16) for _ in range(3)]
        rings = (wgu, wdr, sgb)
        xTv = xT.rearrange("(kc p) t -> p kc t", p=128)
        X1v = X1.rearrange("(kc p) t -> p kc t", p=128)
        U2v = U2.rearrange("(kc p) t -> p kc t", p=128)
        outv = outT.rearrange("(kc p) t -> p kc t", p=128)
        NTK = (("ntm", 0), ("ntm", 1))

        def p1_x(ti): return xts[ti % 2], ("x", ti % 2)

        def p1_load(ti):
            dma(xts[ti % 2], xTv[:, :, ti * 512:(ti + 1) * 512], (), [("x", ti % 2)], ("xl", ti % 2))

        def n1a(ti): norm_a(*p1_x(ti), sqA, "sqA")
        def n1b(ti): norm_b(sqA, "sqA", 6)
        def n1c1(ti): norm_c1(rstdA, "rstdA", 6)
        def n1c2(ti): norm_c2(rstdA, "rstdA")
        def n1c2r(ti): norm_c2r(rstdA, "rstdA")
        def n1c3(ti, kcs):
            xt, xk = p1_x(ti)
            norm_c3(xt, xk, u1[ti % 2], ("u1", ti % 2), rstdA, "rstdA", 0, ntm, NTK, kcs)

        def n2a(ti):
            xt, xk = p1_x(ti)
            if ti < NT1 // 2:
                dma(X1v[:, :, ti * 512:(ti + 1) * 512], xt, [xk], [("X1", ti)], ("xs", ti % 2))
            norm_a(xt, xk, sqB, "sqB")
        def n2b(ti): norm_b(sqB, "sqB", 7)
        def n2c1(ti): norm_c1(rstdB, "rstdB", 7)
        def n2c2(ti): norm_c2(rstdB, "rstdB")
        def n2c2r(ti): norm_c2r(rstdB, "rstdB")
        def n2c3(ti, kcs):
            xt, xk = p1_x(ti)
            norm_c3(xt, xk, u2, "u2", rstdB, "rstdB", 1, ntm2, NTK2, kcs)
        def n2s(ti):
            dma(U2v[:, :, ti * 512:(ti + 1) * 512], u2, ["u2"], [("U2", ti)], "u2s")
        def n2d(ti):
            for ci in range(4):
                for kc in range(KC):
                    mm(ps[6][:, ci * 16:(ci + 1) * 16], u2[:, kc, ci * 128:(ci + 1) * 128], wdt16[:, kc, :], kc == 0, kc == KC - 1,
                       ["u2"], [PSK(6)])
            cp("act", dtraw[:, ti * 4:(ti + 1) * 4, :], ps[6][:, 0:64].rearrange("p (c h) -> p c h", h=16), [PSK(6)], ["dtraw"])
            if ti + 2 < NT1: p1_load(ti + 2)
            per = -(-len(casts2) // NT1)
            for f in casts2[ti * per:(ti + 1) * per]: f()

        ntm2 = [WA.alloc([512], F32) for _ in range(2)]
        NTK2 = (("ntm2", 0), ("ntm2", 1))
        p1_load(0)
        if NT1 > 1: p1_load(1)
        n1a(0); n1b(0); n1c1(0); n1c2(0); n1c2r(0); n1c3(0, range(KC))
        L = lambda f, *a: (lambda: f(*a))
        for ti in range(NT1):
            xt, xk = p1_x(ti)
            hu = {}
            if ti > 0:
                t = ti - 1
                hu = {0: [L(n2b, t)], 1: [L(n2c1, t)], 2: [L(n2c2, t)], 3: [L(n2c2r, t)], 4: [L(n2c3, t, (0, 1))], 5: [L(n2c3, t, (2, 3))],
                      6: [L(n2c3, t, (4, 5))], 7: [L(n2c3, t, (6, 7)), L(n2s, t)], 9: [L(n2d, t)]}
            ffn_up(0, u1[ti % 2], ("u1", ti % 2), hT, rings, hu)
            hd = {7: [L(n2a, ti)]}
            if ti + 1 < NT1:
                t = ti + 1
                hd[-1] = [L(n1a, t)]
                hd[0] = [L(n1b, t)]; hd[1] = [L(n1c1, t)]; hd[2] = [L(n1c2, t)]; hd[3] = [L(n1c2r, t)]
                hd[4] = [L(n1c3, t, (0, 1, 2))]; hd[5] = [L(n1c3, t, (3, 4, 5))]; hd[6] = [L(n1c3, t, (6, 7))]
            ffn_down(0, xt, xk, hT, 0, rings, hd)
        t = NT1 - 1
        n2b(t); n2c1(t); n2c2(t); n2c2r(t); n2c3(t, range(KC)); n2s(t); n2d(t)

        barrier()
        STORE_ENG[0] = "sp"
        win = WA.alloc([KC, INW], BF16)
        dma(win, winb.rearrange("p (kc n) -> p kc n", n=INW), (), ["win"], "win")
        dtv = WA.alloc([NCH, 16], F32); da = WA.alloc([NCH, 16], F32)
        T1 = WA.alloc([NCH, 16], F32); T2 = WA.alloc([NCH, 16], F32); T3 = WA.alloc([NCH, 16], F32)
        DEC = WA.alloc([NCH, 16], F32); WDT = WA.alloc([NCH, 16], F32); EBDt = WA.alloc([NCH, 16], F32)
        tt("dve", T1, dtraw, bc(sm[:, O_DTB:O_DTB + 16], 1, NCH), ALU.add, ["dtraw"], ["T1"])
        ts("dve", T2, T1, -1.0, None, ALU.mult, None, ["T1"], ["T2"])
        tt("dve", T2, T2, T1, ALU.min, ["T2", "T1"], ["T2"])
        act(T3, T2, AF.Exp, ["T2"], ["T3"])
        act(T3, T3, AF.Ln, ["T3"], ["T3"], bias=1.0)
        stt("dve", dtv, T1, 0.0, T3, ALU.max, ALU.add, ["T1", "T3"], ["dtv"])
        tt("dve", da, dtv, bc(a_neg, 1, NCH), ALU.mult, ["dtv"], ["da"])
        CW = 512 // 8
        for c0 in range(0, NCH, CW):
            c1 = min(NCH, c0 + CW); n = c1 - c0
            for (half, msk, bnk) in ((0, LEf, 0), (1, GEf, 1)):
                o3 = ps[bnk][:, 0:n * 8].rearrange("p (c h) -> p c h", h=8)
                mm(o3, msk, da[:, c0:c1, half * 8:(half + 1) * 8], True, True, ["da"], [PSK(bnk)])
                cp("act", T2[:, c0:c1, half * 8:(half + 1) * 8], o3, [PSK(bnk)], ["T2"])
        CW2 = 512 // 16
        for c0 in range(0, NCH, CW2):
            c1 = min(NCH, c0 + CW2); n = c1 - c0
            bnk = 2 + (c0 // CW2) % 2
            o3 = ps[bnk][:, 0:n * 16].rearrange("p (c h) -> p c h", h=16)
            mm(o3, onesf, da[:, c0:c1, :], True, True, ["da"], [PSK(bnk)])
            cp("act", T1[:, c0:c1, :], o3, [PSK(bnk)], ["T1"])
        act(T3, T2, AF.Exp, ["T2"], ["T3"])
        act(DEC, T1, AF.Exp, ["T1"], ["DEC"])
        tt("dve", WDT, T1, T2, ALU.subtract, ["T1", "T2"], ["WDT"])
        act(WDT, WDT, AF.Exp, ["WDT"], ["WDT"])
        tt("dve", WDT, WDT, dtv, ALU.mult, ["WDT", "dtv"], ["WDT"])
        cp("dve", EBDt[:, :, 0:8], T3[:, :, 8:16], ["T3"], ["EBDt"])
        cp("dve", EBDt[:, :, 8:16], DEC[:, :, 8:16], ["DEC"], ["EBDt"])
        for c0 in range(0, NH, 16):
            c1 = min(NH, c0 + 16)
            dma(EBD[c0:c1].rearrange("c p n -> p c n"), EBDt[:, c0:c1, :], ["EBDt"], [("EBD", c0)], "ebds")
        TAB = ["dtv", "da", "T3", "DEC", "WDT"]
        DG = WA.alloc([8, 5, 128], BF16)
        for cc in range(8):
            for k in range(5):
                ts("dve", DG[:, cc, k, :], identf, sm[:, O_CW + cc * 5 + k:O_CW + cc * 5 + k + 1], None, ALU.mult, None, [], ["DG"])
        u2w = [WA.alloc([KC, 260], BF16) for _ in range(2)]
        xpres = [WA.alloc([8, 260], BF16) for _ in range(2)]
        xcs = [WA.alloc([8, 256], BF16) for _ in range(2)]
        PB = 2
        szb = [WA.alloc([512], BF16) for _ in range(PB)]
        qf = [WA.alloc([512], F32) for _ in range(PB)]; qq = [WA.alloc([512], F32) for _ in range(PB)]
        qss = [WA.alloc([8], F32) for _ in range(PB)]; qb = [WA.alloc([512], BF16) for _ in range(PB)]
        kfb = [WA.alloc([128], F32) for _ in range(PB)]; kq = [WA.alloc([128], F32) for _ in range(PB)]
        kss = [WA.alloc([8], F32) for _ in range(PB)]; kb = [WA.alloc([128], BF16) for _ in range(PB)]
        ra = [WA.alloc([64], F32) for _ in range(PB)]; rb = [WA.alloc([64], F32) for _ in range(PB)]
        vext = [WA.alloc([2, 65], BF16) for _ in range(PB)]
        QTb = [WA.alloc([1024], BF16) for _ in range(PB)]; KTb = [WA.alloc([256], BF16) for _ in range(PB)]
        ypart = [WA.alloc([512], F32) for _ in range(PB)]; sbt = [WA.alloc([512], F32) for _ in range(PB)]
        xsB = WA.alloc([768], BF16)
        xdt_f = WA.alloc([512], BF16); xdt_b = WA.alloc([512], BF16); xw_f = WA.alloc([512], BF16); xw_b = WA.alloc([512], BF16)
        CBf = WA.alloc([2, 128], F32); CBb = WA.alloc([2, 128], F32)
        lT = [WA.alloc([4, 128], BF16) for _ in range(4)]
        exs = [WA.alloc([512], F32) for _ in range(4)]
        MT = [WA.alloc([4, 128], BF16) for _ in range(4)]
        hf = WA.alloc([512], F32); hfb = WA.alloc([512], BF16); htmp = WA.alloc([512], F32)
        ytmp = WA.alloc([512], F32)
        zt = WA.alloc([256], BF16)
        memset("dve", hf, 0.0, ["hf"]); memset("dve", hfb, 0.0, ["hfb"])
        for pb in range(PB): memset("dve", vext[pb], 1.0, [("vext", pb)])
        memset("dve", zt, 0.0, ["zt"])
        dma(KTS[0], zt[0:64, 0:256], ["zt"], [("KTS", 0)], "zpad")
        dma(VES[0], zt[:, 0:130], ["zt"], [("VES", 0)], "zpad")

        def load_u2w(tj, slot):
            buf = u2w[slot % 2]; k = ("u2w", slot % 2)
            t0 = tj * 256
            lo = max(t0 - 2, 0); hi = min(t0 + 258, S)
            if lo != t0 - 2 or hi != t0 + 258:
                memset("pool", buf, 0.0, [k])
            dma(buf[:, :, lo - (t0 - 2):hi - (t0 - 2)], U2v[:, :, lo:hi], (), [k], k)

        def emit_zq(n_t_, ci_):
            uw_ = u2w[n_t_ % 2]; uk__ = ("u2w", n_t_ % 2)
            cols_ = slice(2 + ci_ * 128, 2 + ci_ * 128 + 128)
            for kc in range(KC):
                mm(ps[0][:, 0:512], uw_[:, kc, cols_], win[:, kc, 0:512], kc == 0, kc == KC - 1, ["win", uk__], [PSK(0)])
            for kc in range(KC):
                mm(ps[1][:, 0:512], uw_[:, kc, cols_], win[:, kc, 1536:2048], kc == 0, kc == KC - 1, ["win", uk__], [PSK(1)])

        memset("dve", hb, 0.0, ["hb"])
        tile_order = list(range(NT2 - 1, NT2 // 2 - 1, -1)) + list(range(0, NT2 // 2))
        own_next = {}
        _own = [(n_t_, tj_ * 2 + ci_, ci_) for n_t_, tj_ in enumerate(tile_order) if tj_ < NT2 // 2 for ci_ in (0, 1)]
        for a_, b_ in zip(_own[:-1], _own[1:]): own_next[a_[1]] = (b_[0], b_[2])
        xbc_done = set()
        zq_pending = [None]

        def xbc_steps(n_t_, far_, shared_):
            uw_ = u2w[n_t_ % 2]; uk_ = ("u2w", n_t_ % 2); par_ = n_t_ % 2
            xc_ = xcs[par_]; xp_ = xpres[par_]
            ncc_ = 6 if far_ else 8

            def xproj(cc):
                b = 6 + cc % 2
                for kc in range(KC):
                    mm(ps[b][:, 0:260], win[:, kc, 512 + cc * 128:512 + (cc + 1) * 128], uw_[:, kc, :], kc == 0, kc == KC - 1,
                       ["win", uk_], [PSK(b)])
                cp("dve" if far_ else "act", xp_[:, cc, :], ps[b][:, 0:260], [PSK(b)], [("xpre", par_, cc)])

            def xconv(cc):
                b = (6 if shared_ else 4) + cc % 2
                for k in range(5):
                    mm(ps[b][:, 0:256], DG[:, cc, k, :], xp_[:, cc, k:k + 256], k == 0, k == 4, ["DG", ("xpre", par_, cc)], [PSK(b)])
                act(xc_[:, cc, :], ps[b][:, 0:256], AF.Silu, [PSK(b), "sm"], [("xc", par_, cc)], bias=sm[:, O_CB + cc:O_CB + cc + 1])

            steps = []
            for cc in range(ncc_):
                def f(cc=cc):
                    if cc == 0: xproj(0)
                    if cc + 1 < ncc_: xproj(cc + 1)
                    xconv(cc)
                steps.append(f)
            return steps

        load_u2w(tile_order[0], 0)
        for n_t, tj in enumerate(tile_order):
            far = tj >= NT2 // 2
            uw = u2w[n_t % 2]; uk = ("u2w", n_t % 2)
            if n_t + 1 < len(tile_order): load_u2w(tile_order[n_t + 1], n_t + 1)
            par = n_t % 2
            xc = xcs[par]; xpre = xpres[par]
            if n_t not in xbc_done:
                for f in xbc_steps(n_t, far, False): f()
            XC = [("xc", par, i) for i in range(8)]
            xsteps = []
            if (not far) and n_t + 1 < len(tile_order):
                xsteps = xbc_steps(n_t + 1, False, True)
                xbc_done.add(n_t + 1)

            def xstep():
                if ovl and xsteps: xsteps.pop(0)()
            for ci in ((1, 0) if far else (0, 1)):
                c = tj * 2 + ci; pb = c % PB
                own = not far; halo = (c == NH)
                ovl = own and ci == 1 and len(xsteps) > 0
                bY, bO = (0, 1) if ovl else (6, 7)
                tc0 = ci * 128
                ucols = slice(2 + tc0, 2 + tc0 + 128)
                xcols = slice(tc0, tc0 + 128)
                dtc = dtv[:, c, :]; dac = da[:, c, :]; Ec = T3[:, c, :]; decc = DEC[:, c, :]; wc = WDT[:, c, :]
                branches = []
                if own: branches.append((qf[pb], ("qf", pb), qq[pb], qss[pb], 8, qb[pb], "q"))
                if own or halo: branches.append((kfb[pb], ("kfb", pb), kq[pb], kss[pb], 2, kb[pb], "k"))
                if own and c == 0:
                    emit_zq(n_t, 0)
                if own or halo:
                    for kc in range(KC):
                        mm(ps[2][:, 0:256], uw[:, kc, ucols], win[:, kc, 2048:2304], kc == 0, kc == KC - 1, ["win", uk], [PSK(2)])
                combos = ((0, 0, GTb, LEb, CBf), (0, 1, GTb, LEb, CBf), (1, 0, LTb, GEb, CBb), (1, 1, LTb, GEb, CBb))
                if own:
                    for mi, (dr, g, msk, rhsm, CBd) in enumerate(combos):
                        col = dr * 8 + g * 4
                        tt("dve", lT[mi], bc(msk, 1, 4), bc(dac[:, col:col + 4], 2, 128), ALU.mult, ["da"], [("lT", mi)])
                for i in range(6):
                    tr(psb[3][:, i * 128:(i + 1) * 128], xc[:, i, xcols], ident_bf, [("xc", par, i)], [PSK(3)])
                if own:
                    cp("act", qf[pb], ps[1][:, 0:512], [PSK(1)], [("qf", pb)])
                    act(szb[pb], ps[0][:, 0:512], AF.Silu, [PSK(0)], [("szb", pb)])
                    dma(SZ[c], szb[pb], [("szb", pb)], [("SZ", c)], ("szs", pb))
                    nxt = own_next.get(c)
                    if nxt is not None:
                        if ovl: zq_pending[0] = nxt
                        else: emit_zq(*nxt)
                if own or halo:
                    cp("act", kfb[pb], ps[2][:, 0:128], [PSK(2)], [("kfb", pb)])
                    cp("act", vext[pb][:, :, 0:64], ps[2][:, 128:256].rearrange("p (a b) -> p a b", b=64), [PSK(2)], [("vext", pb)])
                cp("act", xsB, psb[3][:, 0:768], [PSK(3)], ["xsB"])
                xs3 = xsB[:, 0:512].rearrange("p (h d) -> p h d", d=64)
                if own:
                    for g in range(2):
                        mm(ps[2][:, 256 + g * 128:256 + (g + 1) * 128], xc[:, 4 + g, xcols], xc[:, 6 + g, xcols], True, True, XC, [PSK(2)])
                    dma(CTS[c].rearrange("p (g l) -> p g l", l=128), xc[:, 6:8, xcols], XC, [("CTS", c)], "cts")
                    xstep()
                    for mi, (dr, g, msk, rhsm, CBd) in enumerate(combos):
                        bS = 4 + mi % 2
                        for r in range(4):
                            mm(ps[bS][:, r * 128:(r + 1) * 128], lT[mi][:, r, :], rhsm, True, True, [("lT", mi)], [PSK(bS)])
                        act(exs[mi], ps[bS][:, 0:512], AF.Exp, [PSK(bS)], [("exs", mi)])
                        xstep()
                        if mi == 1:
                            p2v = ps[2][:, 256:512].rearrange("p (g l) -> p g l", l=128)
                            tt("dve", CBf, p2v, bc(MKF, 1, 2), ALU.mult, [PSK(2)], ["CBf"])
                            tt("dve", CBb, p2v, bc(MKB, 1, 2), ALU.mult, [PSK(2)], ["CBb"])
                            for (dst, col_, nm) in ((xdt_f, dtc[:, 0:8], "xdt_f"), (xdt_b, dtc[:, 8:16], "xdt_b")):
                                tt("dve", dst.rearrange("p (h d) -> p h d", d=64), xs3, bc(col_, 2, 64), ALU.mult, ["xsB", "dtv"], [nm])
                    for mi, (dr, g, msk, rhsm, CBd) in enumerate(combos):
                        tt("dve", MT[mi], exs[mi].rearrange("p (r l) -> p r l", l=128), bc(CBd[:, g, :], 1, 4), ALU.mult,
                           [("exs", mi), "CBf", "CBb"], [("MT", mi)])
                for (src, sk, sqb, ssb, nh, outb, nm) in branches:
                    tt("dve", sqb[:, 0:nh * 64], src, src, ALU.mult, [sk], [(nm + "sq", pb)])
                    red(ssb[:, 0:nh], sqb[:, 0:nh * 64].rearrange("p (h d) -> p h d", d=64), [(nm + "sq", pb)], [(nm + "ss", pb)])
                    ts("dve", ssb[:, 0:nh], ssb[:, 0:nh], 1.0 / 64, EPS, ALU.mult, ALU.add, [(nm + "ss", pb)], [(nm + "ss", pb)])
                    act(ssb[:, 0:nh], ssb[:, 0:nh], AF.Ln, [(nm + "ss", pb)], [(nm + "ss", pb)])
                    act(ssb[:, 0:nh], ssb[:, 0:nh], AF.Exp, [(nm + "ss", pb)], [(nm + "ss", pb)], scale=-0.5)
                if own:
                    for h in range(8):
                        g = h // 4; r = h % 4
                        hs = slice(h * 64, (h + 1) * 64)
                        mm(ps[bY][:, hs], MT[g][:, r, :], xdt_f[:, hs], True, False, [("MT", g), "xdt_f"], [PSK(bY)])
                        mm(ps[bY][:, hs], MT[2 + g][:, r, :], xdt_b[:, hs], False, False, [("MT", 2 + g), "xdt_b"], [PSK(bY)])
                        mm(ps[bY][:, hs], DI[:, h, :], xsB[:, hs], False, True, ["xsB"], [PSK(bY)])
                    for g in range(2):
                        mm(ps[bO][:, g * 256:(g + 1) * 256], xc[:, 6 + g, xcols], hfb[:, g * 256:(g + 1) * 256], True, True,
                           XC + ["hfb"], [PSK(bO)])
                    xstep()
                    tt("dve", xw_f.rearrange("p (h d) -> p h d", d=64), xs3, bc(wc[:, 0:8], 2, 64), ALU.mult, ["xsB", "WDT"], ["xw_f"])
                tt("dve", xw_b.rearrange("p (h d) -> p h d", d=64), xs3, bc(wc[:, 8:16], 2, 64), ALU.mult, ["xsB", "WDT"], ["xw_b"])
                if own:
                    tt("dve", ytmp.rearrange("p (h d) -> p h d", d=64), ps[bO][:, 0:512].rearrange("p (h d) -> p h d", d=64),
                       bc(Ec[:, 0:8], 2, 64), ALU.mult, [PSK(bO), "T3"], ["ytmp"])
                    tt("dve", ypart[pb], ytmp, ps[bY][:, 0:512], ALU.add, ["ytmp", PSK(bY)], [("ypart", pb)])
                    if zq_pending[0] is not None:
                        emit_zq(*zq_pending[0]); zq_pending[0] = None
                    dma(YP[c], ypart[pb], [("ypart", pb)], [("YP", c)], ("yps", pb))
                for (bH, xw, xwn) in (((2, xw_f, "xw_f"), (3, xw_b, "xw_b")) if own else ((3, xw_b, "xw_b"),)):
                    for g in range(2):
                        mm(ps[bH][:, g * 256:(g + 1) * 256], xsB[:, 512 + g * 128:512 + (g + 1) * 128], xw[:, g * 256:(g + 1) * 256],
                           True, True, ["xsB", xwn] + ([("kfb", pb), ("vext", pb)] if bH == 2 else []), [PSK(bH)])
                if own:
                    tt("dve", htmp.rearrange("p (h d) -> p h d", d=64), hf.rearrange("p (h d) -> p h d", d=64), bc(decc[:, 0:8], 2, 64),
                       ALU.mult, ["hf", "DEC"], ["htmp"])
                    tt("dve", hf, htmp, ps[2][:, 0:512], ALU.add, ["htmp", PSK(2)], ["hf"])
                    cp("act", hfb, hf, ["hf"], ["hfb"])
                    cp("act", sbt[pb], ps[3][:, 0:512], [PSK(3)], [("sbt", pb)])
                    dma(SBS[c], sbt[pb], [("sbt", pb)], [("SBS", c)], ("sbss", pb))
                else:
                    tt("dve", htmp.rearrange("p (h d) -> p h d", d=64), hb.rearrange("p (h d) -> p h d", d=64), bc(decc[:, 8:16], 2, 64),
                       ALU.mult, ["hb", "DEC"], ["htmp"])
                    tt("dve", hb, htmp, ps[3][:, 0:512], ALU.add, ["htmp", PSK(3)], ["hb"])
                    if halo:
                        cp("act", hbb, hb, ["hb"], ["hbb"])
                if own:
                    xstep(); xstep()
                    while ovl and xsteps: xsteps.pop(0)()
                for (src, sk, sqb, ssb, nh, outb, nm) in branches:
                    s3 = src.rearrange("p (h d) -> p h d", d=64)
                    tt("dve", s3, s3, bc(ssb[:, 0:nh], 2, 64), ALU.mult, [sk, (nm + "ss", pb)], [sk])
                    if nm == "q":
                        tt("dve", src, src, qw8, ALU.mult, [sk], [sk])
                    else:
                        tt("dve", s3, s3, bc(sm[:, O_KW:O_KW + 64], 1, 2), ALU.mult, [sk], [sk])
                    cp("act", outb, src, [sk], [(nm + "b", pb)])
                    o3 = outb.rearrange("p (h d) -> p h d", d=64)
                    t1 = s3[:, :, 0:8]; t2 = s3[:, :, 8:16]
                    cb_ = bc(cosT[:, c, :], 1, nh); sb_ = bc(sinT[:, c, :], 1, nh)
                    ra3 = ra[pb][:, 0:nh * 8].rearrange("p (h d) -> p h d", d=8); rb3 = rb[pb][:, 0:nh * 8].rearrange("p (h d) -> p h d", d=8)
                    tt("dve", ra3, t1, cb_, ALU.mult, [sk], [("ra", pb)])
                    tt("dve", rb3, t2, sb_, ALU.mult, [sk], [("rb", pb)])
                    tt("dve", o3[:, :, 0:8], ra3, rb3, ALU.subtract, [("ra", pb), ("rb", pb)], [(nm + "b", pb)])
                    tt("dve", ra3, t2, cb_, ALU.mult, [sk], [("ra", pb)])
                    tt("dve", rb3, t1, sb_, ALU.mult, [sk], [("rb", pb)])
                    tt("dve", o3[:, :, 8:16], ra3, rb3, ALU.add, [("ra", pb), ("rb", pb)], [(nm + "b", pb)])
                if own:
                    for h in range(8):
                        tr(psb[4][0:64, h * 128:(h + 1) * 128], qb[pb][:, h * 64:(h + 1) * 64], ident_bf, [("qb", pb)], [PSK(4)])
                    cp("act", QTb[pb][0:64, :], psb[4][0:64, 0:1024], [PSK(4)], [("QTb", pb)])
                    dma(QTS[c], QTb[pb][0:64, :], [("QTb", pb)], [("QTS", c)], ("qts", pb))
                if own or halo:
                    for h in range(2):
                        tr(psb[5][0:64, h * 128:(h + 1) * 128], kb[pb][:, h * 64:(h + 1) * 64], ident_bf, [("kb", pb)], [PSK(5)])
                    cp("act", KTb[pb][0:64, :], psb[5][0:64, 0:256], [PSK(5)], [("KTb", pb)])
                    dma(KTS[c + 1], KTb[pb][0:64, :], [("KTb", pb)], [("KTS", c + 1)], ("kts", pb))
                    dma(VES[c + 1].rearrange("p (a b) -> p a b", b=65), vext[pb], [("vext", pb)], [("VES", c + 1)], ("ves", pb))

        barrier()
        STORE_ENG[0] = "pool"
        wout = WA.alloc([KC, D], BF16)
        dma(wout, w_out_d.rearrange("(kc p) n -> p kc n", p=128), (), ["wout"], "wout", eng="pool")
        xts = [WA.alloc([KC, 512], F32) for _ in range(2)]
        sq = WA.alloc([KC, 512], BF16)
        u3 = WA.alloc([KC, 512], BF16)
        hT = WA.alloc([FC, 512], BF16)
        rstd = WA.alloc([512], F32)
        sgb = [WA.alloc([512], F32) for _ in range(2)]
        ntm3 = sgb
        NTK3 = (("sg", 0), ("sg", 1))
        wgu = [(WA.alloc([KC, 256], BF16), WA.alloc([KC, 256], BF16)) for _ in range(2)]
        wdr = [WA.alloc([FC, 128], BF16) for _ in range(2)]
        rings = (wgu, wdr, sgb)
        ymT = WA.alloc([KC, 512], BF16)
        NB = 2
        ctl = [WA.alloc([2, 128], BF16) for _ in range(NB)]
        ebl = [WA.alloc([16], F32) for _ in range(NB)]
        sbl = [WA.alloc([512], F32) for _ in range(NB)]
        ypl = [WA.alloc([512], F32) for _ in range(NB)]
        szl = [WA.alloc([512], BF16) for _ in range(NB)]
        qtl = [WA.alloc([1024], BF16) for _ in range(NB)]
        ktl = [WA.alloc([3, 256], BF16) for _ in range(NB)]
        vel = [WA.alloc([3, 130], BF16) for _ in range(NB)]
        hb2 = WA.alloc([512], F32)
        yt1 = WA.alloc([512], F32); yg = WA.alloc([512], F32); junk = WA.alloc([512], BF16)
        gss = WA.alloc([4], F32)
        ymix = WA.alloc([1024], BF16)
        PT = [[WA.alloc([512], BF16) for _ in range(3)] for _ in range(2)]
        den = WA.alloc([8], F32)
        NT3 = NT1 // 2
        order = [c for ti in range(NT3 - 1, -1, -1) for c in range(ti * 4 + 3, ti * 4 - 1, -1)]
        BA = (4, 5); BV = 6; BT = 7

        def load_chunk(n):
            c = order[n]; s = n % NB
            dma(ctl[s], CTS[c].rearrange("p (g l) -> p g l", l=128), (), [("ctl", s)], ("l_ct", s))
            dma(ebl[s], EBD[c], (), [("ebl", s)], ("l_eb", s))
            dma(sbl[s], SBS[c], (), [("sbl", s)], ("l_sb", s))
            dma(ypl[s], YP[c], (), [("ypl", s)], ("l_yp", s))
            dma(szl[s], SZ[c], (), [("szl", s)], ("l_sz", s))
            dma(qtl[s][0:64, :], QTS[c], (), [("qtl", s)], ("l_qt", s))
            dma(ktl[s][0:64, :, :], KTS[c:c + 3].rearrange("b p n -> p b n"), (), [("ktl", s)], ("l_kt", s))
            dma(vel[s], VES[c:c + 3].rearrange("b p n -> p b n"), (), [("vel", s)], ("l_ve", s))

        def chunk_stage_fns(n):
            c = order[n]; s = n % NB; ci = c % 4

            def A():
                if n + 1 < len(order): load_chunk(n + 1)
                for g in range(2):
                    mm(ps[BV][:, g * 256:(g + 1) * 256], ctl[s][:, g, :], hbb[:, g * 256:(g + 1) * 256], True, True,
                       [("ctl", s), "hbb"], [PSK(BV)])
                tt("dve", yt1.rearrange("p (h d) -> p h d", d=64), ps[BV][:, 0:512].rearrange("p (h d) -> p h d", d=64),
                   bc(ebl[s][:, 0:8], 2, 64), ALU.mult, [PSK(BV), ("ebl", s)], ["yt1"])
                tt("dve", yt1, yt1, ypl[s], ALU.add, ["yt1", ("ypl", s)], ["yt1"])
                tt("pool", hb2.rearrange("p (h d) -> p h d", d=64), hb.rearrange("p (h d) -> p h d", d=64), bc(ebl[s][:, 8:16], 2, 64),
                   ALU.mult, ["hb", ("ebl", s)], ["hb2"])
                tt("pool", hb, hb2, sbl[s], ALU.add, ["hb2", ("sbl", s)], ["hb"])
                cp("act", hbb, hb, ["hb"], ["hbb"])
                tt("dve", yg, yt1, szl[s], ALU.mult, ["yt1", ("szl", s)], ["yg"])
                memset("dve", gss[:, 0:1], 0.0, ["gss"])
                act(junk, yg, AF.Square, ["yg", "gss"], ["junk", "gss"], accum=gss[:, 0:1])

            def scores(kvh, blk):
                b = BA[(kvh * 3 + blk) % 2]
                mm(ps[b][:, 0:512], ktl[s][0:64, blk, kvh * 128:(kvh + 1) * 128], qtl[s][0:64, kvh * 512:(kvh + 1) * 512],
                   True, blk == 1, [("ktl", s), ("qtl", s)], [PSK(b)])
                if blk != 1:
                    mm(ps[b][:, 0:512], ident_bf, NEGp if blk == 0 else NEGn, False, True, [], [PSK(b)])
                act(PT[kvh][blk], ps[b][:, 0:512], AF.Exp, [PSK(b)], [("PT", kvh, blk)])

            def pv(kvh):
                for r in range(4):
                    for blk in range(3):
                        mm(ps[BV][:, r * 65:(r + 1) * 65], PT[kvh][blk][:, r * 128:(r + 1) * 128], vel[s][:, blk, kvh * 65:(kvh + 1) * 65],
                           blk == 0, blk == 2, [("PT", kvh, blk), ("vel", s)], [PSK(BV)])
                pv3 = ps[BV][:, 0:260].rearrange("p (r d) -> p r d", d=65)
                tt("dve", den[:, kvh * 4:(kvh + 1) * 4], pv3[:, :, 64], esink[:, kvh * 4:(kvh + 1) * 4], ALU.add,
                   [PSK(BV)], [("den", kvh)])
                recip(den[:, kvh * 4:(kvh + 1) * 4], den[:, kvh * 4:(kvh + 1) * 4], [("den", kvh)], [("den", kvh)])
                tt("dve", ymix[:, 512 + kvh * 256:512 + (kvh + 1) * 256].rearrange("p (r d) -> p r d", d=64), pv3[:, :, 0:64],
                   bc(den[:, kvh * 4:(kvh + 1) * 4], 2, 64), ALU.mult, [PSK(BV), ("den", kvh)], [("ymix_a", kvh)])

            def B():
                scores(0, 0)
                ts("dve", gss[:, 1:2], gss[:, 0:1], 1.0 / 512, EPS, ALU.mult, ALU.add, ["gss"], ["gss1"])
                act(gss[:, 1:2], gss[:, 1:2], AF.Sqrt, ["gss1"], ["gss1"])

            def C():
                scores(0, 1)
                recip(gss[:, 2:3], gss[:, 1:2], ["gss1"], ["gss2"])
                stt("dve", ymix[:, 0:512], yg, gss[:, 2:3], sm[:, O_SNW:O_SNW + 512], ALU.mult, ALU.mult, ["yg", "gss2"], ["ymix_s"])

            def Dd(): scores(0, 2)
            def E(): scores(1, 0); pv(0)
            def F(): scores(1, 1)
            def G(): scores(1, 2)
            def H(): pv(1)

            def I():
                for cc in range(8):
                    tr(psb[BT][:, cc * 128:(cc + 1) * 128], ymix[:, cc * 128:(cc + 1) * 128], ident_bf,
                       ["ymix_s", ("ymix_a", 0), ("ymix_a", 1)], [PSK(BT)])
                cp("act", ymT[:, :, ci * 128:(ci + 1) * 128], psb[BT][:, 0:1024].rearrange("p (a b) -> p a b", b=128), [PSK(BT)], [("ymT", ci)])
            return A, B, C, Dd, E, F, G, H, I

        def p3_x(tix): return xts[tix % 2], ("x", tix % 2)

        def seq(*fs):
            def f():
                for g_ in fs: g_()
            return f

        def tile_pre_stages(tix):
            xt, xk = p3_x(tix)
            ch = [chunk_stage_fns(tix * 4 + cq) for cq in range(4)]
            def chunk_list(q, first):
                c_ = ch[q]
                return [first, seq(c_[2], c_[3]), c_[4], seq(c_[5], c_[6]), c_[7]]
            st = chunk_list(0, seq(ch[0][0], ch[0][1]))
            for q in range(1, 4):
                st.append(seq(ch[q - 1][8], ch[q][0]))
                st += chunk_list(q, ch[q][1])
            st.append(ch[3][8])

            def mk_o(ms):
                def f():
                    for m in ms:
                        b = BV if m % 2 == 0 else BT
                        for cc in range(KC):
                            mm(ps[b][:, 0:512], wout[:, cc, m * 128:(m + 1) * 128], ymT[:, cc, :], cc == 0, cc == KC - 1,
                               ["wout"] + [("ymT", i) for i in range(4)], [PSK(b)])
                        stt("dve", xt[:, m, :], ps[b][:, 0:512], Gm[:, 8 + m:8 + m + 1], xt[:, m, :], ALU.mult, ALU.add, [PSK(b), xk], [xk])
                return f
            st.append(mk_o((0, 1, 2, 3)))
            st.append(seq(mk_o((4, 5, 6, 7)), lambda: norm_a(xt, xk, sq, "sq")))
            st.append(lambda: norm_b(sq, "sq", BV))
            st.append(seq(lambda: norm_c1(rstd, "rstd", BV), lambda: norm_c2(rstd, "rstd")))
            st.append(seq(lambda: norm_c2r(rstd, "rstd"), lambda: norm_c3(xt, xk, u3, "u3", rstd, "rstd", 2, ntm3, NTK3, (0, 1, 2, 3))))
            st.append(lambda: norm_c3(xt, xk, u3, "u3", rstd, "rstd", 2, ntm3, NTK3, (4, 5, 6, 7)))
            return st

        def p3_loadx(tix):
            ti = NT3 - 1 - tix
            dma(xts[tix % 2], X1v[:, :, ti * 512:(ti + 1) * 512], (), [("x", tix % 2)], ("xl3", tix % 2))

        load_chunk(0)
        p3_loadx(0)
        for f in tile_pre_stages(0): f()
        slots = [("u", ("c", fc)) for fc in range(FC)] + [("d", -1)]
        for m in range(8): slots += [("d", ("h", m)), ("d", m)]
        for tix in range(NT3):
            ti = NT3 - 1 - tix
            xt, xk = p3_x(tix)
            hu = {}; hd = {}
            if tix + 1 < NT3:
                hu.setdefault(("c", 9), []).append(lambda t=tix + 1: p3_loadx(t))
                stg = tile_pre_stages(tix + 1)
                assert len(stg) <= len(slots), (len(stg), len(slots))
                for f, (kind, k) in zip(stg, slots):
                    (hu if kind == "u" else hd).setdefault(k, []).append(f)
            ffn_up(1, u3, "u3", hT, rings, hu)
            ffn_down(1, xt, xk, hT, 2, rings, hd, banks=(0, 2))
            dma(outv[:, :, ti * 512:(ti + 1) * 512], xt, [xk], [("out", ti)], ("os", tix % 2))
        P.finish(final_streams=[("os", 0), ("os", 1)] if NT3 > 1 else [("os", 0)])
        stats = (len(P.ops), P.nwaits, P.nsem)
    return nc, stats


def _prep_core(b, rev, S, x, c, positions, w_ada, b_ada, norm_ffn1, norm_mix, norm_ffn2, conv_w, conv_b, dt_bias, a_log,
               d_skip, ssd_norm_w, q_norm_w, k_norm_w, sink_logit):
    f = np.float32
    small = np.zeros((128, NSMALL), f)
    pk = lambda v: np.ascontiguousarray(np.asarray(v, f).reshape(-1, 128).T)
    rep = lambda v: np.broadcast_to(np.asarray(v, f).reshape(1, -1), (128, np.asarray(v).size))
    small[:, O_C:O_C + 8] = pk(c[b])
    small[:, O_BADA:O_BADA + 72] = pk(b_ada[0])
    small[:, O_GAIN:O_GAIN + 8] = pk(norm_ffn1[0]); small[:, O_GAIN + 8:O_GAIN + 16] = pk(norm_mix[0])
    small[:, O_GAIN + 16:O_GAIN + 24] = pk(norm_ffn2[0])
    cw = np.asarray(conv_w[0], f)
    if rev: cw = cw[::-1]
    small[:, O_CW:O_CW + 40] = cw.reshape(5, 8, 128).transpose(2, 1, 0).reshape(128, 40)
    small[:, O_CB:O_CB + 8] = pk(conv_b[0])
    dtb = np.asarray(dt_bias[0], f); alg = np.asarray(a_log[0], f)
    if rev: dtb = dtb[::-1]; alg = alg[::-1]
    small[:, O_DTB:O_DTB + 16] = rep(dtb.reshape(-1))
    small[:, O_ALOG:O_ALOG + 16] = rep(alg.reshape(-1))
    small[:, O_DSK:O_DSK + 8] = rep(d_skip[0])
    small[:, O_SINK:O_SINK + 8] = rep(sink_logit[0])
    small[:, O_KW:O_KW + 128] = rep(np.tile(np.asarray(k_norm_w[0], f), 2))
    small[:, O_QW:O_QW + 512] = rep(np.tile(np.asarray(q_norm_w[0], f), 8))
    small[:, O_SNW:O_SNW + 512] = rep(ssd_norm_w[0])
    small[:, O_FLAG] = 0.0 if rev else 1.0
    small[:, O_FLAG + 1] = 1.0 if rev else 0.0
    p = np.asarray(positions[b][:S], np.int32); xb = np.asarray(x[b][:S], f)
    if rev: p = p[::-1]; xb = xb[::-1]
    pos = np.ascontiguousarray(p.reshape(-1, 128).T)
    xT = np.ascontiguousarray(xb.T)
    return {"xT": xT, "small": small, "pos": pos}


_CACHE = {}


def run(inputs, S, debug=False, n_cores=8):
    g = {k: np.asarray(v) for k, v in inputs.items()}
    key = (S, debug)
    if key not in _CACHE:
        _CACHE[key] = build(S, debug)
    nc, stats = _CACHE[key]
    wi = np.asarray(g["w_in"][0], np.float32)
    perm = np.concatenate([np.arange(0, 512), np.arange(512, 1536), np.arange(1552, 2064), np.arange(2064, 2192),
                           np.arange(2192, 2320), np.arange(1536, 1552)])
    perm_r = np.concatenate([perm[:2304], np.arange(1544, 1552), np.arange(1536, 1544)])
    ca = lambda a: np.ascontiguousarray(a, dtype=np.float32)
    shared = {
        "w_ada": ca(g["w_ada"][0]), "wg1": ca(g["ffn1_wg"][0]), "wu1": ca(g["ffn1_wu"][0]), "wd1": ca(g["ffn1_wd"][0]),
        "wg2": ca(g["ffn2_wg"][0]), "wu2": ca(g["ffn2_wu"][0]), "wd2": ca(g["ffn2_wd"][0]), "w_out": ca(g["w_out"][0]),
    }
    w_in_n = np.ascontiguousarray(wi[:, perm]); w_in_r = np.ascontiguousarray(wi[:, perm_r])
    in_maps = []
    for core in range(n_cores):
        b = (core // 2) % 4; rev = bool(core % 2)
        m = _prep_core(b, rev, S, g["x"], g["c"], g["positions"], g["w_ada"], g["b_ada"], g["norm_ffn1"], g["norm_mix"],
                       g["norm_ffn2"], g["conv_w"], g["conv_b"], g["dt_bias"], g["a_log"], g["d_skip"], g["ssd_norm_w"],
                       g["q_norm_w"], g["k_norm_w"], g["sink_logit"])
        m.update(shared)
        m["w_in"] = w_in_r if rev else w_in_n
        in_maps.append(m)
    res = run_bass_kernel_spmd(nc, in_maps, core_ids=list(range(n_cores)))
    return res, stats


def assemble(res, S, nb=4):
    out = np.empty((nb, S, D), np.float32)
    for b in range(nb):
        out[b, :S // 2] = res.results[2 * b]["outT"].T
        out[b, S // 2:] = res.results[2 * b + 1]["outT"].T[::-1]
    return out


def kernel(**inputs):
    S = 8192
    res, _ = run(inputs, S, debug=False, n_cores=8)
    return assemble(res, S, 4)
```

```python
import math
import numpy as np
from contextlib import ExitStack
import concourse.bass as bass
import concourse.mybir as mybir
from concourse.bass_utils import run_bass_kernel_spmd

F32 = mybir.dt.float32; BF16 = mybir.dt.bfloat16; I32 = mybir.dt.int32
AF = mybir.ActivationFunctionType; ALU = mybir.AluOpType; AX = mybir.AxisListType

D = 1024; KC = 8; FF = 2816; FC = 22; INW = 2320
EPS = 1e-6
NSMALL = 1354
O_C = 0; O_BADA = 8; O_GAIN = 80; O_CW = 104; O_CB = 144; O_DTB = 152; O_ALOG = 168; O_DSK = 184
O_SINK = 192; O_KW = 200; O_QW = 328; O_SNW = 840; O_FLAG = 1352


class Op:
    __slots__ = ("eng", "fn", "idx", "dma", "sem", "count", "deps", "signal", "waits")

    def __init__(self, eng, fn, idx, dma):
        self.eng = eng; self.fn = fn; self.idx = idx; self.dma = dma
        self.sem = None; self.count = 0; self.deps = (); self.signal = False; self.waits = []


class Prog:
    ENGS = ("pe", "act", "dve", "pool", "sp")

    def __init__(self, nc, es):
        self.nc = nc; self.es = es
        self.ops = []; self.state = {}; self.streams = {}; self.esem = {}
        self.nsem = 0; self.bar_op = None; self.since_bar = []

    def newsem(self, name):
        self.nsem += 1
        return self.es.enter_context(self.nc.semaphore(f"{name}_{self.nsem}"))

    def op(self, eng, fn, reads=(), writes=(), stream=None):
        o = Op(eng, fn, len(self.ops), stream is not None)
        deps = {}
        st = self.state
        for k in reads:
            s = st.get(k)
            if s is None: s = st[k] = [None, []]
            if s[0] is not None: deps[s[0].idx] = s[0]
        for k in writes:
            s = st.get(k)
            if s is None: s = st[k] = [None, []]
            if s[0] is not None: deps[s[0].idx] = s[0]
            for r in s[1]: deps[r.idx] = r
        for k in reads: st[k][1].append(o)
        for k in writes: st[k] = [o, []]
        if self.bar_op is not None: deps[self.bar_op.idx] = self.bar_op
        deps.pop(o.idx, None)
        o.deps = list(deps.values())
        if stream is not None:
            s = self.streams.get(stream)
            if s is None: s = self.streams[stream] = [self.newsem("d"), 0]
            s[1] += 16
            o.sem = s[0]; o.count = s[1]
        self.ops.append(o); self.since_bar.append(o)
        return o

    def barrier(self, fn):
        o = Op("dve", fn, len(self.ops), False)
        last = {}
        for p in self.since_bar:
            if p.dma: last[("d", id(p.sem), p.count)] = p
            else: last[p.eng] = p
        if self.bar_op is not None: last["bar"] = self.bar_op
        o.deps = list(last.values())
        self.ops.append(o)
        self.bar_op = o; self.since_bar = []; self.state = {}
        return o

    def finish(self, final_streams=()):
        nc = self.nc
        for o in self.ops:
            for d in o.deps:
                if not d.dma:
                    if d.eng == "pe" and o.eng == "pe" and not o.dma: continue
                    d.signal = True
        cnt = {e: 0 for e in self.ENGS}
        for e in self.ENGS: self.esem[e] = self.newsem("e" + e)
        for o in self.ops:
            if not o.dma and o.signal:
                cnt[o.eng] += 1; o.count = cnt[o.eng]; o.sem = self.esem[o.eng]
        known = {e: {} for e in self.ENGS}
        nw = 0
        for o in self.ops:
            need = {}
            for d in o.deps:
                if not d.dma and d.eng == "pe" and o.eng == "pe" and not o.dma: continue
                key = id(d.sem)
                if need.get(key, (None, 0))[1] < d.count: need[key] = (d.sem, d.count)
            kn = known[o.eng]
            for key, (sem, c) in need.items():
                if kn.get(key, 0) >= c: continue
                kn[key] = c; o.waits.append((sem, c)); nw += 1
        self.nwaits = nw
        byeng = {e: [o for o in self.ops if o.eng == e] for e in self.ENGS}
        finals = [tuple(self.streams[s]) for s in final_streams]

        def run(e, lst, fin=False):
            for o in lst:
                for (sem, c) in o.waits: e.wait_ge(sem, c)
                ins = o.fn(e)
                if o.dma: ins.then_inc(o.sem, 16)
                elif o.signal: ins.then_inc(o.sem, 1)
            if fin:
                for (sem, c) in finals: e.wait_ge(sem, c)

        with nc.Block() as block:
            @block.tensor
            def _(e): run(e, byeng["pe"])

            @block.scalar
            def _(e): run(e, byeng["act"])

            @block.vector
            def _(e): run(e, byeng["dve"])

            @block.gpsimd
            def _(e): run(e, byeng["pool"])

            @block.sync
            def _(e): run(e, byeng["sp"], True)


def bc(ap, axis, n):
    l = [list(x) for x in ap.ap]
    l.insert(axis, [0, n])
    return bass.AP(ap.tensor, ap.offset, l)


class Arena:
    def __init__(self, nc, es, name, nbytes):
        self.t = es.enter_context(nc.sbuf_tensor(name, [128, nbytes // 4], F32))
        self.cap = nbytes; self.off = 0

    def reset(self): self.off = 0

    def alloc(self, shape, dt):
        esz = 2 if dt == BF16 else 4
        n = int(np.prod(shape)); nb = (n * esz + 31) // 32 * 32
        assert self.off + nb <= self.cap, (self.off, nb, self.cap)
        w0 = self.off // 4; self.off += nb
        ap = self.t[:, w0:w0 + nb // 4]
        if dt != F32: ap = ap.bitcast(dt)
        ap = ap[:, 0:n]
        if len(shape) == 2: ap = ap.rearrange("p (a b) -> p a b", b=shape[1])
        elif len(shape) == 3: ap = ap.rearrange("p (a b c) -> p a b c", b=shape[1], c=shape[2])
        return ap


def build(S, debug=False):
    NCH = S // 128
    NT1 = S // 512
    NT2 = S // 256
    NH = NCH // 2
    SH = S // 2
    nc = bass.Bass("TRN2", target_bir_lowering=False)
    ext_in = lambda n, sh, dt=F32: nc.dram_tensor(n, sh, dt, kind="ExternalInput").ap()
    dbgk = "ExternalOutput" if debug else "Internal"
    scr = lambda n, sh, dt: nc.dram_tensor(n, sh, dt, kind=dbgk).ap()
    xT = ext_in("xT", [D, S]); small = ext_in("small", [128, NSMALL]); pos_in = ext_in("pos", [128, NCH], I32)
    w_ada = ext_in("w_ada", [D, 9 * D])
    wg_in = [ext_in("wg1", [D, FF]), ext_in("wg2", [D, FF])]
    wu_in = [ext_in("wu1", [D, FF]), ext_in("wu2", [D, FF])]
    wd_in = [ext_in("wd1", [FF, D]), ext_in("wd2", [FF, D])]
    w_in_d = ext_in("w_in", [D, INW]); w_out_d = ext_in("w_out", [D, D])
    outT = nc.dram_tensor("outT", [D, SH], F32, kind="ExternalOutput").ap()
    wgb = [nc.dram_tensor(f"wgb{i}", [11, 128, KC * 256], BF16, kind="Internal").ap() for i in range(2)]
    wub = [nc.dram_tensor(f"wub{i}", [11, 128, KC * 256], BF16, kind="Internal").ap() for i in range(2)]
    wdb = [nc.dram_tensor(f"wdb{i}", [8, 128, FC * 128], BF16, kind="Internal").ap() for i in range(2)]
    winb = nc.dram_tensor("winb", [128, KC * INW], BF16, kind="Internal").ap()
    X1 = scr("X1", [D, SH], F32); U2 = scr("U2", [D, S], BF16)
    SZ = scr("SZ", [NH, 128, 512], BF16); YP = scr("YP", [NH, 128, 512], F32)
    SBS = scr("SBS", [NH, 128, 512], F32); EBD = scr("EBD", [NH, 128, 16], F32)
    CTS = scr("CTS", [NH, 128, 256], BF16); QTS = scr("QTS", [NH, 64, 1024], BF16)
    KTS = scr("KTS", [NH + 2, 64, 256], BF16); VES = scr("VES", [NH + 2, 128, 130], BF16)

    es = ExitStack()
    with es:
        P = Prog(nc, es)
        CA = Arena(nc, es, "carena", 33 * 1024)
        WA = Arena(nc, es, "warena", 164 * 1024)
        psf = [es.enter_context(nc.psum_tensor(f"ps{i}", [128, 512], F32)) for i in range(8)]
        ps = [t[:, :] for t in psf]
        psb = [t[:, :].bitcast(BF16) for t in psf]
        PSK = lambda b: ("ps", b)
        RSQ = AF.Abs_reciprocal_sqrt
        STORE_ENG = ["pool"]
        STORE_NAMES = {"X1", "U2", "SZ", "YP", "SBS", "EBD", "CTS", "QTS", "KTS", "VES", "outT"}

        def mm(out, lhsT, rhs, start, stop, r, w):
            P.op("pe", lambda e: e.matmul(out, lhsT, rhs, start=start, stop=stop), r, w)

        def tr(out, in_, ident_, r, w):
            P.op("pe", lambda e: e.transpose(out, in_, ident_), r, w)

        def act(out, in_, func, r, w, bias=None, scale=None, accum=None):
            kw = {}
            if bias is not None: kw["bias"] = bias
            if scale is not None: kw["scale"] = scale
            if accum is not None: kw["accum_out"] = accum
            P.op("act", lambda e: e.activation(out=out, in_=in_, func=func, **kw), r, w)

        def tt(eng, out, in0, in1, op, r, w):
            P.op(eng, lambda e: e.tensor_tensor(out=out, in0=in0, in1=in1, op=op), r, w)

        def ts(eng, out, in0, s1, s2, op0, op1, r, w):
            if s2 is None:
                P.op(eng, lambda e: e.tensor_scalar(out=out, in0=in0, scalar1=s1, scalar2=None, op0=op0), r, w)
            else:
                P.op(eng, lambda e: e.tensor_scalar(out=out, in0=in0, scalar1=s1, scalar2=s2, op0=op0, op1=op1), r, w)

        def stt(eng, out, in0, scalar, in1, op0, op1, r, w):
            P.op(eng, lambda e: e.scalar_tensor_tensor(out=out, in0=in0, scalar=scalar, in1=in1, op0=op0, op1=op1), r, w)

        def cp(eng, out, in_, r, w):
            if eng == "act":
                P.op("act", lambda e: e.activation(out=out, in_=in_, func=AF.Copy), r, w)
            else:
                P.op(eng, lambda e: e.tensor_copy(out=out, in_=in_), r, w)

        def red(out, in_, r, w):
            P.op("dve", lambda e: e.tensor_reduce(out=out, in_=in_, axis=AX.X, op=ALU.add), r, w)

        def recip(out, in_, r, w):
            P.op("dve", lambda e: e.reciprocal(out=out, in_=in_), r, w)

        def memset(eng, ap, val, w):
            P.op(eng, lambda e: e.memset(ap, val), (), w)

        def dma(out, in_, r, w, stream, eng="sp"):
            if eng == "sp" and getattr(out.tensor, "name", "") in STORE_NAMES: eng = STORE_ENG[0]
            P.op(eng, lambda e: e.dma_start(out=out, in_=in_), r, w, stream=stream)

        sm = CA.alloc([NSMALL], F32)
        dma(sm, small, (), ["sm"], "sm")
        posi = CA.alloc([NCH], I32)
        dma(posi, pos_in, (), ["posi"], "posi")
        ones_bf = CA.alloc([128], BF16); ident_bf = CA.alloc([128], BF16)
        LEb = CA.alloc([128], BF16); GEb = CA.alloc([128], BF16); GTb = CA.alloc([128], BF16); LTb = CA.alloc([128], BF16)
        LEf = CA.alloc([128], F32); GEf = CA.alloc([128], F32); GTf = CA.alloc([128], F32); onesf = CA.alloc([128], F32)
        NEGp = CA.alloc([512], BF16); NEGn = CA.alloc([512], BF16)
        dif = WA.alloc([128], F32); tmpm = WA.alloc([128], F32)
        P.op("pool", lambda e: e.iota(dif, pattern=[[1, 128]], base=0, channel_multiplier=-1,
                                      allow_small_or_imprecise_dtypes=True), (), ["dif"])
        memset("dve", onesf, 1.0, ["onesf"])
        cp("dve", ones_bf, onesf, ["onesf"], ["ones_bf"])
        for (mf, mb, op_) in ((LEf, LEb, ALU.is_ge), (GEf, GEb, ALU.is_le), (GTf, GTb, ALU.is_lt), (tmpm, LTb, ALU.is_gt)):
            ts("dve", mf, dif, 0.0, None, op_, None, ["dif"], [("m", id(mf))])
            cp("dve", mb, mf, [("m", id(mf))], [("mb", id(mb))])
        identf = CA.alloc([128], F32)
        ts("dve", identf, dif, 0.0, None, ALU.is_equal, None, ["dif"], ["identf"])
        cp("dve", ident_bf, identf, ["identf"], ["ident"])
        ts("dve", tmpm, dif, 0.0, -30000.0, ALU.is_gt, ALU.mult, ["dif"], [("m", id(tmpm))])
        cp("dve", NEGp.rearrange("p (a b) -> p a b", b=128), bc(tmpm, 1, 4), [("m", id(tmpm))], ["NEGp"])
        ts("dve", tmpm, dif, 0.0, -30000.0, ALU.is_lt, ALU.mult, ["dif"], [("m", id(tmpm))])
        cp("dve", NEGn.rearrange("p (a b) -> p a b", b=128), bc(tmpm, 1, 4), [("m", id(tmpm))], ["NEGn"])
        CONSTS = ["ones_bf", "ident", "NEGp", "NEGn", "onesf"] + [("mb", id(x)) for x in (LEb, GEb, GTb, LTb)] + \
                 [("m", id(x)) for x in (LEf, GEf, GTf)]
        cosT = CA.alloc([NCH, 8], F32); sinT = CA.alloc([NCH, 8], F32)
        posf = WA.alloc([NCH], F32); invf = WA.alloc([8], F32)
        ang = WA.alloc([NCH, 8], F32); kf = WA.alloc([NCH, 8], F32); ki = WA.alloc([NCH, 8], I32)
        cp("dve", posf, posi, ["posi"], ["posf"])
        for i in range(8):
            memset("dve", invf[:, i:i + 1], float(500000.0 ** (-(i * 2.0) / 16.0)), ["invf"])
        tt("dve", ang, bc(posf, 2, 8), bc(invf, 1, NCH), ALU.mult, ["posf", "invf"], ["ang"])
        for (tab, shift) in ((sinT, 0.0), (cosT, math.pi / 2)):
            if shift != 0.0:
                ts("dve", ang, ang, shift, None, ALU.add, None, ["ang"], ["ang"])
            ts("dve", ki, ang, 1.0 / (2 * math.pi), None, ALU.mult, None, ["ang"], ["ki"])
            cp("dve", kf, ki, ["ki"], ["kf"])
            stt("dve", kf, kf, -2 * math.pi, ang, ALU.mult, ALU.add, ["kf", "ang"], ["kf"])
            ts("dve", kf, kf, 3.1415925, -3.1415925, ALU.min, ALU.max, ["kf"], ["kf"])
            act(tab, kf, AF.Sin, ["kf"], [("tab", id(tab))])
        a_neg = CA.alloc([16], F32); esink = CA.alloc([8], F32); qw8 = CA.alloc([512], F32)
        act(a_neg, sm[:, O_ALOG:O_ALOG + 16], AF.Exp, ["sm"], ["a_neg"])
        ts("dve", a_neg, a_neg, -1.0, None, ALU.mult, None, ["a_neg"], ["a_neg"])
        act(esink, sm[:, O_SINK:O_SINK + 8], AF.Exp, ["sm"], ["esink"])
        ts("dve", qw8, sm[:, O_QW:O_QW + 512], 0.125, None, ALU.mult, None, ["sm"], ["qw8"])
        LTf = CA.alloc([128], F32)
        ts("dve", LTf, dif, 0.0, None, ALU.is_gt, None, ["dif"], ["LTf0"])
        MKF = CA.alloc([128], F32); MKB = CA.alloc([128], F32)
        stt("dve", MKF, identf, sm[:, O_FLAG:O_FLAG + 1], LTf, ALU.mult, ALU.add, ["identf", "sm", "LTf0"], ["MKF"])
        stt("dve", MKB, identf, sm[:, O_FLAG + 1:O_FLAG + 2], GTf, ALU.mult, ALU.add, ["identf", "sm", ("m", id(GTf))], ["MKB"])
        hb = CA.alloc([512], F32); hbb = CA.alloc([512], BF16)
        DI = CA.alloc([8, 128], BF16)
        for h in range(8):
            ts("dve", DI[:, h, :], identf, sm[:, O_DSK + h:O_DSK + h + 1], None, ALU.mult, None, ["sm", "identf"], ["DI"])
        cvs = CA.alloc([8], F32)
        act(cvs, sm[:, O_C:O_C + 8], AF.Silu, ["sm"], ["cvs"])
        modv = CA.alloc([72], F32); Am = CA.alloc([24], F32); Gm = CA.alloc([24], F32)

        def cast_list(i):
            lst = []
            svg = wg_in[i].rearrange("(kc p) (s f) -> s p kc f", p=128, f=256)
            svu = wu_in[i].rearrange("(kc p) (s f) -> s p kc f", p=128, f=256)
            for s_ in range(11):
                kg = ("wgb", i, s_) if i == 0 else ("wgb", i); ku = ("wub", i, s_) if i == 0 else ("wub", i)
                lst.append(lambda s_=s_, kg=kg: dma(wgb[i][s_].rearrange("p (kc f) -> p kc f", f=256), svg[s_], (), [kg], ("cast",) + kg, eng="pool"))
                lst.append(lambda s_=s_, ku=ku: dma(wub[i][s_].rearrange("p (kc f) -> p kc f", f=256), svu[s_], (), [ku], ("cast",) + ku, eng="pool"))
            svd = wd_in[i].rearrange("(fc p) (m c) -> m p fc c", p=128, c=128)
            for m in range(8):
                lst.append(lambda m=m: dma(wdb[i][m].rearrange("p (fc c) -> p fc c", c=128), svd[m], (), [("wdb", i)], ("cast", "wdb", i), eng="pool"))
            return lst

        def cast_ffn(i):
            for f in cast_list(i): f()
        casts1 = cast_list(0)

        wav = w_ada.rearrange("(kc p) n -> p kc n", p=128)
        wab = [WA.alloc([KC, 1024], F32) for _ in range(2)]
        wbb = [WA.alloc([KC, 1024], BF16) for _ in range(2)]
        cvsb = CA.alloc([8], BF16)
        cp("dve", cvsb, cvs, ["cvs"], ["cvsb"])
        for blk in range(9):
            buf = wab[blk % 2]; bk = ("wab", blk % 2)
            bb = wbb[blk % 2]; bbk = ("wbb", blk % 2)
            dma(buf, wav[:, :, blk * 1024:(blk + 1) * 1024], (), [bk], bk)
            cp("dve", bb[:, 0:4, :], buf[:, 0:4, :], [bk], [(bbk, 0)])
            cp("act", bb[:, 4:8, :], buf[:, 4:8, :], [bk], [(bbk, 1)])
            for j in range(8):
                for kc in range(KC):
                    mm(ps[0][:, blk * 8 + j: blk * 8 + j + 1], bb[:, kc, j * 128:(j + 1) * 128], cvsb[:, kc:kc + 1],
                       kc == 0, kc == KC - 1, [(bbk, kc // 4), "cvsb"], [PSK(0)])
        tt("dve", modv, ps[0][:, 0:72], sm[:, O_BADA:O_BADA + 72], ALU.add, [PSK(0), "sm"], ["modv"])
        for i in range(3):
            stt("dve", Am[:, i * 8:(i + 1) * 8], modv[:, (3 * i + 1) * 8:(3 * i + 2) * 8], 1.0,
                sm[:, O_GAIN + i * 8:O_GAIN + (i + 1) * 8], ALU.add, ALU.mult, ["modv", "sm"], ["Am"])
            ts("dve", Gm[:, i * 8:(i + 1) * 8], modv[:, (3 * i + 2) * 8:(3 * i + 3) * 8], 1.0, (1.0 if i == 1 else 0.5),
               ALU.add, ALU.mult, ["modv"], ["Gm"])
        Bm = lambda i, kc: modv[:, (3 * i) * 8 + kc:(3 * i) * 8 + kc + 1]
        scratch1 = CA.alloc([8], F32)
        dtraw = CA.alloc([NCH, 16], F32)
        wdt16 = CA.alloc([KC, 16], BF16)
        dma(wdt16, w_in_d.rearrange("(kc p) n -> p kc n", p=128)[:, :, 2304:2320], (), ["wdt16"], "wdt16", eng="pool")

        def barrier():
            P.barrier(lambda e: e.memset(scratch1, 0.0))
            WA.reset()

        def norm_a(xt, xkey, sq, sqk):
            act(sq, xt, AF.Square, [xkey], [sqk])

        def norm_b(sq, sqk, bank, T=512):
            for kc in range(KC):
                mm(ps[bank][:, 0:T], ones_bf, sq[:, kc, :], kc == 0, kc == KC - 1, [sqk], [PSK(bank)])

        def norm_c1(rstd, rk, bank, T=512):
            ts("dve", rstd, ps[bank][:, 0:T], 1.0 / D, EPS, ALU.mult, ALU.add, [PSK(bank)], [rk])

        def norm_c2(rstd, rk):
            act(rstd, rstd, AF.Sqrt, [rk], [rk])

        def norm_c2r(rstd, rk):
            recip(rstd, rstd, [rk], [rk])

        def norm_c3(xt, xkey, u, ukey, rstd, rk, i, tmps, tks, kcs):
            for kc in kcs:
                tb = tmps[kc % 2]; tk = tks[kc % 2]
                tt("dve", tb, xt[:, kc, :], rstd, ALU.mult, [xkey, rk], [tk])
                ts("dve", u[:, kc, :], tb, Am[:, i * 8 + kc:i * 8 + kc + 1], Bm(i, kc), ALU.mult, ALU.add, [tk], [ukey])

        def norm_c(xt, xkey, u, ukey, rstd, rk, i, tmps, tks, bank, T=512):
            norm_c1(rstd, rk, bank); norm_c2(rstd, rk); norm_c2r(rstd, rk)
            norm_c3(xt, xkey, u, ukey, rstd, rk, i, tmps, tks, range(KC))

        def norm_tile(xt, xkey, u, ukey, sq, rstd, i, tmps, T=512):
            norm_a(xt, xkey, sq, "sq")
            norm_b(sq, "sq", 6)
            norm_c(xt, xkey, u, ukey, rstd, "rstd", i, tmps, (("sg", 0), ("sg", 1)), 6)

        def run_hooks(hooks, key):
            if hooks and key in hooks:
                for f in hooks[key]: f()

        def ffn_up(fi, u, ukey, hT, rings, hooks=None, T=512):
            wgu, wdr, sgb = rings
            for s in range(11):
                slot = s % len(wgu)
                gs, us = wgu[slot]
                gk = ("wg", slot); uk_ = ("wu", slot)
                dma(gs, wgb[fi][s].rearrange("p (kc f) -> p kc f", f=256), [("wgb", fi, s) if fi == 0 else ("wgb", fi)], [gk], ("wg", slot))
                dma(us, wub[fi][s].rearrange("p (kc f) -> p kc f", f=256), [("wub", fi, s) if fi == 0 else ("wub", fi)], [uk_], ("wu", slot))
                for j in range(2):
                    fc = 2 * s + j
                    bG = (fc % 2) * 2; bU = bG + 1
                    for kc in range(KC):
                        mm(ps[bG][:, 0:T], gs[:, kc, j * 128:(j + 1) * 128], u[:, kc, :], kc == 0, kc == KC - 1, [gk, ukey], [PSK(bG)])
                    for kc in range(KC):
                        mm(ps[bU][:, 0:T], us[:, kc, j * 128:(j + 1) * 128], u[:, kc, :], kc == 0, kc == KC - 1, [uk_, ukey], [PSK(bU)])
                    sg = sgb[fc % 2]; sk = ("sg", fc % 2)
                    act(sg, ps[bG][:, 0:T], AF.Silu, [PSK(bG)], [sk])
                    tt("dve", hT[:, fc, :], ps[bU][:, 0:T], sg, ALU.mult, [PSK(bU), sk], [("h", fc)])
                    run_hooks(hooks, ("c", fc))
                run_hooks(hooks, s)

        def ffn_down(fi, xt, xkey, hT, gi, rings, hooks=None, T=512, banks=(4, 5)):
            wgu, wdr, sgb = rings
            run_hooks(hooks, -1)
            for m in range(8):
                slot = m % len(wdr)
                ws = wdr[slot]; wk = ("wd", slot)
                dma(ws, wdb[fi][m].rearrange("p (fc c) -> p fc c", c=128), [("wdb", fi)], [wk], wk)
                b = banks[m % 2]
                for fc in range(FC):
                    mm(ps[b][:, 0:T], ws[:, fc, :], hT[:, fc, :], fc == 0, fc == FC - 1, [wk, ("h", fc)], [PSK(b)])
                    if fc == 10: run_hooks(hooks, ("h", m))
                stt("dve", xt[:, m, :], ps[b][:, 0:T], Gm[:, gi * 8 + m:gi * 8 + m + 1], xt[:, m, :], ALU.mult, ALU.add,
                    [PSK(b), xkey], [xkey])
                run_hooks(hooks, m)

        def ffn_tile(fi, xt, xkey, u, ukey, hT, gi, rings, T=512):
            ffn_up(fi, u, ukey, hT, rings)
            ffn_down(fi, xt, xkey, hT, gi, rings)

        barrier()
        for f in casts1: f()
        casts2 = cast_list(1)
        dma(winb.rearrange("p (kc n) -> p kc n", n=INW), w_in_d.rearrange("(kc p) n -> p kc n", p=128), (), ["winb"], "winb", eng="pool")
        xts = [WA.alloc([KC, 512], F32) for _ in range(2)]
        sqA = WA.alloc([KC, 512], BF16); sqB = WA.alloc([KC, 512], BF16)
        u1 = [WA.alloc([KC, 512], BF16) for _ in range(2)]; u2 = WA.alloc([KC, 512], BF16)
        hT = WA.alloc([FC, 512], BF16)
        rstdA = WA.alloc([512], F32); rstdB = WA.alloc([512], F32)
        sgb = [WA.alloc([512], F32) for _ in range(2)]
        ntm = [WA.alloc([512], F32) for _ in range(2)]
        wgu = [(WA.alloc([KC, 256], BF16), WA.alloc([KC, 256], BF16)) for _ in range(3)]
        wdr = [WA.alloc([FC, 128], BF16) for _ in range(3)]
        rings = (wgu, wdr, sgb)
        xTv = xT.rearrange("(kc p) t -> p kc t", p=128)
        X1v = X1.rearrange("(kc p) t -> p kc t", p=128)
        U2v = U2.rearrange("(kc p) t -> p kc t", p=128)
        outv = outT.rearrange("(kc p) t -> p kc t", p=128)
        NTK = (("ntm", 0), ("ntm", 1))

        def p1_x(ti): return xts[ti % 2], ("x", ti % 2)

        def p1_load(ti):
            dma(xts[ti % 2], xTv[:, :, ti * 512:(ti + 1) * 512], (), [("x", ti % 2)], ("xl", ti % 2))

        def n1a(ti): norm_a(*p1_x(ti), sqA, "sqA")
        def n1b(ti): norm_b(sqA, "sqA", 6)
        def n1c1(ti): norm_c1(rstdA, "rstdA", 6)
        def n1c2(ti): norm_c2(rstdA, "rstdA")
        def n1c2r(ti): norm_c2r(rstdA, "rstdA")
        def n1c3(ti, kcs):
            xt, xk = p1_x(ti)
            norm_c3(xt, xk, u1[ti % 2], ("u1", ti % 2), rstdA, "rstdA", 0, ntm, NTK, kcs)

        def n2a(ti):
            xt, xk = p1_x(ti)
            if ti < NT1 // 2:
                dma(X1v[:, :, ti * 512:(ti + 1) * 512], xt, [xk], [("X1", ti)], ("xs", ti % 2))
            norm_a(xt, xk, sqB, "sqB")
        def n2b(ti): norm_b(sqB, "sqB", 7)
        def n2c1(ti): norm_c1(rstdB, "rstdB", 7)
        def n2c2(ti): norm_c2(rstdB, "rstdB")
        def n2c2r(ti): norm_c2r(rstdB, "rstdB")
        def n2c3(ti, kcs):
            xt, xk = p1_x(ti)
            norm_c3(xt, xk, u2, "u2", rstdB, "rstdB", 1, ntm2, NTK2, kcs)
        def n2s(ti):
            dma(U2v[:, :, ti * 512:(ti + 1) * 512], u2, ["u2"], [("U2", ti)], "u2s")
        def n2d(ti):
            for ci in range(4):
                for kc in range(KC):
                    mm(ps[6][:, ci * 16:(ci + 1) * 16], u2[:, kc, ci * 128:(ci + 1) * 128], wdt16[:, kc, :], kc == 0, kc == KC - 1,
                       ["u2"], [PSK(6)])
            cp("act", dtraw[:, ti * 4:(ti + 1) * 4, :], ps[6][:, 0:64].rearrange("p (c h) -> p c h", h=16), [PSK(6)], ["dtraw"])
            if ti + 2 < NT1: p1_load(ti + 2)
            per = -(-len(casts2) // NT1)
            for f in casts2[ti * per:(ti + 1) * per]: f()

        ntm2 = [WA.alloc([512], F32) for _ in range(2)]
        NTK2 = (("ntm2", 0), ("ntm2", 1))
        p1_load(0)
        if NT1 > 1: p1_load(1)
        n1a(0); n1b(0); n1c1(0); n1c2(0); n1c2r(0); n1c3(0, range(KC))
        L = lambda f, *a: (lambda: f(*a))
        for ti in range(NT1):
            xt, xk = p1_x(ti)
            hu = {}
            if ti > 0:
                t = ti - 1
                hu = {0: [L(n2b, t)], 1: [L(n2c1, t)], 2: [L(n2c2, t)], 3: [L(n2c2r, t)], 4: [L(n2c3, t, (0, 1))], 5: [L(n2c3, t, (2, 3))],
                      6: [L(n2c3, t, (4, 5))], 7: [L(n2c3, t, (6, 7)), L(n2s, t)], 9: [L(n2d, t)]}
            ffn_up(0, u1[ti % 2], ("u1", ti % 2), hT, rings, hu)
            hd = {7: [L(n2a, ti)]}
            if ti + 1 < NT1:
                t = ti + 1
                hd[-1] = [L(n1a, t)]
                hd[0] = [L(n1b, t)]; hd[1] = [L(n1c1, t)]; hd[2] = [L(n1c2, t)]; hd[3] = [L(n1c2r, t)]
                hd[4] = [L(n1c3, t, (0, 1, 2))]; hd[5] = [L(n1c3, t, (3, 4, 5))]; hd[6] = [L(n1c3, t, (6, 7))]
            ffn_down(0, xt, xk, hT, 0, rings, hd)
        t = NT1 - 1
        n2b(t); n2c1(t); n2c2(t); n2c2r(t); n2c3(t, range(KC)); n2s(t); n2d(t)

        barrier()
        STORE_ENG[0] = "sp"
        win = WA.alloc([KC, INW], BF16)
        dma(win, winb.rearrange("p (kc n) -> p kc n", n=INW), (), ["win"], "win")
        dtv = WA.alloc([NCH, 16], F32); da = WA.alloc([NCH, 16], F32)
        T1 = WA.alloc([NCH, 16], F32); T2 = WA.alloc([NCH, 16], F32); T3 = WA.alloc([NCH, 16], F32)
        DEC = WA.alloc([NCH, 16], F32); WDT = WA.alloc([NCH, 16], F32); EBDt = WA.alloc([NCH, 16], F32)
        tt("dve", T1, dtraw, bc(sm[:, O_DTB:O_DTB + 16], 1, NCH), ALU.add, ["dtraw"], ["T1"])
        ts("dve", T2, T1, -1.0, None, ALU.mult, None, ["T1"], ["T2"])
        tt("dve", T2, T2, T1, ALU.min, ["T2", "T1"], ["T2"])
        act(T3, T2, AF.Exp, ["T2"], ["T3"])
        act(T3, T3, AF.Ln, ["T3"], ["T3"], bias=1.0)
        stt("dve", dtv, T1, 0.0, T3, ALU.max, ALU.add, ["T1", "T3"], ["dtv"])
        tt("dve", da, dtv, bc(a_neg, 1, NCH), ALU.mult, ["dtv"], ["da"])
        CW = 512 // 8
        for c0 in range(0, NCH, CW):
            c1 = min(NCH, c0 + CW); n = c1 - c0
            for (half, msk, bnk) in ((0, LEf, 0), (1, GEf, 1)):
                o3 = ps[bnk][:, 0:n * 8].rearrange("p (c h) -> p c h", h=8)
                mm(o3, msk, da[:, c0:c1, half * 8:(half + 1) * 8], True, True, ["da"], [PSK(bnk)])
                cp("act", T2[:, c0:c1, half * 8:(half + 1) * 8], o3, [PSK(bnk)], ["T2"])
        CW2 = 512 // 16
        for c0 in range(0, NCH, CW2):
            c1 = min(NCH, c0 + CW2); n = c1 - c0
            bnk = 2 + (c0 // CW2) % 2
            o3 = ps[bnk][:, 0:n * 16].rearrange("p (c h) -> p c h", h=16)
            mm(o3, onesf, da[:, c0:c1, :], True, True, ["da"], [PSK(bnk)])
            cp("act", T1[:, c0:c1, :], o3, [PSK(bnk)], ["T1"])
        act(T3, T2, AF.Exp, ["T2"], ["T3"])
        act(DEC, T1, AF.Exp, ["T1"], ["DEC"])
        tt("dve", WDT, T1, T2, ALU.subtract, ["T1", "T2"], ["WDT"])
        act(WDT, WDT, AF.Exp, ["WDT"], ["WDT"])
        tt("dve", WDT, WDT, dtv, ALU.mult, ["WDT", "dtv"], ["WDT"])
        cp("dve", EBDt[:, :, 0:8], T3[:, :, 8:16], ["T3"], ["EBDt"])
        cp("dve", EBDt[:, :, 8:16], DEC[:, :, 8:16], ["DEC"], ["EBDt"])
        for c0 in range(0, NH, 16):
            c1 = min(NH, c0 + 16)
            dma(EBD[c0:c1].rearrange("c p n -> p c n"), EBDt[:, c0:c1, :], ["EBDt"], [("EBD", c0)], "ebds")
        TAB = ["dtv", "da", "T3", "DEC", "WDT"]
        DG = WA.alloc([8, 5, 128], BF16)
        for cc in range(8):
            for k in range(5):
                ts("dve", DG[:, cc, k, :], identf, sm[:, O_CW + cc * 5 + k:O_CW + cc * 5 + k + 1], None, ALU.mult, None, [], ["DG"])
        u2w = [WA.alloc([KC, 260], BF16) for _ in range(2)]
        xpres = [WA.alloc([8, 260], BF16) for _ in range(2)]
        xcs = [WA.alloc([8, 256], BF16) for _ in range(2)]
        PB = 2
        szb = [WA.alloc([512], BF16) for _ in range(PB)]
        qf = [WA.alloc([512], F32) for _ in range(PB)]; qq = [WA.alloc([512], F32) for _ in range(PB)]
        qss = [WA.alloc([8], F32) for _ in range(PB)]; qb = [WA.alloc([512], BF16) for _ in range(PB)]
        kfb = [WA.alloc([128], F32) for _ in range(PB)]; kq = [WA.alloc([128], F32) for _ in range(PB)]
        kss = [WA.alloc([8], F32) for _ in range(PB)]; kb = [WA.alloc([128], BF16) for _ in range(PB)]
        ra = [WA.alloc([64], F32) for _ in range(PB)]; rb = [WA.alloc([64], F32) for _ in range(PB)]
        vext = [WA.alloc([2, 65], BF16) for _ in range(PB)]
        QTb = [WA.alloc([1024], BF16) for _ in range(PB)]; KTb = [WA.alloc([256], BF16) for _ in range(PB)]
        ypart = [WA.alloc([512], F32) for _ in range(PB)]; sbt = [WA.alloc([512], F32) for _ in range(PB)]
        xsB = WA.alloc([768], BF16)
        xdt_f = WA.alloc([512], BF16); xdt_b = WA.alloc([512], BF16); xw_f = WA.alloc([512], BF16); xw_b = WA.alloc([512], BF16)
        CBf = WA.alloc([2, 128], F32); CBb = WA.alloc([2, 128], F32)
        lT = [WA.alloc([4, 128], BF16) for _ in range(4)]
        exs = [WA.alloc([512], F32) for _ in range(4)]
        MT = [WA.alloc([4, 128], BF16) for _ in range(4)]
        hf = WA.alloc([512], F32); hfb = WA.alloc([512], BF16); htmp = WA.alloc([512], F32)
        ytmp = WA.alloc([512], F32)
        zt = WA.alloc([256], BF16)
        memset("dve", hf, 0.0, ["hf"]); memset("dve", hfb, 0.0, ["hfb"])
        for pb in range(PB): memset("dve", vext[pb], 1.0, [("vext", pb)])
        memset("dve", zt, 0.0, ["zt"])
        dma(KTS[0], zt[0:64, 0:256], ["zt"], [("KTS", 0)], "zpad")
        dma(VES[0], zt[:, 0:130], ["zt"], [("VES", 0)], "zpad")

        def load_u2w(tj, slot):
            buf = u2w[slot % 2]; k = ("u2w", slot % 2)
            t0 = tj * 256
            lo = max(t0 - 2, 0); hi = min(t0 + 258, S)
            if lo != t0 - 2 or hi != t0 + 258:
                memset("pool", buf, 0.0, [k])
            dma(buf[:, :, lo - (t0 - 2):hi - (t0 - 2)], U2v[:, :, lo:hi], (), [k], k)

        def emit_zq(n_t_, ci_):
            uw_ = u2w[n_t_ % 2]; uk__ = ("u2w", n_t_ % 2)
            cols_ = slice(2 + ci_ * 128, 2 + ci_ * 128 + 128)
            for kc in range(KC):
                mm(ps[0][:, 0:512], uw_[:, kc, cols_], win[:, kc, 0:512], kc == 0, kc == KC - 1, ["win", uk__], [PSK(0)])
            for kc in range(KC):
                mm(ps[1][:, 0:512], uw_[:, kc, cols_], win[:, kc, 1536:2048], kc == 0, kc == KC - 1, ["win", uk__], [PSK(1)])

        memset("dve", hb, 0.0, ["hb"])
        tile_order = list(range(NT2 - 1, NT2 // 2 - 1, -1)) + list(range(0, NT2 // 2))
        own_next = {}
        _own = [(n_t_, tj_ * 2 + ci_, ci_) for n_t_, tj_ in enumerate(tile_order) if tj_ < NT2 // 2 for ci_ in (0, 1)]
        for a_, b_ in zip(_own[:-1], _own[1:]): own_next[a_[1]] = (b_[0], b_[2])
        xbc_done = set()
        zq_pending = [None]

        def xbc_steps(n_t_, far_, shared_):
            uw_ = u2w[n_t_ % 2]; uk_ = ("u2w", n_t_ % 2); par_ = n_t_ % 2
            xc_ = xcs[par_]; xp_ = xpres[par_]
            ncc_ = 6 if far_ else 8

            def xproj(cc):
                b = 6 + cc % 2
                for kc in range(KC):
                    mm(ps[b][:, 0:260], win[:, kc, 512 + cc * 128:512 + (cc + 1) * 128], uw_[:, kc, :], kc == 0, kc == KC - 1,
                       ["win", uk_], [PSK(b)])
                cp("dve" if far_ else "act", xp_[:, cc, :], ps[b][:, 0:260], [PSK(b)], [("xpre", par_, cc)])

            def xconv(cc):
                b = (6 if shared_ else 4) + cc % 2
                for k in range(5):
                    mm(ps[b][:, 0:256], DG[:, cc, k, :], xp_[:, cc, k:k + 256], k == 0, k == 4, ["DG", ("xpre", par_, cc)], [PSK(b)])
                act(xc_[:, cc, :], ps[b][:, 0:256], AF.Silu, [PSK(b), "sm"], [("xc", par_, cc)], bias=sm[:, O_CB + cc:O_CB + cc + 1])

            steps = []
            for cc in range(ncc_):
                def f(cc=cc):
                    if cc == 0: xproj(0)
                    if cc + 1 < ncc_: xproj(cc + 1)
                    xconv(cc)
                steps.append(f)
            return steps

        load_u2w(tile_order[0], 0)
        for n_t, tj in enumerate(tile_order):
            far = tj >= NT2 // 2
            uw = u2w[n_t % 2]; uk = ("u2w", n_t % 2)
            if n_t + 1 < len(tile_order): load_u2w(tile_order[n_t + 1], n_t + 1)
            par = n_t % 2
            xc = xcs[par]; xpre = xpres[par]
            if n_t not in xbc_done:
                for f in xbc_steps(n_t, far, False): f()
            XC = [("xc", par, i) for i in range(8)]
            xsteps = []
            if (not far) and n_t + 1 < len(tile_order):
                xsteps = xbc_steps(n_t + 1, False, True)
                xbc_done.add(n_t + 1)

            def xstep():
                if ovl and xsteps: xsteps.pop(0)()
            for ci in ((1, 0) if far else (0, 1)):
                c = tj * 2 + ci; pb = c % PB
                own = not far; halo = (c == NH)
                ovl = own and ci == 1 and len(xsteps) > 0
                bY, bO = (0, 1) if ovl else (6, 7)
                tc0 = ci * 128
                ucols = slice(2 + tc0, 2 + tc0 + 128)
                xcols = slice(tc0, tc0 + 128)
                dtc = dtv[:, c, :]; dac = da[:, c, :]; Ec = T3[:, c, :]; decc = DEC[:, c, :]; wc = WDT[:, c, :]
                branches = []
                if own: branches.append((qf[pb], ("qf", pb), qq[pb], qss[pb], 8, qb[pb], "q"))
                if own or halo: branches.append((kfb[pb], ("kfb", pb), kq[pb], kss[pb], 2, kb[pb], "k"))
                if own and c == 0:
                    emit_zq(n_t, 0)
                if own or halo:
                    for kc in range(KC):
                        mm(ps[2][:, 0:256], uw[:, kc, ucols], win[:, kc, 2048:2304], kc == 0, kc == KC - 1, ["win", uk], [PSK(2)])
                combos = ((0, 0, GTb, LEb, CBf), (0, 1, GTb, LEb, CBf), (1, 0, LTb, GEb, CBb), (1, 1, LTb, GEb, CBb))
                if own:
                    for mi, (dr, g, msk, rhsm, CBd) in enumerate(combos):
                        col = dr * 8 + g * 4
                        tt("dve", lT[mi], bc(msk, 1, 4), bc(dac[:, col:col + 4], 2, 128), ALU.mult, ["da"], [("lT", mi)])
                for i in range(6):
                    tr(psb[3][:, i * 128:(i + 1) * 128], xc[:, i, xcols], ident_bf, [("xc", par, i)], [PSK(3)])
                if own:
                    cp("act", qf[pb], ps[1][:, 0:512], [PSK(1)], [("qf", pb)])
                    act(szb[pb], ps[0][:, 0:512], AF.Silu, [PSK(0)], [("szb", pb)])
                    dma(SZ[c], szb[pb], [("szb", pb)], [("SZ", c)], ("szs", pb))
                    nxt = own_next.get(c)
                    if nxt is not None:
                        if ovl: zq_pending[0] = nxt
                        else: emit_zq(*nxt)
                if own or halo:
                    cp("act", kfb[pb], ps[2][:, 0:128], [PSK(2)], [("kfb", pb)])
                    cp("act", vext[pb][:, :, 0:64], ps[2][:, 128:256].rearrange("p (a b) -> p a b", b=64), [PSK(2)], [("vext", pb)])
                cp("act", xsB, psb[3][:, 0:768], [PSK(3)], ["xsB"])
                xs3 = xsB[:, 0:512].rearrange("p (h d) -> p h d", d=64)
                if own:
                    for g in range(2):
                        mm(ps[2][:, 256 + g * 128:256 + (g + 1) * 128], xc[:, 4 + g, xcols], xc[:, 6 + g, xcols], True, True, XC, [PSK(2)])
                    dma(CTS[c].rearrange("p (g l) -> p g l", l=128), xc[:, 6:8, xcols], XC, [("CTS", c)], "cts")
                    xstep()
                    for mi, (dr, g, msk, rhsm, CBd) in enumerate(combos):
                        bS = 4 + mi % 2
                        for r in range(4):
                            mm(ps[bS][:, r * 128:(r + 1) * 128], lT[mi][:, r, :], rhsm, True, True, [("lT", mi)], [PSK(bS)])
                        act(exs[mi], ps[bS][:, 0:512], AF.Exp, [PSK(bS)], [("exs", mi)])
                        xstep()
                        if mi == 1:
                            p2v = ps[2][:, 256:512].rearrange("p (g l) -> p g l", l=128)
                            tt("dve", CBf, p2v, bc(MKF, 1, 2), ALU.mult, [PSK(2)], ["CBf"])
                            tt("dve", CBb, p2v, bc(MKB, 1, 2), ALU.mult, [PSK(2)], ["CBb"])
                            for (dst, col_, nm) in ((xdt_f, dtc[:, 0:8], "xdt_f"), (xdt_b, dtc[:, 8:16], "xdt_b")):
                                tt("dve", dst.rearrange("p (h d) -> p h d", d=64), xs3, bc(col_, 2, 64), ALU.mult, ["xsB", "dtv"], [nm])
                    for mi, (dr, g, msk, rhsm, CBd) in enumerate(combos):
                        tt("dve", MT[mi], exs[mi].rearrange("p (r l) -> p r l", l=128), bc(CBd[:, g, :], 1, 4), ALU.mult,
                           [("exs", mi), "CBf", "CBb"], [("MT", mi)])
                for (src, sk, sqb, ssb, nh, outb, nm) in branches:
                    tt("dve", sqb[:, 0:nh * 64], src, src, ALU.mult, [sk], [(nm + "sq", pb)])
                    red(ssb[:, 0:nh], sqb[:, 0:nh * 64].rearrange("p (h d) -> p h d", d=64), [(nm + "sq", pb)], [(nm + "ss", pb)])
                    ts("dve", ssb[:, 0:nh], ssb[:, 0:nh], 1.0 / 64, EPS, ALU.mult, ALU.add, [(nm + "ss", pb)], [(nm + "ss", pb)])
                    act(ssb[:, 0:nh], ssb[:, 0:nh], AF.Ln, [(nm + "ss", pb)], [(nm + "ss", pb)])
                    act(ssb[:, 0:nh], ssb[:, 0:nh], AF.Exp, [(nm + "ss", pb)], [(nm + "ss", pb)], scale=-0.5)
                if own:
                    for h in range(8):
                        g = h // 4; r = h % 4
                        hs = slice(h * 64, (h + 1) * 64)
                        mm(ps[bY][:, hs], MT[g][:, r, :], xdt_f[:, hs], True, False, [("MT", g), "xdt_f"], [PSK(bY)])
                        mm(ps[bY][:, hs], MT[2 + g][:, r, :], xdt_b[:, hs], False, False, [("MT", 2 + g), "xdt_b"], [PSK(bY)])
                        mm(ps[bY][:, hs], DI[:, h, :], xsB[:, hs], False, True, ["xsB"], [PSK(bY)])
                    for g in range(2):
                        mm(ps[bO][:, g * 256:(g + 1) * 256], xc[:, 6 + g, xcols], hfb[:, g * 256:(g + 1) * 256], True, True,
                           XC + ["hfb"], [PSK(bO)])
                    xstep()
                    tt("dve", xw_f.rearrange("p (h d) -> p h d", d=64), xs3, bc(wc[:, 0:8], 2, 64), ALU.mult, ["xsB", "WDT"], ["xw_f"])
                tt("dve", xw_b.rearrange("p (h d) -> p h d", d=64), xs3, bc(wc[:, 8:16], 2, 64), ALU.mult, ["xsB", "WDT"], ["xw_b"])
                if own:
                    tt("dve", ytmp.rearrange("p (h d) -> p h d", d=64), ps[bO][:, 0:512].rearrange("p (h d) -> p h d", d=64),
                       bc(Ec[:, 0:8], 2, 64), ALU.mult, [PSK(bO), "T3"], ["ytmp"])
                    tt("dve", ypart[pb], ytmp, ps[bY][:, 0:512], ALU.add, ["ytmp", PSK(bY)], [("ypart", pb)])
                    if zq_pending[0] is not None:
                        emit_zq(*zq_pending[0]); zq_pending[0] = None
                    dma(YP[c], ypart[pb], [("ypart", pb)], [("YP", c)], ("yps", pb))
                for (bH, xw, xwn) in (((2, xw_f, "xw_f"), (3, xw_b, "xw_b")) if own else ((3, xw_b, "xw_b"),)):
                    for g in range(2):
                        mm(ps[bH][:, g * 256:(g + 1) * 256], xsB[:, 512 + g * 128:512 + (g + 1) * 128], xw[:, g * 256:(g + 1) * 256],
                           True, True, ["xsB", xwn] + ([("kfb", pb), ("vext", pb)] if bH == 2 else []), [PSK(bH)])
                if own:
                    tt("dve", htmp.rearrange("p (h d) -> p h d", d=64), hf.rearrange("p (h d) -> p h d", d=64), bc(decc[:, 0:8], 2, 64),
                       ALU.mult, ["hf", "DEC"], ["htmp"])
                    tt("dve", hf, htmp, ps[2][:, 0:512], ALU.add, ["htmp", PSK(2)], ["hf"])
                    cp("act", hfb, hf, ["hf"], ["hfb"])
                    cp("act", sbt[pb], ps[3][:, 0:512], [PSK(3)], [("sbt", pb)])
                    dma(SBS[c], sbt[pb], [("sbt", pb)], [("SBS", c)], ("sbss", pb))
                else:
                    tt("dve", htmp.rearrange("p (h d) -> p h d", d=64), hb.rearrange("p (h d) -> p h d", d=64), bc(decc[:, 8:16], 2, 64),
                       ALU.mult, ["hb", "DEC"], ["htmp"])
                    tt("dve", hb, htmp, ps[3][:, 0:512], ALU.add, ["htmp", PSK(3)], ["hb"])
                    if halo:
                        cp("act", hbb, hb, ["hb"], ["hbb"])
                if own:
                    xstep(); xstep()
                    while ovl and xsteps: xsteps.pop(0)()
                for (src, sk, sqb, ssb, nh, outb, nm) in branches:
                    s3 = src.rearrange("p (h d) -> p h d", d=64)
                    tt("dve", s3, s3, bc(ssb[:, 0:nh], 2, 64), ALU.mult, [sk, (nm + "ss", pb)], [sk])
                    if nm == "q":
                        tt("dve", src, src, qw8, ALU.mult, [sk], [sk])
                    else:
                        tt("dve", s3, s3, bc(sm[:, O_KW:O_KW + 64], 1, 2), ALU.mult, [sk], [sk])
                    cp("act", outb, src, [sk], [(nm + "b", pb)])
                    o3 = outb.rearrange("p (h d) -> p h d", d=64)
                    t1 = s3[:, :, 0:8]; t2 = s3[:, :, 8:16]
                    cb_ = bc(cosT[:, c, :], 1, nh); sb_ = bc(sinT[:, c, :], 1, nh)
                    ra3 = ra[pb][:, 0:nh * 8].rearrange("p (h d) -> p h d", d=8); rb3 = rb[pb][:, 0:nh * 8].rearrange("p (h d) -> p h d", d=8)
                    tt("dve", ra3, t1, cb_, ALU.mult, [sk], [("ra", pb)])
                    tt("dve", rb3, t2, sb_, ALU.mult, [sk], [("rb", pb)])
                    tt("dve", o3[:, :, 0:8], ra3, rb3, ALU.subtract, [("ra", pb), ("rb", pb)], [(nm + "b", pb)])
                    tt("dve", ra3, t2, cb_, ALU.mult, [sk], [("ra", pb)])
                    tt("dve", rb3, t1, sb_, ALU.mult, [sk], [("rb", pb)])
                    tt("dve", o3[:, :, 8:16], ra3, rb3, ALU.add, [("ra", pb), ("rb", pb)], [(nm + "b", pb)])
                if own:
                    for h in range(8):
                        tr(psb[4][0:64, h * 128:(h + 1) * 128], qb[pb][:, h * 64:(h + 1) * 64], ident_bf, [("qb", pb)], [PSK(4)])
                    cp("act", QTb[pb][0:64, :], psb[4][0:64, 0:1024], [PSK(4)], [("QTb", pb)])
                    dma(QTS[c], QTb[pb][0:64, :], [("QTb", pb)], [("QTS", c)], ("qts", pb))
                if own or halo:
                    for h in range(2):
                        tr(psb[5][0:64, h * 128:(h + 1) * 128], kb[pb][:, h * 64:(h + 1) * 64], ident_bf, [("kb", pb)], [PSK(5)])
                    cp("act", KTb[pb][0:64, :], psb[5][0:64, 0:256], [PSK(5)], [("KTb", pb)])
                    dma(KTS[c + 1], KTb[pb][0:64, :], [("KTb", pb)], [("KTS", c + 1)], ("kts", pb))
                    dma(VES[c + 1].rearrange("p (a b) -> p a b", b=65), vext[pb], [("vext", pb)], [("VES", c + 1)], ("ves", pb))

        barrier()
        STORE_ENG[0] = "pool"
        wout = WA.alloc([KC, D], BF16)
        dma(wout, w_out_d.rearrange("(kc p) n -> p kc n", p=128), (), ["wout"], "wout", eng="pool")
        xts = [WA.alloc([KC, 512], F32) for _ in range(2)]
        sq = WA.alloc([KC, 512], BF16)
        u3 = WA.alloc([KC, 512], BF16)
        hT = WA.alloc([FC, 512], BF16)
        rstd = WA.alloc([512], F32)
        sgb = [WA.alloc([512], F32) for _ in range(2)]
        ntm3 = sgb
        NTK3 = (("sg", 0), ("sg", 1))
        wgu = [(WA.alloc([KC, 256], BF16), WA.alloc([KC, 256], BF16)) for _ in range(2)]
        wdr = [WA.alloc([FC, 128], BF16) for _ in range(2)]
        rings = (wgu, wdr, sgb)
        ymT = WA.alloc([KC, 512], BF16)
        NB = 2
        ctl = [WA.alloc([2, 128], BF16) for _ in range(NB)]
        ebl = [WA.alloc([16], F32) for _ in range(NB)]
        sbl = [WA.alloc([512], F32) for _ in range(NB)]
        ypl = [WA.alloc([512], F32) for _ in range(NB)]
        szl = [WA.alloc([512], BF16) for _ in range(NB)]
        qtl = [WA.alloc([1024], BF16) for _ in range(NB)]
        ktl = [WA.alloc([3, 256], BF16) for _ in range(NB)]
        vel = [WA.alloc([3, 130], BF16) for _ in range(NB)]
        hb2 = WA.alloc([512], F32)
        yt1 = WA.alloc([512], F32); yg = WA.alloc([512], F32); junk = WA.alloc([512], BF16)
        gss = WA.alloc([4], F32)
        ymix = WA.alloc([1024], BF16)
        PT = [[WA.alloc([512], BF16) for _ in range(3)] for _ in range(2)]
        den = WA.alloc([8], F32)
        NT3 = NT1 // 2
        order = [c for ti in range(NT3 - 1, -1, -1) for c in range(ti * 4 + 3, ti * 4 - 1, -1)]
        BA = (4, 5); BV = 6; BT = 7

        def load_chunk(n):
            c = order[n]; s = n % NB
            dma(ctl[s], CTS[c].rearrange("p (g l) -> p g l", l=128), (), [("ctl", s)], ("l_ct", s))
            dma(ebl[s], EBD[c], (), [("ebl", s)], ("l_eb", s))
            dma(sbl[s], SBS[c], (), [("sbl", s)], ("l_sb", s))
            dma(ypl[s], YP[c], (), [("ypl", s)], ("l_yp", s))
            dma(szl[s], SZ[c], (), [("szl", s)], ("l_sz", s))
            dma(qtl[s][0:64, :], QTS[c], (), [("qtl", s)], ("l_qt", s))
            dma(ktl[s][0:64, :, :], KTS[c:c + 3].rearrange("b p n -> p b n"), (), [("ktl", s)], ("l_kt", s))
            dma(vel[s], VES[c:c + 3].rearrange("b p n -> p b n"), (), [("vel", s)], ("l_ve", s))

        def chunk_stage_fns(n):
            c = order[n]; s = n % NB; ci = c % 4

            def A():
                if n + 1 < len(order): load_chunk(n + 1)
                for g in range(2):
                    mm(ps[BV][:, g * 256:(g + 1) * 256], ctl[s][:, g, :], hbb[:, g * 256:(g + 1) * 256], True, True,
                       [("ctl", s), "hbb"], [PSK(BV)])
                tt("dve", yt1.rearrange("p (h d) -> p h d", d=64), ps[BV][:, 0:512].rearrange("p (h d) -> p h d", d=64),
                   bc(ebl[s][:, 0:8], 2, 64), ALU.mult, [PSK(BV), ("ebl", s)], ["yt1"])
                tt("dve", yt1, yt1, ypl[s], ALU.add, ["yt1", ("ypl", s)], ["yt1"])
                tt("pool", hb2.rearrange("p (h d) -> p h d", d=64), hb.rearrange("p (h d) -> p h d", d=64), bc(ebl[s][:, 8:16], 2, 64),
                   ALU.mult, ["hb", ("ebl", s)], ["hb2"])
                tt("pool", hb, hb2, sbl[s], ALU.add, ["hb2", ("sbl", s)], ["hb"])
                cp("act", hbb, hb, ["hb"], ["hbb"])
                tt("dve", yg, yt1, szl[s], ALU.mult, ["yt1", ("szl", s)], ["yg"])
                memset("dve", gss[:, 0:1], 0.0, ["gss"])
                act(junk, yg, AF.Square, ["yg", "gss"], ["junk", "gss"], accum=gss[:, 0:1])

            def scores(kvh, blk):
                b = BA[(kvh * 3 + blk) % 2]
                mm(ps[b][:, 0:512], ktl[s][0:64, blk, kvh * 128:(kvh + 1) * 128], qtl[s][0:64, kvh * 512:(kvh + 1) * 512],
                   True, blk == 1, [("ktl", s), ("qtl", s)], [PSK(b)])
                if blk != 1:
                    mm(ps[b][:, 0:512], ident_bf, NEGp if blk == 0 else NEGn, False, True, [], [PSK(b)])
                act(PT[kvh][blk], ps[b][:, 0:512], AF.Exp, [PSK(b)], [("PT", kvh, blk)])

            def pv(kvh):
                for r in range(4):
                    for blk in range(3):
                        mm(ps[BV][:, r * 65:(r + 1) * 65], PT[kvh][blk][:, r * 128:(r + 1) * 128], vel[s][:, blk, kvh * 65:(kvh + 1) * 65],
                           blk == 0, blk == 2, [("PT", kvh, blk), ("vel", s)], [PSK(BV)])
                pv3 = ps[BV][:, 0:260].rearrange("p (r d) -> p r d", d=65)
                tt("dve", den[:, kvh * 4:(kvh + 1) * 4], pv3[:, :, 64], esink[:, kvh * 4:(kvh + 1) * 4], ALU.add,
                   [PSK(BV)], [("den", kvh)])
                recip(den[:, kvh * 4:(kvh + 1) * 4], den[:, kvh * 4:(kvh + 1) * 4], [("den", kvh)], [("den", kvh)])
                tt("dve", ymix[:, 512 + kvh * 256:512 + (kvh + 1) * 256].rearrange("p (r d) -> p r d", d=64), pv3[:, :, 0:64],
                   bc(den[:, kvh * 4:(kvh + 1) * 4], 2, 64), ALU.mult, [PSK(BV), ("den", kvh)], [("ymix_a", kvh)])

            def B():
                scores(0, 0)
                ts("dve", gss[:, 1:2], gss[:, 0:1], 1.0 / 512, EPS, ALU.mult, ALU.add, ["gss"], ["gss1"])
                act(gss[:, 1:2], gss[:, 1:2], AF.Sqrt, ["gss1"], ["gss1"])

            def C():
                scores(0, 1)
                recip(gss[:, 2:3], gss[:, 1:2], ["gss1"], ["gss2"])
                stt("dve", ymix[:, 0:512], yg, gss[:, 2:3], sm[:, O_SNW:O_SNW + 512], ALU.mult, ALU.mult, ["yg", "gss2"], ["ymix_s"])

            def Dd(): scores(0, 2)
            def E(): scores(1, 0); pv(0)
            def F(): scores(1, 1)
            def G(): scores(1, 2)
            def H(): pv(1)

            def I():
                for cc in range(8):
                    tr(psb[BT][:, cc * 128:(cc + 1) * 128], ymix[:, cc * 128:(cc + 1) * 128], ident_bf,
                       ["ymix_s", ("ymix_a", 0), ("ymix_a", 1)], [PSK(BT)])
                cp("act", ymT[:, :, ci * 128:(ci + 1) * 128], psb[BT][:, 0:1024].rearrange("p (a b) -> p a b", b=128), [PSK(BT)], [("ymT", ci)])
            return A, B, C, Dd, E, F, G, H, I

        def p3_x(tix): return xts[tix % 2], ("x", tix % 2)

        def seq(*fs):
            def f():
                for g_ in fs: g_()
            return f

        def tile_pre_stages(tix):
            xt, xk = p3_x(tix)
            ch = [chunk_stage_fns(tix * 4 + cq) for cq in range(4)]
            def chunk_list(q, first):
                c_ = ch[q]
                return [first, seq(c_[2], c_[3]), c_[4], seq(c_[5], c_[6]), c_[7]]
            st = chunk_list(0, seq(ch[0][0], ch[0][1]))
            for q in range(1, 4):
                st.append(seq(ch[q - 1][8], ch[q][0]))
                st += chunk_list(q, ch[q][1])
            st.append(ch[3][8])

            def mk_o(ms):
                def f():
                    for m in ms:
                        b = BV if m % 2 == 0 else BT
                        for cc in range(KC):
                            mm(ps[b][:, 0:512], wout[:, cc, m * 128:(m + 1) * 128], ymT[:, cc, :], cc == 0, cc == KC - 1,
                               ["wout"] + [("ymT", i) for i in range(4)], [PSK(b)])
                        stt("dve", xt[:, m, :], ps[b][:, 0:512], Gm[:, 8 + m:8 + m + 1], xt[:, m, :], ALU.mult, ALU.add, [PSK(b), xk], [xk])
                return f
            st.append(mk_o((0, 1, 2, 3)))
            st.append(seq(mk_o((4, 5, 6, 7)), lambda: norm_a(xt, xk, sq, "sq")))
            st.append(lambda: norm_b(sq, "sq", BV))
            st.append(seq(lambda: norm_c1(rstd, "rstd", BV), lambda: norm_c2(rstd, "rstd")))
            st.append(seq(lambda: norm_c2r(rstd, "rstd"), lambda: norm_c3(xt, xk, u3, "u3", rstd, "rstd", 2, ntm3, NTK3, (0, 1, 2, 3))))
            st.append(lambda: norm_c3(xt, xk, u3, "u3", rstd, "rstd", 2, ntm3, NTK3, (4, 5, 6, 7)))
            return st

        def p3_loadx(tix):
            ti = NT3 - 1 - tix
            dma(xts[tix % 2], X1v[:, :, ti * 512:(ti + 1) * 512], (), [("x", tix % 2)], ("xl3", tix % 2))

        load_chunk(0)
        p3_loadx(0)
        for f in tile_pre_stages(0): f()
        slots = [("u", ("c", fc)) for fc in range(FC)] + [("d", -1)]
        for m in range(8): slots += [("d", ("h", m)), ("d", m)]
        for tix in range(NT3):
            ti = NT3 - 1 - tix
            xt, xk = p3_x(tix)
            hu = {}; hd = {}
            if tix + 1 < NT3:
                hu.setdefault(("c", 9), []).append(lambda t=tix + 1: p3_loadx(t))
                stg = tile_pre_stages(tix + 1)
                assert len(stg) <= len(slots), (len(stg), len(slots))
                for f, (kind, k) in zip(stg, slots):
                    (hu if kind == "u" else hd).setdefault(k, []).append(f)
            ffn_up(1, u3, "u3", hT, rings, hu)
            ffn_down(1, xt, xk, hT, 2, rings, hd, banks=(0, 2))
            dma(outv[:, :, ti * 512:(ti + 1) * 512], xt, [xk], [("out", ti)], ("os", tix % 2))
        P.finish(final_streams=[("os", 0), ("os", 1)] if NT3 > 1 else [("os", 0)])
        stats = (len(P.ops), P.nwaits, P.nsem)
    return nc, stats


def _prep_core(b, rev, S, x, c, positions, w_ada, b_ada, norm_ffn1, norm_mix, norm_ffn2, conv_w, conv_b, dt_bias, a_log,
               d_skip, ssd_norm_w, q_norm_w, k_norm_w, sink_logit):
    f = np.float32
    small = np.zeros((128, NSMALL), f)
    pk = lambda v: np.ascontiguousarray(np.asarray(v, f).reshape(-1, 128).T)
    rep = lambda v: np.broadcast_to(np.asarray(v, f).reshape(1, -1), (128, np.asarray(v).size))
    small[:, O_C:O_C + 8] = pk(c[b])
    small[:, O_BADA:O_BADA + 72] = pk(b_ada[0])
    small[:, O_GAIN:O_GAIN + 8] = pk(norm_ffn1[0]); small[:, O_GAIN + 8:O_GAIN + 16] = pk(norm_mix[0])
    small[:, O_GAIN + 16:O_GAIN + 24] = pk(norm_ffn2[0])
    cw = np.asarray(conv_w[0], f)
    if rev: cw = cw[::-1]
    small[:, O_CW:O_CW + 40] = cw.reshape(5, 8, 128).transpose(2, 1, 0).reshape(128, 40)
    small[:, O_CB:O_CB + 8] = pk(conv_b[0])
    dtb = np.asarray(dt_bias[0], f); alg = np.asarray(a_log[0], f)
    if rev: dtb = dtb[::-1]; alg = alg[::-1]
    small[:, O_DTB:O_DTB + 16] = rep(dtb.reshape(-1))
    small[:, O_ALOG:O_ALOG + 16] = rep(alg.reshape(-1))
    small[:, O_DSK:O_DSK + 8] = rep(d_skip[0])
    small[:, O_SINK:O_SINK + 8] = rep(sink_logit[0])
    small[:, O_KW:O_KW + 128] = rep(np.tile(np.asarray(k_norm_w[0], f), 2))
    small[:, O_QW:O_QW + 512] = rep(np.tile(np.asarray(q_norm_w[0], f), 8))
    small[:, O_SNW:O_SNW + 512] = rep(ssd_norm_w[0])
    small[:, O_FLAG] = 0.0 if rev else 1.0
    small[:, O_FLAG + 1] = 1.0 if rev else 0.0
    p = np.asarray(positions[b][:S], np.int32); xb = np.asarray(x[b][:S], f)
    if rev: p = p[::-1]; xb = xb[::-1]
    pos = np.ascontiguousarray(p.reshape(-1, 128).T)
    xT = np.ascontiguousarray(xb.T)
    return {"xT": xT, "small": small, "pos": pos}


_CACHE = {}


def run(inputs, S, debug=False, n_cores=8):
    g = {k: np.asarray(v) for k, v in inputs.items()}
    key = (S, debug)
    if key not in _CACHE:
        _CACHE[key] = build(S, debug)
    nc, stats = _CACHE[key]
    wi = np.asarray(g["w_in"][0], np.float32)
    perm = np.concatenate([np.arange(0, 512), np.arange(512, 1536), np.arange(1552, 2064), np.arange(2064, 2192),
                           np.arange(2192, 2320), np.arange(1536, 1552)])
    perm_r = np.concatenate([perm[:2304], np.arange(1544, 1552), np.arange(1536, 1544)])
    ca = lambda a: np.ascontiguousarray(a, dtype=np.float32)
    shared = {
        "w_ada": ca(g["w_ada"][0]), "wg1": ca(g["ffn1_wg"][0]), "wu1": ca(g["ffn1_wu"][0]), "wd1": ca(g["ffn1_wd"][0]),
        "wg2": ca(g["ffn2_wg"][0]), "wu2": ca(g["ffn2_wu"][0]), "wd2": ca(g["ffn2_wd"][0]), "w_out": ca(g["w_out"][0]),
    }
    w_in_n = np.ascontiguousarray(wi[:, perm]); w_in_r = np.ascontiguousarray(wi[:, perm_r])
    in_maps = []
    for core in range(n_cores):
        b = (core // 2) % 4; rev = bool(core % 2)
        m = _prep_core(b, rev, S, g["x"], g["c"], g["positions"], g["w_ada"], g["b_ada"], g["norm_ffn1"], g["norm_mix"],
                       g["norm_ffn2"], g["conv_w"], g["conv_b"], g["dt_bias"], g["a_log"], g["d_skip"], g["ssd_norm_w"],
                       g["q_norm_w"], g["k_norm_w"], g["sink_logit"])
        m.update(shared)
        m["w_in"] = w_in_r if rev else w_in_n
        in_maps.append(m)
    res = run_bass_kernel_spmd(nc, in_maps, core_ids=list(range(n_cores)))
    return res, stats


def assemble(res, S, nb=4):
    out = np.empty((nb, S, D), np.float32)
    for b in range(nb):
        out[b, :S // 2] = res.results[2 * b]["outT"].T
        out[b, S // 2:] = res.results[2 * b + 1]["outT"].T[::-1]
    return out


def kernel(**inputs):
    S = 8192
    res, _ = run(inputs, S, debug=False, n_cores=8)
    return assemble(res, S, 4)
```
